# Optimizing a Trainium2 kernel written in Bass

```python
import math
import jax
import jax.numpy as jnp
from jax import lax
import numpy as np

D_MODEL = 1024
BATCH = 8
SEQ = 2048
DEPTH = 4

GRID_W = 64
CTX_LEN = 256
N_MIXERS = 4
MIX_DIFF_ATTN = 0
MIX_HYENA = 1
MIX_RETENTION = 2
MIX_SHORTCONV = 3
CTX_READING_MIXERS = (MIX_DIFF_ATTN, MIX_RETENTION)
N_MOD = 9
D_FF = 2816
LN_EPS = 1e-5
ROPE_BASE = 10000.0
DA_HEAD_DIM = 64
DA_HEADS = D_MODEL // (2 * DA_HEAD_DIM)
DA_V_DIM = 2 * DA_HEAD_DIM
Q_BLOCK = 128
HY_EMB = 33
HY_FH = 64
HY_TARGET = 1e-2
HY_FAST_PCT = 0.3
HY_SLOW_PCT = 1.5
RT_HEADS = 4
RT_QK_DIM = D_MODEL // RT_HEADS
RT_V_DIM = 2 * D_MODEL // RT_HEADS
RT_CHUNK = 128

kernel_name = 'hybrid_interleaved_flow_backbone'


def layer_norm(x, g, b):
    xf = x.astype(jnp.float32)
    mu = jnp.mean(xf, -1, keepdims=True)
    var = jnp.mean(jnp.square(xf - mu), -1, keepdims=True)
    y = (xf - mu) * lax.rsqrt(var + LN_EPS) * g.astype(jnp.float32) + b.astype(jnp.float32)
    return y.astype(x.dtype)


def modulation(cond, w, b):
    m = jax.nn.silu(cond) @ w + b
    return jnp.split(m[..., None, :], N_MOD, axis=-1)


def modulate(x, shift, scale):
    return x * (1.0 + scale) + shift


def swiglu(h, wi, wo):
    a, u = jnp.split(h @ wi, 2, axis=-1)
    return (jax.nn.silu(a) * u) @ wo


def half_ffn(x, shift, scale, gate, wi, wo, g, b, alpha):
    y = swiglu(modulate(x, shift, scale), wi, wo)
    return layer_norm(alpha * x + 0.5 * gate * y, g, b)


def axial_rope(n_tokens, dim):
    rows = n_tokens // GRID_W
    row = jnp.repeat(jnp.arange(rows), GRID_W).astype(jnp.float32)
    col = jnp.tile(jnp.arange(GRID_W), rows).astype(jnp.float32)
    n_freq = dim // 4
    inv = ROPE_BASE ** (-jnp.arange(n_freq, dtype=jnp.float32) / n_freq)
    ang = jnp.concatenate([row[:, None] * inv, col[:, None] * inv], axis=-1)
    return jnp.cos(ang), jnp.sin(ang)


def apply_rope(x, cos, sin):
    shape = (x.shape[1],) + (1,) * (x.ndim - 3) + (cos.shape[-1],)
    cos = cos.reshape(shape).astype(x.dtype)
    sin = sin.reshape(shape).astype(x.dtype)
    x1, x2 = jnp.split(x, 2, axis=-1)
    return jnp.concatenate([x1 * cos - x2 * sin, x1 * sin + x2 * cos], axis=-1)


def dwconv3(u, w):
    up = jnp.pad(u, ((0, 0), (1, 1), (0, 0)))
    return up[:, :-2] * w[0] + up[:, 1:-1] * w[1] + up[:, 2:] * w[2]


def diff_attention(h_lat, h_ctx, ctx_out, w_qkv, w_o, lam, subln_g, lam_init):
    def project(h):
        b, n, _ = h.shape
        q, k, v = jnp.split(h @ w_qkv, 3, axis=-1)
        q = q.reshape(b, n, DA_HEADS, 2, DA_HEAD_DIM) * DA_HEAD_DIM ** -0.5
        k = k.reshape(b, n, DA_HEADS, 2, DA_HEAD_DIM)
        v = v.reshape(b, n, DA_HEADS, DA_V_DIM)
        return q, k, v

    lam = lam.astype(jnp.float32)
    lam_full = jnp.exp(jnp.sum(lam[0] * lam[1])) - jnp.exp(jnp.sum(lam[2] * lam[3])) + lam_init

    def attend(q, k, v):
        s = jnp.einsum('bqhmd,bkhmd->bhmqk', q, k).astype(jnp.float32)
        p = jax.nn.softmax(s, axis=-1)
        a = (p[:, :, 0] - lam_full * p[:, :, 1]).astype(v.dtype)
        return jnp.einsum('bhqk,bkhe->bqhe', a, v)

    def finish(o):
        b, n = o.shape[:2]
        of = o.astype(jnp.float32)
        of = of * lax.rsqrt(jnp.mean(of * of, -1, keepdims=True) + LN_EPS)
        of = of * subln_g.astype(jnp.float32) * (1.0 - lam_init)
        return of.astype(o.dtype).reshape(b, n, DA_HEADS * DA_V_DIM) @ w_o

    B, L = h_lat.shape[:2]
    qc, kc, vc = project(h_ctx)
    ql, kl, vl = project(h_lat)
    cos, sin = axial_rope(L, DA_HEAD_DIM)
    ql = apply_rope(ql, cos, sin)
    kl = apply_rope(kl, cos, sin)
    k_all = jnp.concatenate([kc, kl], axis=1)
    v_all = jnp.concatenate([vc, vl], axis=1)
    nb = L // Q_BLOCK
    q_blocks = jnp.moveaxis(ql.reshape(B, nb, Q_BLOCK, DA_HEADS, 2, DA_HEAD_DIM), 1, 0)
    o = lax.map(lambda qb: attend(qb, k_all, v_all), q_blocks)
    o = jnp.moveaxis(o, 0, 1).reshape(B, L, DA_HEADS, DA_V_DIM)
    y_lat = finish(o)
    y_ctx = finish(attend(qc, kc, vc)) if ctx_out else None
    return y_lat, y_ctx


def hyena_filters(n, w1, b1, f1, w2, b2, f2, w3):
    f32 = jnp.float32
    t = jnp.linspace(0.0, 1.0, n, dtype=f32)[:, None]
    bands = (HY_EMB - 1) // 2
    fr = jnp.linspace(1e-4, bands - 1, bands, dtype=f32)[None, :]
    w = 2.0 * math.pi * jnp.arange(n, dtype=f32)[:, None] / n
    z = jnp.concatenate([t, jnp.cos(fr * w), -jnp.sin(fr * w)], axis=-1)
    h = jnp.sin(f1.astype(f32) * (z @ w1.astype(f32) + b1.astype(f32)))
    h = jnp.sin(f2.astype(f32) * (h @ w2.astype(f32) + b2.astype(f32)))
    h = h @ w3.astype(f32)
    max_decay = math.log(HY_TARGET) / HY_FAST_PCT
    min_decay = math.log(HY_TARGET) / HY_SLOW_PCT
    deltas = jnp.abs(jnp.linspace(min_decay, max_decay, D_MODEL, dtype=f32))
    h = h * jnp.exp(-t * jnp.tile(deltas, 2)[None, :])
    return h[:, :D_MODEL], h[:, D_MODEL:]


def bidir_long_conv(u, h_f, h_b, d_skip):
    L = u.shape[1]
    n = 2 * L
    h_full = jnp.concatenate([h_f, jnp.zeros((1, h_f.shape[1]), h_f.dtype), h_b[1:][::-1]], axis=0)
    uf = jnp.fft.rfft(u.astype(jnp.float32), n=n, axis=1)
    hf = jnp.fft.rfft(h_full, n=n, axis=0)
    y = jnp.fft.irfft(uf * hf[None], n=n, axis=1)[:, :L]
    return (y + u.astype(jnp.float32) * d_skip.astype(jnp.float32)).astype(u.dtype)


def hyena(h_lat, h_ctx, ctx_out, w_in, conv_w, conv_b, fw1, fb1, ff1, fw2, fb2, ff2, fw3, d_skip, w_o):
    def run(h):
        n = h.shape[1]
        h_f, h_b = hyena_filters(n, fw1, fb1, ff1, fw2, fb2, ff2, fw3)
        u = dwconv3(h @ w_in, conv_w) + conv_b
        x0, x1, v = jnp.split(u, 3, axis=-1)
        return (x0 * bidir_long_conv(x1 * v, h_f, h_b, d_skip)) @ w_o
    return run(h_lat), (run(h_ctx) if ctx_out else None)


def retention_chunkwise(q, k, v, gamma, state):
    f32 = jnp.float32
    b, n, h, _ = q.shape
    dv = v.shape[-1]
    nc = n // RT_CHUNK
    log_g = jnp.log(gamma)
    pos = jnp.arange(RT_CHUNK, dtype=f32)
    lag = pos[:, None] - pos[None, :]
    d_intra = jnp.where(lag >= 0, jnp.exp(jnp.maximum(lag, 0.0)[None] * log_g[:, None, None]), 0.0)
    d_read = jnp.exp((pos[:, None] + 1.0) * log_g[None, :])[None, :, :, None]
    d_write = jnp.exp((RT_CHUNK - 1.0 - pos[:, None]) * log_g[None, :])[None, :, :, None]
    d_chunk = jnp.exp(RT_CHUNK * log_g)[None, :, None, None]

    def chunks(a):
        return jnp.moveaxis(a.astype(f32).reshape(b, nc, RT_CHUNK, h, a.shape[-1]), 1, 0)

    def step(s, qkv):
        qc, kc, vc = qkv
        scores = jnp.einsum('bihd,bjhd->bhij', qc, kc) * d_intra
        out = jnp.einsum('bhij,bjhe->bihe', scores, vc) + jnp.einsum('bihd,bhde->bihe', qc, s) * d_read
        s = d_chunk * s + jnp.einsum('bjhd,bjhe->bhde', kc * d_write, vc)
        return s, out

    state, out = lax.scan(step, state, (chunks(q), chunks(k), chunks(v)))
    return jnp.moveaxis(out, 0, 1).reshape(b, n, h, dv), state


def retention(h_lat, h_ctx, ctx_out, w_in, decay_logit, gn_g, w_o):
    qk_w = RT_HEADS * RT_QK_DIM
    v_w = RT_HEADS * RT_V_DIM

    def project(h):
        b, n, _ = h.shape
        q, k, v, g = jnp.split(h @ w_in, [qk_w, 2 * qk_w, 2 * qk_w + v_w], axis=-1)
        q = q.reshape(b, n, RT_HEADS, RT_QK_DIM)
        k = k.reshape(b, n, RT_HEADS, RT_QK_DIM) * RT_QK_DIM ** -0.5
        v = v.reshape(b, n, RT_HEADS, RT_V_DIM)
        return q, k, v, g

    def finish(o, g):
        b, n = o.shape[:2]
        mu = jnp.mean(o, -1, keepdims=True)
        var = jnp.mean(jnp.square(o - mu), -1, keepdims=True)
        o = ((o - mu) * lax.rsqrt(var + LN_EPS)).reshape(b, n, v_w) * gn_g.astype(jnp.float32)
        return (jax.nn.silu(g) * o.astype(g.dtype)) @ w_o

    B, L = h_lat.shape[:2]
    gam = jax.nn.sigmoid(decay_logit.astype(jnp.float32))
    qc, kc, vc, gc = project(h_ctx)
    ql, kl, vl, gl = project(h_lat)
    cos, sin = axial_rope(L, RT_QK_DIM)
    ql = apply_rope(ql, cos, sin)
    kl = apply_rope(kl, cos, sin)
    zero = jnp.zeros((B, RT_HEADS, RT_QK_DIM, RT_V_DIM), jnp.float32)
    flip = lambda a: jnp.flip(a, axis=1)
    oc_f, s_f = retention_chunkwise(qc, kc, vc, gam[0], zero)
    oc_b, s_b = retention_chunkwise(flip(qc), flip(kc), flip(vc), gam[1], zero)
    ol_f, _ = retention_chunkwise(ql, kl, vl, gam[0], s_f)
    ol_b, _ = retention_chunkwise(flip(ql), flip(kl), flip(vl), gam[1], s_b)
    y_lat = finish(ol_f + flip(ol_b), gl)
    y_ctx = finish(oc_f + flip(oc_b), gc) if ctx_out else None
    return y_lat, y_ctx


def short_conv_mixer(h_lat, h_ctx, ctx_out, w_in, conv_w, w_o):
    def run(h):
        b_gate, c_gate, u = jnp.split(h @ w_in, 3, axis=-1)
        return (b_gate * dwconv3(c_gate * u, conv_w)) @ w_o
    return run(h_lat), (run(h_ctx) if ctx_out else None)


def setup_inputs(seed: int = 0) -> dict:
    key = jax.random.key(seed)
    keys = iter(jax.random.split(key, 48))

    def nrm(shape, scale):
        return scale * jax.random.normal(next(keys), shape, jnp.float32)

    d = D_MODEL
    beta = (8.0 * DEPTH) ** -0.25
    n_da = len(range(MIX_DIFF_ATTN, DEPTH, N_MIXERS))
    n_hy = len(range(MIX_HYENA, DEPTH, N_MIXERS))
    n_rt = len(range(MIX_RETENTION, DEPTH, N_MIXERS))
    n_sc = len(range(MIX_SHORTCONV, DEPTH, N_MIXERS))
    qk_w = RT_HEADS * RT_QK_DIM
    v_w = RT_HEADS * RT_V_DIM
    gamma0 = 1.0 - 2.0 ** (-5.0 - jnp.arange(RT_HEADS, dtype=jnp.float32))
    logit0 = jnp.log(gamma0) - jnp.log1p(-gamma0)
    return {
        'x': nrm((BATCH, SEQ, d), 1.0),
        'c': nrm((BATCH, d), 1.0),
        'ctx': nrm((BATCH, CTX_LEN, d), 1.0),
        'c_ctx': nrm((d,), 1.0),
        'ada_w': nrm((DEPTH, d, N_MOD * d), 0.5 * d ** -0.5),
        'ada_b': nrm((DEPTH, N_MOD * d), 0.02),
        'ln_g': 1.0 + nrm((DEPTH, 3, d), 0.02),
        'ln_b': nrm((DEPTH, 3, d), 0.02),
        'ffa_wi': nrm((DEPTH, d, 2 * D_FF), d ** -0.5),
        'ffa_wo': nrm((DEPTH, D_FF, d), beta * D_FF ** -0.5),
        'ffb_wi': nrm((DEPTH, d, 2 * D_FF), d ** -0.5),
        'ffb_wo': nrm((DEPTH, D_FF, d), beta * D_FF ** -0.5),
        'da_w_qkv': nrm((n_da, d, 3 * d), d ** -0.5),
        'da_w_o': nrm((n_da, d, d), beta * d ** -0.5),
        'da_lambda': nrm((n_da, 4, DA_HEAD_DIM), 0.1),
        'da_subln_g': 1.0 + nrm((n_da, DA_V_DIM), 0.02),
        'hy_w_in': nrm((n_hy, d, 3 * d), d ** -0.5),
        'hy_conv_w': nrm((n_hy, 3, 3 * d), 3 ** -0.5),
        'hy_conv_b': nrm((n_hy, 3 * d), 0.02),
        'hy_fw1': nrm((n_hy, HY_EMB, HY_FH), HY_EMB ** -0.5),
        'hy_fb1': nrm((n_hy, HY_FH), 0.02),
        'hy_ff1': 1.0 + nrm((n_hy, HY_FH), 0.02),
        'hy_fw2': nrm((n_hy, HY_FH, HY_FH), HY_FH ** -0.5),
        'hy_fb2': nrm((n_hy, HY_FH), 0.02),
        'hy_ff2': 1.0 + nrm((n_hy, HY_FH), 0.02),
        'hy_fw3': nrm((n_hy, HY_FH, 2 * d), HY_FH ** -0.5),
        'hy_d_skip': nrm((n_hy, d), 1.0),
        'hy_w_o': nrm((n_hy, d, d), beta * d ** -0.5),
        'rt_w_in': nrm((n_rt, d, 2 * qk_w + 2 * v_w), d ** -0.5),
        'rt_decay_logit': logit0 + nrm((n_rt, 2, RT_HEADS), 0.1),
        'rt_gn_g': 1.0 + nrm((n_rt, v_w), 0.02),
        'rt_w_o': nrm((n_rt, v_w, d), beta * v_w ** -0.5),
        'sc_w_in': nrm((n_sc, d, 3 * d), d ** -0.5),
        'sc_conv_w': nrm((n_sc, 3, d), 3 ** -0.5),
        'sc_w_o': nrm((n_sc, d, d), beta * d ** -0.5),
    }


def reference(x, c, ctx, c_ctx, ada_w, ada_b, ln_g, ln_b, ffa_wi, ffa_wo, ffb_wi, ffb_wo,
              da_w_qkv, da_w_o, da_lambda, da_subln_g,
              hy_w_in, hy_conv_w, hy_conv_b, hy_fw1, hy_fb1, hy_ff1, hy_fw2, hy_fb2, hy_ff2, hy_fw3,
              hy_d_skip, hy_w_o,
              rt_w_in, rt_decay_logit, rt_gn_g, rt_w_o,
              sc_w_in, sc_conv_w, sc_w_o):
    alpha = (2.0 * DEPTH) ** 0.25
    ctx_last = max(i for i in range(DEPTH) if i % N_MIXERS in CTX_READING_MIXERS)
    xl, xc = x, ctx
    for i in range(DEPTH):
        kind, j = i % N_MIXERS, i // N_MIXERS
        use_ctx, ctx_next = i <= ctx_last, i < ctx_last
        ml = modulation(c, ada_w[i], ada_b[i])
        mc = modulation(c_ctx, ada_w[i], ada_b[i]) if use_ctx else None

        xl = half_ffn(xl, ml[0], ml[1], ml[2], ffa_wi[i], ffa_wo[i], ln_g[i, 0], ln_b[i, 0], alpha)
        if use_ctx:
            xc = half_ffn(xc, mc[0], mc[1], mc[2], ffa_wi[i], ffa_wo[i], ln_g[i, 0], ln_b[i, 0], alpha)

        hl = modulate(xl, ml[3], ml[4])
        hc = modulate(xc, mc[3], mc[4]) if use_ctx else None
        if kind == MIX_DIFF_ATTN:
            yl, yc = diff_attention(hl, hc, ctx_next, da_w_qkv[j], da_w_o[j], da_lambda[j], da_subln_g[j],
                                    0.8 - 0.6 * math.exp(-0.3 * i))
        elif kind == MIX_HYENA:
            yl, yc = hyena(hl, hc, ctx_next, hy_w_in[j], hy_conv_w[j], hy_conv_b[j], hy_fw1[j], hy_fb1[j],
                           hy_ff1[j], hy_fw2[j], hy_fb2[j], hy_ff2[j], hy_fw3[j], hy_d_skip[j], hy_w_o[j])
        elif kind == MIX_RETENTION:
            yl, yc = retention(hl, hc, ctx_next, rt_w_in[j], rt_decay_logit[j], rt_gn_g[j], rt_w_o[j])
        else:
            yl, yc = short_conv_mixer(hl, hc, ctx_next, sc_w_in[j], sc_conv_w[j], sc_w_o[j])
        xl = layer_norm(alpha * xl + ml[5] * yl, ln_g[i, 1], ln_b[i, 1])
        if ctx_next:
            xc = layer_norm(alpha * xc + mc[5] * yc, ln_g[i, 1], ln_b[i, 1])

        xl = half_ffn(xl, ml[6], ml[7], ml[8], ffb_wi[i], ffb_wo[i], ln_g[i, 2], ln_b[i, 2], alpha)
        if ctx_next:
            xc = half_ffn(xc, mc[6], mc[7], mc[8], ffb_wi[i], ffb_wo[i], ln_g[i, 2], ln_b[i, 2], alpha)
    return xl
```

```python
import math
import os
from contextlib import ExitStack

import numpy as np
import ml_dtypes
import concourse.bass as bass
import concourse.mybir as mybir
from concourse.bass_utils import run_bass_kernel_spmd

F32 = mybir.dt.float32
BF16 = mybir.dt.bfloat16
AF = mybir.ActivationFunctionType
ALU = mybir.AluOpType

D = 1024
NCH = 8
SEQ = 2048
CTX = 256
TT = SEQ + CTX
DEPTH = 4
DFF = 2816
NF = 22
ALPHA = (2.0 * DEPTH) ** 0.25
EPS = 1e-5
ENGS = ['tensor', 'vector', 'scalar', 'gpsimd', 'sync']


class Op:
    __slots__ = ('eng', 'fn', 'deps', 'signals', 'dma', 'semkey', 'semval')

    def __init__(self, eng, fn, dma, semkey):
        self.eng = eng
        self.fn = fn
        self.deps = []
        self.signals = False
        self.dma = dma
        self.semkey = semkey
        self.semval = None


class Prog:
    def __init__(self, nc):
        self.nc = nc
        self.ops = {e: [] for e in ENGS}
        self.last_w = {}
        self.readers = {}
        self.nops = 0
        self.last_op = {}
        self.scoped_dma = []

    def op(self, eng, fn, reads=(), writes=(), dma=False, semkey=None, scoped=True):
        if dma and semkey is None:
            semkey = writes[0]
        o = Op(eng, fn, dma, semkey)
        deps = set()
        for k in reads:
            w = self.last_w.get(k)
            if w is not None:
                deps.add(w)
        for k in writes:
            w = self.last_w.get(k)
            if w is not None:
                deps.add(w)
            for r in self.readers.get(k, ()):
                deps.add(r)
        for d in deps:
            if (not d.dma) and d.eng == 'tensor' and eng == 'tensor' and not dma:
                continue
            d.signals = True
            o.deps.append(d)
        for k in writes:
            self.last_w[k] = o
            self.readers[k] = []
        for k in reads:
            self.readers.setdefault(k, []).append(o)
        self.ops[eng].append(o)
        self.nops += 1
        if dma:
            if scoped:
                self.scoped_dma.append(o)
        else:
            self.last_op[eng] = o
        return o

    def barrier(self, engs=('tensor', 'vector', 'scalar', 'sync')):
        targets = [o for o in self.last_op.values()] + list(self.scoped_dma)
        self.scoped_dma = []
        for e in engs:
            o = Op(e, None, False, None)
            for d in targets:
                d.signals = True
                o.deps.append(d)
            self.ops[e].append(o)

    def emit(self, final_waits=()):
        nc = self.nc
        eng_sem = {}
        dma_sem = {}
        dma_cnt = {}
        with ExitStack() as es:
            for e in ENGS:
                cnt = 0
                for o in self.ops[e]:
                    if o.dma:
                        if o.semkey not in dma_sem:
                            dma_sem[o.semkey] = es.enter_context(nc.semaphore('d%d' % len(dma_sem)))
                            dma_cnt[o.semkey] = 0
                        dma_cnt[o.semkey] += 16
                        o.semval = dma_cnt[o.semkey]
                    elif o.signals:
                        cnt += 1
                        o.semval = cnt
                eng_sem[e] = es.enter_context(nc.semaphore('e_' + e))
            self.n_dma_sems = len(dma_sem)
            block = es.enter_context(nc.Block())
            fw = {}
            for (e, o) in final_waits:
                fw.setdefault(e, []).append((dma_sem[o.semkey], o.semval))

            def run(e, engobj):
                waited = {}
                for o in self.ops[e]:
                    need = {}
                    for d in o.deps:
                        if d.dma:
                            key = ('d', d.semkey)
                            sem = dma_sem[d.semkey]
                        else:
                            key = ('e', d.eng)
                            sem = eng_sem[d.eng]
                        if waited.get(key, 0) >= d.semval:
                            continue
                        if key not in need or need[key][1] < d.semval:
                            need[key] = (sem, d.semval)
                    for key, (sem, val) in need.items():
                        engobj.wait_ge(sem, val)
                        waited[key] = val
                    if o.fn is None:
                        continue
                    ins = o.fn(engobj)
                    if o.dma:
                        ins.then_inc(dma_sem[o.semkey], 16)
                    elif o.signals:
                        ins.then_inc(eng_sem[e], 1)
                for (sem, val) in fw.get(e, ()):
                    engobj.wait_ge(sem, val)

            block.tensor(lambda eng: run('tensor', eng))
            block.vector(lambda eng: run('vector', eng))
            block.scalar(lambda eng: run('scalar', eng))
            block.gpsimd(lambda eng: run('gpsimd', eng))
            block.sync(lambda eng: run('sync', eng))


class Blk:
    def __init__(self, stream, t0, n):
        self.s = stream
        self.t0 = t0
        self.n = n
        self.g0 = t0 if stream == 0 else SEQ + t0
        self.key = (stream, t0)

    @property
    def sl(self):
        return slice(self.g0, self.g0 + self.n)


LAT_BLKS = [Blk(0, i * 512, 512) for i in range(4)]
CTX_BLK = Blk(1, 0, 256)


class K:
    pass


def build_program(layers=(0, 1, 2, 3), first_affine_identity=True, dbg=None):
    nc = bass.Bass("TRN2", target_bir_lowering=False)
    k = K()
    k.nc = nc
    k.P = Prog(nc)
    k.es = ExitStack()
    k.dram = {}
    k.layers = layers

    def din(name, shape, dt=F32):
        k.dram[name] = nc.dram_tensor(name, list(shape), dt, kind="ExternalInput").ap()
        return k.dram[name]

    din('xT', [D, SEQ]); din('ctxT', [D, CTX]); din('cvec', [128, 2 * NCH])
    din('ada_w', [DEPTH, D, 9 * D]); din('ada_b', [128, DEPTH * 72])
    din('ln_g', [128, DEPTH * 3 * NCH]); din('ln_b', [128, DEPTH * 3 * NCH])
    din('ffa_wi', [DEPTH, D, 2 * DFF]); din('ffa_wo', [DEPTH, DFF, D])
    din('ffb_wi', [DEPTH, D, 2 * DFF]); din('ffb_wo', [DEPTH, DFF, D])
    din('da_w_qkv', [D, 3 * D]); din('da_wq_sw', [D, D]); din('da_wk_sw', [D, D]); din('da_w_o', [D, D])
    din('da_lamT', [64, 4]); din('da_subg', [128, 1]); din('rope_da', [128, 2 * SEQ], BF16)
    din('rt_w_in', [D, 6 * D]); din('rt_w_o', [2 * D, D]); din('rt_decay', [128, 8]); din('rt_gng', [128, 16])
    din('rope_rt', [128, 2 * SEQ], BF16); din('rt_const', [128, 1024 + 4 + 16 + 18])
    din('hy_w_in', [D, 3 * D]); din('hy_w_o', [D, D]); din('hy_cw', [128, 72]); din('hy_cb', [128, 24])
    din('hy_fw1', [33, 64]); din('hy_fw2', [64, 64]); din('hy_vec64', [64, 4]); din('hy_fw3', [64, 2 * D])
    din('hy_dskip', [1, D]); din('hy_delta', [128, D]); din('hy_tn', [128, 18]); din('hy_zT', [33, TT])
    din('ident', [128, 128], BF16)
    din('dft_f', [16, 128, 4096], BF16); din('dft_i', [16, 128, 4096], BF16)
    din('dft_fc', [2, 128, 512], BF16); din('dft_ic', [128, 1024], BF16)
    din('sc_w_in', [D, 3 * D]); din('sc_conv_w', [128, 3 * NCH]); din('sc_w_o', [D, D])
    k.out = nc.dram_tensor('outT', [D, SEQ], F32, kind="ExternalOutput").ap()
    if dbg:
        k.dbgc = nc.dram_tensor('dbgc', [D, CTX], F32, kind="ExternalOutput").ap()

    with k.es:
        es = k.es
        P = k.P

        k.sbcnt = 0

        def sb(name, shape, dt=F32, stack=None):
            k.sbcnt += 1
            return (stack or es).enter_context(nc.sbuf_tensor('%s_%d' % (name, k.sbcnt), list(shape), dt))

        k.sb = sb
        k.nbuf = sb('nbuf', [128, NCH, TT], F32)
        k.NS = 6
        k.WSZ = 2048
        k.wslots = [sb('ws%d' % i, [128, k.WSZ], BF16) for i in range(k.NS)]
        k.ws_next = 0
        k.mod = sb('mod', [128, DEPTH * 2 * 72], F32)
        k.adab = sb('adab', [128, DEPTH * 72], F32)
        k.lng = sb('lng', [128, DEPTH * 3 * NCH], F32)
        k.lnb = sb('lnb', [128, DEPTH * 3 * NCH], F32)
        k.cv = sb('cv', [128, 2 * NCH], F32)
        k.scb = sb('scb', [128, NCH, 2], BF16)
        k.sg = sb('sg', [128, 2 * NCH], F32)
        k.bg = []
        k.bg_loaded = None
        k.ln_pending = []
        k.coef = sb('coef', [128, DEPTH * 3 * 2 * 5 * NCH], F32)
        k.ones = sb('ones', [128, 128], BF16)
        k.one1 = sb('one1', [128, 128], BF16)
        k.epsc = sb('epsc', [128, 1], F32)
        k.scw = sb('scw', [128, 3 * NCH], F32)
        k.ps = [es.enter_context(nc.psum_tensor('ps%d' % i, [128, 512], F32)) for i in range(8)]
        k.ps_rr = 0

        P.op('vector', lambda e: e.memset(k.ones[:], 1.0 / 1024.0), writes=['ones'])
        P.op('vector', lambda e: e.memset(k.one1[:], 1.0), writes=['one1'])
        P.op('vector', lambda e: e.memset(k.epsc[:], EPS), writes=['epsc'])

        def small_load(dst, src, key):
            P.op('sync', lambda e: e.dma_start(out=dst[:], in_=src), writes=[key], dma=True)

        small_load(k.adab, k.dram['ada_b'], 'adab')
        small_load(k.lng, k.dram['ln_g'], 'lng')
        small_load(k.lnb, k.dram['ln_b'], 'lnb')
        small_load(k.cv, k.dram['cvec'], 'cv')
        small_load(k.scw, k.dram['sc_conv_w'], 'scw')
        for c in range(NCH):
            P.op('sync', (lambda c: lambda e: e.dma_start(out=k.nbuf[:, c, 0:SEQ], in_=k.dram['xT'][c * 128:(c + 1) * 128, :]))(c),
                 writes=[('n', c, b.key) for b in LAT_BLKS], dma=True, semkey=('nload', c))
        P.op('sync', lambda e: e.dma_start(out=k.nbuf[:, :, SEQ:TT], in_=k.dram['ctxT'].rearrange("(c p) t -> p c t", p=128)),
             writes=[('n', c, CTX_BLK.key) for c in range(NCH)], dma=True, semkey=('nload', 'c'))

        compute_mods(k)
        for lidx, li in enumerate(layers):
            use_ctx = li <= 2
            ctx_next = li < 2
            compute_coefs(k, li, first_affine_identity and li == layers[0])
            ffn(k, li, 0, use_ctx)
            P.barrier()
            if lidx >= 1 and lidx + 1 < len(layers) and li in (1, 2):
                k.bg = [(layers[lidx + 1], p) for p in range(36)]
            if li == 3:
                shortconv(k, li)
            else:
                MIXERS[li](k, li, use_ctx, ctx_next)
            bg_flush(k)
            P.barrier()
            ffn(k, li, 2, ctx_next, defer_tail=(lidx + 1 < len(layers)))
            P.barrier()
        final_out(k, layers[-1], dbg)
        P.emit(final_waits=k.final_waits)
    k.nops = P.nops
    return nc, k


def next_ps(k, n=1):
    r = []
    for _ in range(n):
        r.append(k.ps_rr)
        k.ps_rr = (k.ps_rr + 1) % 8
    return r if n > 1 else r[0]


def load_w(k, src_ap, nk, ncols):
    assert nk * ncols <= k.WSZ
    s = k.ws_next
    k.ws_next = (k.ws_next + 1) % k.NS
    dst = k.wslots[s][:, 0:nk * ncols].rearrange("p (a b) -> p a b", b=ncols)
    k.P.op('gpsimd', lambda e: e.dma_start(out=dst, in_=src_ap.rearrange("(a p) n -> p a n", p=128)),
           writes=[('ws', s)], dma=True, scoped=False)
    return s, dst


def coef_ap(k, li, sub, stream, which, c):
    idx = ((((li * 3 + sub) * 2 + stream) * 5 + which) * NCH) + c
    return k.coef[:, idx:idx + 1]


def mod_ap(k, li, stream, j, c=None):
    base = (li * 2 + stream) * 72 + j * NCH
    if c is None:
        return k.mod[:, base:base + NCH]
    return k.mod[:, base + c:base + c + 1]


def ada_load(k, li, piece):
    return load_w(k, k.dram['ada_w'][li, :, piece * 256:(piece + 1) * 256], NCH, 256)


def ada_compute(k, li, piece, pb, handle):
    P = k.P
    s, wv = handle
    for q in range(2):
        for kc in range(NCH):
            P.op('tensor', (lambda wv, q, kc, pb: lambda e: e.matmul(
                k.ps[pb][:, 2 * q:2 * q + 2], lhsT=wv[:, kc, q * 128:(q + 1) * 128], rhs=k.scb[:, kc, :],
                start=(kc == 0), stop=(kc == NCH - 1)))(wv, q, kc, pb),
                reads=[('ws', s), 'scb'], writes=[('ps', pb)])
    j0 = 2 * piece
    for s_ in range(2):
        base = (li * 2 + s_) * 72 + j0
        P.op('vector', (lambda pb, s_, base, li, j0: lambda e: e.tensor_tensor(
            out=k.mod[:, base:base + 2], in0=k.ps[pb][:, s_:4:2], in1=k.adab[:, li * 72 + j0:li * 72 + j0 + 2], op=ALU.add))(pb, s_, base, li, j0),
            reads=[('ps', pb), 'adab'], writes=['mod'])


def bg_drain(k, pb):
    if k.bg_loaded is not None:
        li, piece, handle = k.bg_loaded
        k.bg_loaded = None
        k.ln_pending = []
        ada_compute(k, li, piece, pb, handle)


def bg_step(k, pb):
    bg_drain(k, pb)
    if k.bg:
        li, piece = k.bg.pop(0)
        k.bg_loaded = (li, piece, ada_load(k, li, piece))


def bg_flush(k):
    while k.bg or k.bg_loaded is not None:
        bg_step(k, next_ps(k))


def compute_mods(k):
    P = k.P
    P.op('scalar', lambda e: e.activation(out=k.sg[:], in_=k.cv[:], func=AF.Silu), reads=['cv'], writes=['sg'])
    P.op('vector', lambda e: e.tensor_copy(out=k.scb[:, :, 0], in_=k.sg[:, 0:NCH]), reads=['sg'], writes=['scb'])
    P.op('vector', lambda e: e.tensor_copy(out=k.scb[:, :, 1], in_=k.sg[:, NCH:2 * NCH]), reads=['sg'], writes=['scb'])
    k.bg = [(li, p) for li in k.layers[0:2] for p in range(36)]
    bg_flush(k)
    P.barrier()


def compute_coefs(k, li, identity_first):
    P = k.P
    for sub in range(3):
        for s_ in range(2):
            shift = mod_ap(k, li, s_, 3 * sub + 0)
            scale = mod_ap(k, li, s_, 3 * sub + 1)
            gate = mod_ap(k, li, s_, 3 * sub + 2)
            i0 = (((li * 3 + sub) * 2 + s_) * 5) * NCH
            Ah = k.coef[:, i0:i0 + NCH]
            Bh = k.coef[:, i0 + NCH:i0 + 2 * NCH]
            Ar = k.coef[:, i0 + 2 * NCH:i0 + 3 * NCH]
            Br = k.coef[:, i0 + 3 * NCH:i0 + 4 * NCH]
            G = k.coef[:, i0 + 4 * NCH:i0 + 5 * NCH]
            ident = identity_first and sub == 0
            if not ident:
                pl, psub = (li - 1, 2) if sub == 0 else (li, sub - 1)
                gp = k.lng[:, (pl * 3 + psub) * NCH:(pl * 3 + psub + 1) * NCH]
                bp = k.lnb[:, (pl * 3 + psub) * NCH:(pl * 3 + psub + 1) * NCH]
            rk = ['mod', 'lng', 'lnb', 'coef']
            if ident:
                P.op('vector', (lambda Ah, scale: lambda e: e.tensor_scalar(out=Ah, in0=scale, scalar1=1.0, scalar2=None, op0=ALU.add))(Ah, scale), reads=rk, writes=['coef'])
                P.op('vector', (lambda Bh, shift: lambda e: e.tensor_copy(out=Bh, in_=shift))(Bh, shift), reads=rk, writes=['coef'])
                P.op('vector', (lambda Ar: lambda e: e.memset(Ar, ALPHA))(Ar), reads=rk, writes=['coef'])
                P.op('vector', (lambda Br: lambda e: e.memset(Br, 0.0))(Br), reads=rk, writes=['coef'])
            else:
                P.op('vector', (lambda Ah, scale, gp: lambda e: e.scalar_tensor_tensor(out=Ah, in0=scale, scalar=1.0, in1=gp, op0=ALU.add, op1=ALU.mult))(Ah, scale, gp), reads=rk, writes=['coef'])
                P.op('vector', (lambda Bh, scale, bp: lambda e: e.scalar_tensor_tensor(out=Bh, in0=scale, scalar=1.0, in1=bp, op0=ALU.add, op1=ALU.mult))(Bh, scale, bp), reads=rk, writes=['coef'])
                P.op('vector', (lambda Bh, shift: lambda e: e.tensor_tensor(out=Bh, in0=Bh, in1=shift, op=ALU.add))(Bh, shift), reads=rk, writes=['coef'])
                P.op('vector', (lambda Ar, gp: lambda e: e.tensor_scalar(out=Ar, in0=gp, scalar1=ALPHA, scalar2=None, op0=ALU.mult))(Ar, gp), reads=rk, writes=['coef'])
                P.op('vector', (lambda Br, bp: lambda e: e.tensor_scalar(out=Br, in0=bp, scalar1=ALPHA, scalar2=None, op0=ALU.mult))(Br, bp), reads=rk, writes=['coef'])
            gsc = 1.0 if sub == 1 else 0.5
            P.op('vector', (lambda G, gate, gsc: lambda e: e.tensor_scalar(out=G, in0=gate, scalar1=gsc, scalar2=None, op0=ALU.mult))(G, gate, gsc), reads=rk, writes=['coef'])


def make_h(k, li, sub, blks, hbuf, hcol0, eng='scalar', hkey=None):
    P = k.P
    for b in blks:
        for c in range(NCH):
            wkey = ('h', c, b.key) if hkey is None else ('h', c, hkey(b))
            Ah = coef_ap(k, li, sub, b.s, 0, c)
            Bh = coef_ap(k, li, sub, b.s, 1, c)
            dst = hbuf[:, c, b.g0 - hcol0:b.g0 - hcol0 + b.n]
            src = k.nbuf[:, c, b.sl]
            P.op('scalar', (lambda dst, src, Ah, Bh: lambda e: e.activation(out=dst, in_=src, func=AF.Identity, bias=Bh, scale=Ah))(dst, src, Ah, Bh),
                 reads=[('n', c, b.key), 'coef'], writes=[wkey])


def make_xa(k, li, sub, blks, eng='gpsimd'):
    P = k.P
    for b in blks:
        for c in range(NCH):
            Ar = coef_ap(k, li, sub, b.s, 2, c)
            Br = coef_ap(k, li, sub, b.s, 3, c)
            v = k.nbuf[:, c, b.sl]
            P.op('scalar', (lambda v, Ar, Br: lambda e: e.activation(out=v, in_=v, func=AF.Identity, bias=Br, scale=Ar))(v, Ar, Br),
                 reads=[('n', c, b.key), 'coef'], writes=[('n', c, b.key)])


def layer_norm_steps(k, b, lnt):
    P = k.P
    n = b.n
    rb, sq, mean, rstd, nmr, tmp = lnt
    steps = []

    def s_cs(c):
        src = k.nbuf[:, c, b.sl]
        P.op('scalar', lambda e: e.activation(out=rb[:, c, 0:n], in_=src, func=AF.Copy), reads=[('n', c, b.key)], writes=[('rb', c)])
        P.op('scalar', lambda e: e.activation(out=sq[:, c, 0:n], in_=src, func=AF.Square), reads=[('n', c, b.key)], writes=[('sq', c)])
    for c in range(NCH):
        steps.append((lambda c: lambda: s_cs(c))(c))
    pp = {}

    def s_mm(which):
        pb = next_ps(k)
        pp[which] = pb
        src_, key_ = (rb, 'rb') if which == 0 else (sq, 'sq')
        for c in range(NCH):
            P.op('tensor', (lambda c: lambda e: e.matmul(k.ps[pb][:, 0:n], lhsT=k.ones[:], rhs=src_[:, c, 0:n], start=(c == 0), stop=(c == NCH - 1)))(c),
                 reads=[(key_, c), 'ones'], writes=[('ps', pb)])
    steps.append(lambda: s_mm(0))
    steps.append(lambda: s_mm(1))

    def s_small():
        p1, p2 = pp[0], pp[1]
        P.op('scalar', lambda e: e.activation(out=mean[:, 0:n], in_=k.ps[p1][:, 0:n], func=AF.Copy), reads=[('ps', p1)], writes=['ln_mean'])
        P.op('vector', lambda e: e.tensor_tensor(out=tmp[:, 0:n], in0=mean[:, 0:n], in1=mean[:, 0:n], op=ALU.mult), reads=['ln_mean'], writes=['ln_tmp'])
        P.op('vector', lambda e: e.tensor_tensor(out=tmp[:, 0:n], in0=k.ps[p2][:, 0:n], in1=tmp[:, 0:n], op=ALU.subtract), reads=[('ps', p2), 'ln_tmp'], writes=['ln_tmp'])
        P.op('scalar', lambda e: e.activation(out=tmp[:, 0:n], in_=tmp[:, 0:n], func=AF.Sqrt, bias=k.epsc[:, 0:1], scale=1.0), reads=['ln_tmp', 'epsc'], writes=['ln_tmp'])
        P.op('vector', lambda e: e.reciprocal(out=rstd[:, 0:n], in_=tmp[:, 0:n]), reads=['ln_tmp'], writes=['ln_rstd'])
        P.op('vector', lambda e: e.scalar_tensor_tensor(out=nmr[:, 0:n], in0=mean[:, 0:n], scalar=-1.0, in1=rstd[:, 0:n], op0=ALU.mult, op1=ALU.mult),
             reads=['ln_mean', 'ln_rstd'], writes=['ln_nmr'])
    steps.append(s_small)

    def s_norm(c):
        v = k.nbuf[:, c, b.sl]
        P.op('vector', lambda e: e.tensor_tensor(out=v, in0=v, in1=rstd[:, 0:n], op=ALU.mult), reads=[('n', c, b.key), 'ln_rstd'], writes=[('n', c, b.key)])
        P.op('vector', lambda e: e.tensor_tensor(out=v, in0=v, in1=nmr[:, 0:n], op=ALU.add), reads=[('n', c, b.key), 'ln_nmr'], writes=[('n', c, b.key)])
    for c in range(NCH):
        steps.append((lambda c: lambda: s_norm(c))(c))
    return steps


def layer_norm_blk(k, b, lnt):
    for st_ in layer_norm_steps(k, b, lnt):
        st_()


def alloc_ln_tmps(k, st):
    rb = k.sb('ln_rb', [128, NCH, 512], BF16, st)
    sq = k.sb('ln_sq', [128, NCH, 512], BF16, st)
    mean = k.sb('ln_mean', [128, 512], F32, st)
    rstd = k.sb('ln_rstd', [128, 512], F32, st)
    nmr = k.sb('ln_nmr', [128, 512], F32, st)
    tmp = k.sb('ln_tmp', [128, 512], F32, st)
    return (rb, sq, mean, rstd, nmr, tmp)


def out_proj_residual(k, li, sub, blks, w_dram, nkc, rhs_fn, rhs_keys_fn, lnt, do_ln=True):
    P = k.P
    nb = len(blks)
    dper = max(1, min(NCH, 6 // nb))
    d0 = 0
    while d0 < NCH:
        dn = min(dper, NCH - d0)
        banks = {(dc, bi): next_ps(k) for dc in range(dn) for bi in range(nb)}
        kc0 = 0
        while kc0 < nkc:
            kn = min(k.WSZ // (dn * 128), nkc - kc0)
            s, wv = load_w(k, w_dram[kc0 * 128:(kc0 + kn) * 128, d0 * 128:(d0 + dn) * 128], kn, dn * 128)
            for kk in range(kn):
                kc = kc0 + kk
                for dc in range(dn):
                    for bi, b in enumerate(blks):
                        pb = banks[(dc, bi)]
                        rhs = rhs_fn(kc, b)
                        P.op('tensor', (lambda pb, b, wv, kk, dc, kc, rhs: lambda e: e.matmul(
                            k.ps[pb][:, 0:b.n], lhsT=wv[:, kk, dc * 128:(dc + 1) * 128], rhs=rhs,
                            start=(kc == 0), stop=(kc == nkc - 1)))(pb, b, wv, kk, dc, kc, rhs),
                            reads=[('ws', s)] + rhs_keys_fn(kc, b), writes=[('ps', pb)])
            kc0 += kn
        for dc in range(dn):
            c = d0 + dc
            for bi, b in enumerate(blks):
                pb = banks[(dc, bi)]
                G = coef_ap(k, li, sub, b.s, 4, c)
                v = k.nbuf[:, c, b.sl]
                P.op('vector', (lambda pb, b, G, v: lambda e: e.scalar_tensor_tensor(
                    out=v, in0=k.ps[pb][:, 0:b.n], scalar=G, in1=v, op0=ALU.mult, op1=ALU.add))(pb, b, G, v),
                    reads=[('ps', pb), ('n', c, b.key), 'coef'], writes=[('n', c, b.key)])
        d0 += dn
    if do_ln:
        for b in blks:
            layer_norm_blk(k, b, lnt)


def ffn(k, li, sub, with_ctx, defer_tail=False):
    P = k.P
    wi = k.dram['ffa_wi' if sub == 0 else 'ffb_wi'][li]
    wo = k.dram['ffa_wo' if sub == 0 else 'ffb_wo'][li]
    groups = [[LAT_BLKS[0], LAT_BLKS[1]], [LAT_BLKS[2], LAT_BLKS[3]]]
    gw = 1024
    if with_ctx:
        groups[1] = groups[1] + [CTX_BLK]
        gw = 1280
    with ExitStack() as st:
        hb = k.sb('ffn_h', [128, NCH, gw], BF16, st)
        gb = k.sb('ffn_g', [128, NF, gw], BF16, st)
        stmp = [k.sb('ffn_s%d' % i, [128, 512], F32, st) for i in range(2)]
        lnt = alloc_ln_tmps(k, st)
        si = 0
        pending = []

        def hloc(gcol0):
            return lambda b: ('loc', (b.g0 - gcol0) // 512)
        make_h(k, li, sub, groups[0], hb, groups[0][0].g0, hkey=hloc(groups[0][0].g0))
        make_xa(k, li, sub, groups[0])
        for b in k.ln_pending:
            pending.extend(layer_norm_steps(k, b, lnt))
        k.ln_pending = []
        for gi, grp in enumerate(groups):
            gcol0 = grp[0].g0
            hk = hloc(gcol0)
            for f0 in range(0, NF, 2):
                fn_ = min(2, NF - f0)
                sa, wa = load_w(k, wi[:, f0 * 128:(f0 + fn_) * 128], NCH, fn_ * 128)
                su, wu = load_w(k, wi[:, DFF + f0 * 128:DFF + (f0 + fn_) * 128], NCH, fn_ * 128)
                for ff in range(fn_):
                    f = f0 + ff
                    for b in grp:
                        lc = b.g0 - gcol0
                        pa, pu = next_ps(k, 2)
                        for (pb, wv, s) in ((pa, wa, sa), (pu, wu, su)):
                            for kc in range(NCH):
                                P.op('tensor', (lambda pb, wv, kc, ff, lc, b: lambda e: e.matmul(
                                    k.ps[pb][:, 0:b.n], lhsT=wv[:, kc, ff * 128:(ff + 1) * 128], rhs=hb[:, kc, lc:lc + b.n],
                                    start=(kc == 0), stop=(kc == NCH - 1)))(pb, wv, kc, ff, lc, b),
                                    reads=[('ws', s), ('h', kc, hk(b))], writes=[('ps', pb)])
                        tmp = stmp[si % 2]
                        tk = ('ffn_s', si % 2)
                        si += 1
                        P.op('scalar', (lambda tmp, pa, b: lambda e: e.activation(out=tmp[:, 0:b.n], in_=k.ps[pa][:, 0:b.n], func=AF.Silu))(tmp, pa, b),
                             reads=[('ps', pa)], writes=[tk])
                        P.op('vector', (lambda tmp, pu, b, f, lc: lambda e: e.tensor_tensor(
                            out=gb[:, f, lc:lc + b.n], in0=k.ps[pu][:, 0:b.n], in1=tmp[:, 0:b.n], op=ALU.mult))(tmp, pu, b, f, lc),
                            reads=[('ps', pu), tk], writes=[('g', f, b.key)])
                        for _ in range(2):
                            if pending:
                                pending.pop(0)()
            while pending:
                pending.pop(0)()
            if gi + 1 < len(groups):
                nxt = groups[gi + 1]
                make_h(k, li, sub, nxt, hb, nxt[0].g0, hkey=hloc(nxt[0].g0))
                make_xa(k, li, sub, nxt)
            out_proj_residual(k, li, sub, grp, wo, NF,
                              lambda kc, b: gb[:, kc, b.g0 - gcol0:b.g0 - gcol0 + b.n],
                              lambda kc, b: [('g', kc, b.key)], lnt, do_ln=False)
            if gi + 1 == len(groups) and defer_tail:
                k.ln_pending = list(grp)
            else:
                for b in grp:
                    pending.extend(layer_norm_steps(k, b, lnt))
            if gi + 1 == len(groups):
                while pending:
                    pending.pop(0)()


def shortconv(k, li):
    P = k.P
    sub = 1
    blks = LAT_BLKS
    w_in = k.dram['sc_w_in']
    with ExitStack() as st0:
      zb = k.sb('sc_z', [128, NCH, SEQ], BF16, st0)
      with ExitStack() as st:
        hb = k.sb('sc_h', [128, NCH, SEQ], BF16, st)
        vb = [k.sb('sc_v%d' % i, [128, SEQ + 2], F32, st) for i in range(2)]
        tb = [k.sb('sc_t%d' % i, [128, 512], F32, st) for i in range(2)]
        make_h(k, li, sub, blks, hb, 0)
        make_xa(k, li, sub, blks)
        for i in range(2):
            P.op('vector', (lambda i: lambda e: e.memset(vb[i][:, 0:1], 0.0))(i), writes=[('sc_v', i, 'pad')])
            P.op('vector', (lambda i: lambda e: e.memset(vb[i][:, SEQ + 1:SEQ + 2], 0.0))(i), writes=[('sc_v', i, 'pad')])
        ti = 0
        for c in range(NCH):
            vi = c % 2
            v = vb[vi]
            w3s = [load_w(k, w_in[:, j * D + c * 128:j * D + (c + 1) * 128], NCH, 128) for j in range(3)]
            for b in blks:
                pbg, pcg, pu = next_ps(k, 3)
                for j, pb in enumerate((pbg, pcg, pu)):
                    s, wv = w3s[j]
                    for kc in range(NCH):
                        P.op('tensor', (lambda pb, kc, b, wv: lambda e: e.matmul(
                            k.ps[pb][:, 0:b.n], lhsT=wv[:, kc, :], rhs=hb[:, kc, b.sl],
                            start=(kc == 0), stop=(kc == NCH - 1)))(pb, kc, b, wv),
                            reads=[('ws', s), ('h', kc, b.key)], writes=[('ps', pb)])
                t = tb[ti % 2]
                tk = ('sc_t', ti % 2)
                ti += 1
                P.op('scalar', (lambda t, pcg, b: lambda e: e.activation(out=t[:, 0:b.n], in_=k.ps[pcg][:, 0:b.n], func=AF.Copy))(t, pcg, b),
                     reads=[('ps', pcg)], writes=[tk])
                P.op('vector', (lambda t, pu, b, v: lambda e: e.tensor_tensor(out=v[:, 1 + b.t0:1 + b.t0 + b.n], in0=k.ps[pu][:, 0:b.n], in1=t[:, 0:b.n], op=ALU.mult))(t, pu, b, v),
                     reads=[('ps', pu), tk], writes=[('sc_v', vi, b.key)])
                P.op('scalar', (lambda pbg, b, c: lambda e: e.activation(out=zb[:, c, b.sl], in_=k.ps[pbg][:, 0:b.n], func=AF.Copy))(pbg, b, c),
                     reads=[('ps', pbg)], writes=[('z', c, b.key)])
            for bi, b in enumerate(blks):
                t = tb[ti % 2]
                tk = ('sc_t', ti % 2)
                ti += 1
                rk = [('sc_v', vi, bb.key) for bb in blks[max(0, bi - 1):bi + 2]] + [('sc_v', vi, 'pad'), 'scw']
                w0 = k.scw[:, 0 * NCH + c:0 * NCH + c + 1]
                w1 = k.scw[:, 1 * NCH + c:1 * NCH + c + 1]
                w2 = k.scw[:, 2 * NCH + c:2 * NCH + c + 1]
                o = 1 + b.t0
                P.op('vector', (lambda t, v, o, b, w1: lambda e: e.tensor_scalar(out=t[:, 0:b.n], in0=v[:, o:o + b.n], scalar1=w1, scalar2=None, op0=ALU.mult))(t, v, o, b, w1),
                     reads=rk, writes=[tk])
                P.op('vector', (lambda t, v, o, b, w0: lambda e: e.scalar_tensor_tensor(out=t[:, 0:b.n], in0=v[:, o - 1:o - 1 + b.n], scalar=w0, in1=t[:, 0:b.n], op0=ALU.mult, op1=ALU.add))(t, v, o, b, w0),
                     reads=rk + [tk], writes=[tk])
                P.op('vector', (lambda t, v, o, b, w2: lambda e: e.scalar_tensor_tensor(out=t[:, 0:b.n], in0=v[:, o + 1:o + 1 + b.n], scalar=w2, in1=t[:, 0:b.n], op0=ALU.mult, op1=ALU.add))(t, v, o, b, w2),
                     reads=rk + [tk], writes=[tk])
                P.op('vector', (lambda t, b, c: lambda e: e.tensor_tensor(out=zb[:, c, b.sl], in0=zb[:, c, b.sl], in1=t[:, 0:b.n], op=ALU.mult))(t, b, c),
                     reads=[tk, ('z', c, b.key)], writes=[('z', c, b.key)])
      P.barrier()
      with ExitStack() as st:
        lnt = alloc_ln_tmps(k, st)
        for gi_, grp in enumerate(([blks[0], blks[1]], [blks[2], blks[3]])):
            out_proj_residual(k, li, sub, grp, k.dram['sc_w_o'], NCH,
                              lambda kc, b: zb[:, kc, b.sl], lambda kc, b: [('z', kc, b.key)], lnt, do_ln=(gi_ == 0))
            if gi_ == 1:
                k.ln_pending = list(grp)


def final_out(k, li, dbg):
    P = k.P
    k.final_waits = []
    with ExitStack() as st:
        ob = [k.sb('fo%d' % i, [128, SEQ], F32, st) for i in range(2)]
        for c in range(NCH):
            o = ob[c % 2]
            g = k.lng[:, (li * 3 + 2) * NCH + c:(li * 3 + 2) * NCH + c + 1]
            bb = k.lnb[:, (li * 3 + 2) * NCH + c:(li * 3 + 2) * NCH + c + 1]
            P.op('vector', (lambda o, c, g, bb: lambda e: e.tensor_scalar(out=o[:], in0=k.nbuf[:, c, 0:SEQ], scalar1=g, scalar2=bb, op0=ALU.mult, op1=ALU.add))(o, c, g, bb),
                 reads=[('n', c, b.key) for b in LAT_BLKS] + ['lng', 'lnb'], writes=[('fo', c % 2)])
            d = P.op('sync', (lambda o, c: lambda e: e.dma_start(out=k.out[c * 128:(c + 1) * 128, :], in_=o[:]))(o, c),
                     reads=[('fo', c % 2)], writes=[('out', c)], dma=True)
            k.final_waits.append(('sync', d))
        if dbg:
            oc = k.sb('foc', [128, NCH, CTX], F32, st)
            for c in range(NCH):
                g = k.lng[:, (li * 3 + 2) * NCH + c:(li * 3 + 2) * NCH + c + 1]
                bb = k.lnb[:, (li * 3 + 2) * NCH + c:(li * 3 + 2) * NCH + c + 1]
                P.op('vector', (lambda c, g, bb: lambda e: e.tensor_scalar(out=oc[:, c, :], in0=k.nbuf[:, c, SEQ:TT], scalar1=g, scalar2=bb, op0=ALU.mult, op1=ALU.add))(c, g, bb),
                     reads=[('n', c, CTX_BLK.key), 'lng', 'lnb'], writes=[('foc', c)])
            d = P.op('sync', lambda e: e.dma_start(out=k.dbgc.rearrange("(c p) t -> p c t", p=128), in_=oc[:]),
                     reads=[('foc', c) for c in range(NCH)], writes=['dbgc'], dma=True)
            k.final_waits.append(('sync', d))


def diff_attn(k, li, use_ctx, ctx_next):
    P = k.P
    nc = k.nc
    sub = 1
    lam_init = 0.8 - 0.6 * math.exp(-0.3 * li)
    allb = LAT_BLKS + [CTX_BLK]
    wqkv = k.dram['da_w_qkv']
    with ExitStack() as st:
        hb = k.sb('da_h', [128, NCH, TT], BF16, st)
        rope = k.sb('da_rope', [128, 2 * SEQ], BF16, st)
        qT = k.sb('da_q', [128, 2, TT], BF16, st)
        kT = k.sb('da_k', [128, 2, TT], BF16, st)
        vtm = k.sb('da_v', [128, 18, 256], BF16, st)
        ob = k.sb('da_o', [128, 2, TT], BF16, st)
        pt = [k.sb('da_p%d' % i, [128, 2, 512], BF16, st) for i in range(3)]
        zacc = k.sb('da_zacc', [128, 512], F32, st)
        zhl = k.sb('da_zhl', [128, 2, 512], BF16, st)
        onesf = k.sb('da_onesf', [128, 128], F32, st)
        tt = [k.sb('da_t%d' % i, [128, 512], F32, st) for i in range(4)]
        sqb = k.sb('da_sq', [128, 512], BF16, st)
        lam = k.sb('da_lam', [128, 8], F32, st)
        lamT = k.sb('da_lamT', [64, 4], F32, st)
        subg = k.sb('da_subg', [128, 1], F32, st)
        onef = k.sb('da_onef', [64, 128], F32, st)
        P.op('sync', lambda e: e.dma_start(out=rope[:], in_=k.dram['rope_da']), writes=['rope'], dma=True)
        P.op('sync', lambda e: e.dma_start(out=lamT[:], in_=k.dram['da_lamT']), writes=['lamT'], dma=True)
        P.op('sync', lambda e: e.dma_start(out=subg[:], in_=k.dram['da_subg']), writes=['subg'], dma=True)
        P.op('vector', lambda e: e.memset(onef[:], 1.0), writes=['onef'])
        P.op('vector', lambda e: e.memset(onesf[:], 1.0), writes=['onesf'])
        P.op('vector', lambda e: e.tensor_tensor(out=lamT[:, 0:1], in0=lamT[:, 0:1], in1=lamT[:, 1:2], op=ALU.mult), reads=['lamT'], writes=['lamT'])
        P.op('vector', lambda e: e.tensor_tensor(out=lamT[:, 1:2], in0=lamT[:, 2:3], in1=lamT[:, 3:4], op=ALU.mult), reads=['lamT'], writes=['lamT'])
        pl = next_ps(k)
        P.op('tensor', lambda e: e.matmul(k.ps[pl][:, 0:2], lhsT=onef[:], rhs=lamT[:, 0:2], start=True, stop=True), reads=['lamT', 'onef'], writes=[('ps', pl)])
        P.op('scalar', lambda e: e.activation(out=lam[:, 0:2], in_=k.ps[pl][:, 0:2], func=AF.Exp), reads=[('ps', pl)], writes=['lam'])
        P.op('vector', lambda e: e.tensor_tensor(out=lam[:, 2:3], in0=lam[:, 1:2], in1=lam[:, 0:1], op=ALU.subtract), reads=['lam'], writes=['lam'])
        P.op('vector', lambda e: e.tensor_scalar(out=lam[:, 2:3], in0=lam[:, 2:3], scalar1=-lam_init, scalar2=None, op0=ALU.add), reads=['lam'], writes=['lam'])
        P.op('vector', lambda e: e.tensor_scalar(out=lam[:, 3:4], in0=subg[:, 0:1], scalar1=1.0 - lam_init, scalar2=None, op0=ALU.mult), reads=['subg', 'lam'], writes=['lam'])
        neg_lam = lam[:, 2:3]
        gsub = lam[:, 3:4]

        make_h(k, li, sub, allb, hb, 0)
        make_xa(k, li, sub, allb)
        cosT = rope[:, 0:SEQ]
        sinS = rope[:, SEQ:2 * SEQ]
        pi = [0]
        ti = [0]
        sr = [0]
        fin = []

        def nxt_p():
            i = pi[0] % len(pt)
            pi[0] += 1
            return pt[i], ('da_p', i)

        def nxt_t():
            i = ti[0] % len(tt)
            ti[0] += 1
            return tt[i], ('da_t', i)

        for hp in range(4):
            for (dstT, c0, wsw, dkey) in ((qT, hp * 256, k.dram['da_wq_sw'], 'q'), (kT, D + hp * 256, k.dram['da_wk_sw'], 'k')):
                s1, w1 = load_w(k, wqkv[:, c0:c0 + 256], NCH, 256)
                s2, w2 = load_w(k, wsw[:, hp * 256:(hp + 1) * 256], NCH, 256)
                for j in range(2):
                    for b in allb:
                        pa = next_ps(k)
                        for kc in range(NCH):
                            P.op('tensor', (lambda pa, w1, kc, j, b: lambda e: e.matmul(
                                k.ps[pa][:, 0:b.n], lhsT=w1[:, kc, j * 128:(j + 1) * 128], rhs=hb[:, kc, b.sl],
                                start=(kc == 0), stop=(kc == NCH - 1)))(pa, w1, kc, j, b),
                                reads=[('ws', s1), ('h', kc, b.key)], writes=[('ps', pa)])
                        if b.s == 1:
                            P.op('vector', (lambda pa, dstT, j, b: lambda e: e.tensor_copy(out=dstT[:, j, b.sl], in_=k.ps[pa][:, 0:b.n]))(pa, dstT, j, b),
                                 reads=[('ps', pa)], writes=[(dkey, j, b.key)])
                            continue
                        pbk = next_ps(k)
                        for kc in range(NCH):
                            P.op('tensor', (lambda pbk, w2, kc, j, b: lambda e: e.matmul(
                                k.ps[pbk][:, 0:b.n], lhsT=w2[:, kc, j * 128:(j + 1) * 128], rhs=hb[:, kc, b.sl],
                                start=(kc == 0), stop=(kc == NCH - 1)))(pbk, w2, kc, j, b),
                                reads=[('ws', s2), ('h', kc, b.key)], writes=[('ps', pbk)])
                        t1, k1 = nxt_t()
                        t2, k2 = nxt_t()
                        P.op('vector', (lambda t1, pa, b: lambda e: e.tensor_tensor(out=t1[:, 0:b.n], in0=k.ps[pa][:, 0:b.n], in1=cosT[:, b.sl], op=ALU.mult))(t1, pa, b),
                             reads=[('ps', pa), 'rope'], writes=[k1])
                        P.op('vector', (lambda t2, pbk, b: lambda e: e.tensor_tensor(out=t2[:, 0:b.n], in0=k.ps[pbk][:, 0:b.n], in1=sinS[:, b.sl], op=ALU.mult))(t2, pbk, b),
                             reads=[('ps', pbk), 'rope'], writes=[k2])
                        P.op('vector', (lambda t1, t2, dstT, j, b: lambda e: e.tensor_tensor(out=dstT[:, j, b.sl], in0=t1[:, 0:b.n], in1=t2[:, 0:b.n], op=ALU.add))(t1, t2, dstT, j, b),
                             reads=[k1, k2], writes=[(dkey, j, b.key)])
            s3, w3 = load_w(k, wqkv[:, 2 * D + hp * 256:2 * D + (hp + 1) * 256], NCH, 256)
            for kc18 in range(18):
                pv = next_ps(k)
                bkey = allb[kc18 // 4].key if kc18 < 16 else CTX_BLK.key
                for kc in range(NCH):
                    P.op('tensor', (lambda pv, kc, kc18, w3: lambda e: e.matmul(
                        k.ps[pv][:, 0:256], lhsT=hb[:, kc, kc18 * 128:(kc18 + 1) * 128], rhs=w3[:, kc, :],
                        start=(kc == 0), stop=(kc == NCH - 1)))(pv, kc, kc18, w3),
                        reads=[('ws', s3), ('h', kc, bkey)], writes=[('ps', pv)])
                P.op('vector', (lambda pv, kc18: lambda e: e.tensor_copy(out=vtm[:, kc18, :], in_=k.ps[pv][:, 0:256]))(pv, kc18),
                     reads=[('ps', pv)], writes=[('v', kc18)])
            qblks = allb if ctx_next else LAT_BLKS
            spairs = [(4, 5), (6, 7)]
            for j in range(2):
                for b in qblks:
                    kcs = list(range(18)) if b.s == 0 else [16, 17]
                    n = b.n
                    pend = []
                    nk = len(kcs)
                    for idx in range(nk + 1):
                        if idx < nk:
                            kc18 = kcs[idx]
                            kblk = allb[kc18 // 4].key if kc18 < 16 else CTX_BLK.key
                            pair = spairs[sr[0] % 2]
                            sr[0] += 1
                            for m in range(2):
                                pr = slice(64 * m, 64 * m + 64)
                                P.op('tensor', (lambda m, pr, kc18, j, b, sbm: lambda e: e.matmul(
                                    k.ps[sbm][:, 0:b.n], lhsT=kT[pr, j, kc18 * 128:(kc18 + 1) * 128], rhs=qT[pr, j, b.sl],
                                    start=True, stop=True))(m, pr, kc18, j, b, pair[m]),
                                    reads=[('k', j, kblk), ('q', j, b.key)], writes=[('ps', pair[m])])
                            pt_, pk = nxt_p()
                            for m in range(2):
                                P.op('scalar', (lambda pt_, pm, n, m: lambda e: e.activation(out=pt_[:, m, 0:n], in_=k.ps[pm][:, 0:n], func=AF.Exp, scale=0.125))(pt_, pair[m], n, m),
                                     reads=[('ps', pair[m])], writes=[pk])
                            if idx == 0:
                                P.op('vector', (lambda pt_, n: lambda e: e.tensor_copy(out=zacc[:, 0:n], in_=pt_[:, 0, 0:n]))(pt_, n), reads=[pk], writes=['zacc'])
                            else:
                                P.op('vector', (lambda pt_, n: lambda e: e.tensor_tensor(out=zacc[:, 0:n], in0=zacc[:, 0:n], in1=pt_[:, 0, 0:n], op=ALU.add))(pt_, n), reads=[pk, 'zacc'], writes=['zacc'])
                            pend.append((pt_, pk, kc18, idx))
                            if fin and idx >= 1:
                                fin.pop(0)()
                        if idx >= 1:
                            pt_, pk, kc18, pidx = pend.pop(0)
                            first = (pidx == 0)
                            last = (pidx == nk - 1)
                            for m in range(2):
                                P.op('tensor', (lambda pt_, kc18, j, m, n, first, last: lambda e: e.matmul(
                                    k.ps[m][:, 0:n], lhsT=vtm[:, kc18, j * 128:(j + 1) * 128], rhs=pt_[:, m, 0:n], start=first, stop=last))(pt_, kc18, j, m, n, first, last),
                                    reads=[pk, ('v', kc18)], writes=[('ps', m)])
                            P.op('tensor', (lambda pt_, n, first, last: lambda e: e.matmul(
                                k.ps[2][:, 0:n], lhsT=k.one1[:], rhs=pt_[:, 1, 0:n], start=first, stop=last))(pt_, n, first, last),
                                reads=[pk, 'one1'], writes=[('ps', 2)])
                    while fin:
                        fin.pop(0)()
                    P.op('vector', (lambda n: lambda e: e.tensor_copy(out=zhl[:, 0, 0:n], in_=zacc[:, 0:n]))(n), reads=['zacc'], writes=['zhl'])
                    P.op('vector', (lambda n: lambda e: e.tensor_tensor(out=zhl[:, 1, 0:n], in0=zacc[:, 0:n], in1=zhl[:, 0, 0:n], op=ALU.subtract))(n), reads=['zacc', 'zhl'], writes=['zhl'])
                    for hl in range(2):
                        P.op('tensor', (lambda n, hl: lambda e: e.matmul(k.ps[3][:, 0:n], lhsT=k.one1[:], rhs=zhl[:, hl, 0:n], start=(hl == 0), stop=(hl == 1)))(n, hl),
                             reads=['zhl', 'one1'], writes=[('ps', 3)])
                    while fin:
                        fin.pop(0)()
                    c0, c1, c2, c3 = tt
                    P.op('vector', (lambda n: lambda e: e.tensor_copy(out=c0[:, 0:n], in_=k.ps[0][:, 0:n]))(n), reads=[('ps', 0)], writes=[('da_t', 0)])
                    P.op('vector', (lambda n: lambda e: e.tensor_copy(out=c1[:, 0:n], in_=k.ps[1][:, 0:n]))(n), reads=[('ps', 1)], writes=[('da_t', 1)])
                    P.op('scalar', (lambda n: lambda e: e.activation(out=c2[:, 0:n], in_=k.ps[2][:, 0:n], func=AF.Ln))(n), reads=[('ps', 2)], writes=[('da_t', 2)])

                    def mk(n, j, b):
                        return [
                            lambda: P.op('scalar', lambda e: e.activation(out=c3[:, 0:n], in_=k.ps[3][:, 0:n], func=AF.Ln), reads=[('ps', 3)], writes=[('da_t', 3)]),
                            lambda: P.op('scalar', lambda e: e.activation(out=c2[:, 0:n], in_=c2[:, 0:n], func=AF.Exp, scale=-1.0), reads=[('da_t', 2)], writes=[('da_t', 2)]),
                            lambda: P.op('scalar', lambda e: e.activation(out=c3[:, 0:n], in_=c3[:, 0:n], func=AF.Exp, scale=-1.0), reads=[('da_t', 3)], writes=[('da_t', 3)]),
                            lambda: P.op('vector', lambda e: e.tensor_tensor(out=c0[:, 0:n], in0=c0[:, 0:n], in1=c3[:, 0:n], op=ALU.mult), reads=[('da_t', 0), ('da_t', 3)], writes=[('da_t', 0)]),
                            lambda: P.op('vector', lambda e: e.tensor_tensor(out=c1[:, 0:n], in0=c1[:, 0:n], in1=c2[:, 0:n], op=ALU.mult), reads=[('da_t', 1), ('da_t', 2)], writes=[('da_t', 1)]),
                            lambda: P.op('vector', lambda e: e.scalar_tensor_tensor(out=c0[:, 0:n], in0=c1[:, 0:n], scalar=neg_lam, in1=c0[:, 0:n], op0=ALU.mult, op1=ALU.add),
                                         reads=[('da_t', 0), ('da_t', 1), 'lam'], writes=[('da_t', 0)]),
                            lambda: P.op('scalar', lambda e: e.activation(out=sqb[:, 0:n], in_=c0[:, 0:n], func=AF.Square), reads=[('da_t', 0)], writes=['da_sq']),
                            lambda: P.op('tensor', lambda e: e.matmul(k.ps[3][:, 0:n], lhsT=k.one1[:], rhs=sqb[:, 0:n], start=True, stop=True), reads=['da_sq', 'one1'], writes=[('ps', 3)]),
                            lambda: P.op('scalar', lambda e: e.activation(out=c1[:, 0:n], in_=k.ps[3][:, 0:n], func=AF.Ln, bias=k.epsc[:, 0:1], scale=1.0 / 128.0),
                                         reads=[('ps', 3), 'epsc'], writes=[('da_t', 1)]),
                            lambda: P.op('scalar', lambda e: e.activation(out=c1[:, 0:n], in_=c1[:, 0:n], func=AF.Exp, scale=-0.5), reads=[('da_t', 1)], writes=[('da_t', 1)]),
                            lambda: P.op('vector', lambda e: e.scalar_tensor_tensor(out=ob[:, j, b.sl], in0=c0[:, 0:n], scalar=gsub, in1=c1[:, 0:n], op0=ALU.mult, op1=ALU.mult),
                                         reads=[('da_t', 0), ('da_t', 1), 'lam'], writes=[('o', j, b.key)]),
                        ]
                    fin.extend(mk(n, j, b))
            while fin:
                fin.pop(0)()
            wo_part = k.dram['da_w_o'][hp * 256:(hp + 1) * 256, :]
            for grp in ([LAT_BLKS[0], LAT_BLKS[1]], [LAT_BLKS[2], LAT_BLKS[3]]) + (([CTX_BLK],) if ctx_next else ()):
                out_proj_residual(k, li, sub, grp, wo_part, 2,
                                  lambda kc, b: ob[:, kc, b.sl], lambda kc, b: [('o', kc, b.key)], None, do_ln=False)
    P.barrier()
    with ExitStack() as st:
        lnt = alloc_ln_tmps(k, st)
        for b in (allb if ctx_next else LAT_BLKS):
            if b.s == 0 and b.t0 < 1024:
                layer_norm_blk(k, b, lnt)
            else:
                k.ln_pending.append(b)


def retention(k, li, use_ctx, ctx_next):
    P = k.P
    sub = 1
    allb = LAT_BLKS + [CTX_BLK]
    w_in = k.dram['rt_w_in']
    with ExitStack() as st:
        hb = k.sb('rt_h', [128, NCH, TT], BF16, st)
        cst = k.sb('rt_cst', [128, 1024 + 4 + 16 + 18], F32, st)
        qT = k.sb('rt_q', [128, 2, SEQ], BF16, st)
        kT = k.sb('rt_k', [128, 2, TT], BF16, st)
        vtm = k.sb('rt_v', [128, 18, 512], BF16, st)
        lg = k.sb('rt_lg', [128, 8], F32, st)
        gng = k.sb('rt_gng', [128, 16], F32, st)
        ksf = k.sb('rt_ksf', [128, 16], F32, st)
        ksb = k.sb('rt_ksb', [128, 18], F32, st)
        o512 = k.sb('rt_o512', [128, 128], BF16, st)
        P.op('sync', lambda e: e.dma_start(out=cst[:], in_=k.dram['rt_const']), writes=['rt_cst'], dma=True)
        P.op('sync', lambda e: e.dma_start(out=lg[:], in_=k.dram['rt_decay']), writes=['rt_lg'], dma=True)
        P.op('sync', lambda e: e.dma_start(out=gng[:], in_=k.dram['rt_gng']), writes=['rt_gng'], dma=True)
        P.op('vector', lambda e: e.memset(o512[:], 1.0 / 512.0), writes=['o512'])
        io_f = cst[:, 0:512]
        io_b = cst[:, 512:1024]
        offs = cst[:, 1024:1028]
        E_f = cst[:, 1028:1044]
        E_b = cst[:, 1044:1062]
        P.op('scalar', lambda e: e.activation(out=lg[:], in_=lg[:], func=AF.Exp, scale=-1.0), reads=['rt_lg'], writes=['rt_lg'])
        P.op('scalar', lambda e: e.activation(out=lg[:], in_=lg[:], func=AF.Ln, bias=1.0, scale=1.0), reads=['rt_lg'], writes=['rt_lg'])
        P.op('vector', lambda e: e.tensor_scalar(out=lg[:], in0=lg[:], scalar1=-1.0, scalar2=None, op0=ALU.mult), reads=['rt_lg'], writes=['rt_lg'])
        make_h(k, li, sub, allb, hb, 0)
        make_xa(k, li, sub, LAT_BLKS)
        sr = [0]

        def sbank():
            b_ = 4 + (sr[0] % 4)
            sr[0] += 1
            return b_

        for hd in range(4):
            lgf = lg[:, hd:hd + 1]
            lgb = lg[:, 4 + hd:5 + hd]
            P.op('scalar', (lambda lgf: lambda e: e.activation(out=ksf[:], in_=E_f, func=AF.Exp, scale=lgf))(lgf), reads=['rt_cst', 'rt_lg'], writes=['ksf'])
            P.op('scalar', (lambda lgb: lambda e: e.activation(out=ksb[:], in_=E_b, func=AF.Exp, scale=lgb))(lgb), reads=['rt_cst', 'rt_lg'], writes=['ksb'])
            P.op('vector', lambda e: e.tensor_scalar(out=ksf[:], in0=ksf[:], scalar1=1.0 / 16.0, scalar2=None, op0=ALU.mult), reads=['ksf'], writes=['ksf'])
            P.op('vector', lambda e: e.tensor_scalar(out=ksb[:], in0=ksb[:], scalar1=1.0 / 16.0, scalar2=None, op0=ALU.mult), reads=['ksb'], writes=['ksb'])
            with ExitStack() as s1:
                rope = k.sb('rt_rope', [128, 2 * SEQ], BF16, s1)
                tt = [k.sb('rt_t%d' % i, [128, 512], F32, s1) for i in range(4)]
                P.op('sync', lambda e: e.dma_start(out=rope[:], in_=k.dram['rope_rt']), writes=['rope'], dma=True)
                cosT = rope[:, 0:SEQ]
                sinT = rope[:, SEQ:2 * SEQ]
                for (dstT, c0, dkey, blks_) in ((qT, hd * 256, 'q', LAT_BLKS), (kT, D + hd * 256, 'k', allb)):
                    s_, w_ = load_w(k, w_in[:, c0:c0 + 256], NCH, 256)
                    for b in blks_:
                        pa, pb_ = next_ps(k, 2)
                        for j, pp in enumerate((pa, pb_)):
                            for kc in range(NCH):
                                P.op('tensor', (lambda pp, w_, kc, j, b: lambda e: e.matmul(
                                    k.ps[pp][:, 0:b.n], lhsT=w_[:, kc, j * 128:(j + 1) * 128], rhs=hb[:, kc, b.sl],
                                    start=(kc == 0), stop=(kc == NCH - 1)))(pp, w_, kc, j, b),
                                    reads=[('ws', s_), ('h', kc, b.key)], writes=[('ps', pp)])
                        if b.s == 1:
                            for j, pp in enumerate((pa, pb_)):
                                P.op('vector', (lambda pp, dstT, j, b: lambda e: e.tensor_copy(out=dstT[:, j, b.sl], in_=k.ps[pp][:, 0:b.n]))(pp, dstT, j, b),
                                     reads=[('ps', pp)], writes=[(dkey, j, b.key)])
                            continue
                        for j in range(2):
                            t1, t2 = tt[2 * j], tt[2 * j + 1]
                            k1, k2 = ('rt_t', 2 * j), ('rt_t', 2 * j + 1)
                            tabA = cosT if j == 0 else sinT
                            tabB = sinT if j == 0 else cosT
                            opf = ALU.subtract if j == 0 else ALU.add
                            P.op('vector', (lambda t1, pa, b, tabA: lambda e: e.tensor_tensor(out=t1[:, 0:b.n], in0=k.ps[pa][:, 0:b.n], in1=tabA[:, b.sl], op=ALU.mult))(t1, pa, b, tabA),
                                 reads=[('ps', pa), 'rope'], writes=[k1])
                            P.op('vector', (lambda t2, pb_, b, tabB: lambda e: e.tensor_tensor(out=t2[:, 0:b.n], in0=k.ps[pb_][:, 0:b.n], in1=tabB[:, b.sl], op=ALU.mult))(t2, pb_, b, tabB),
                                 reads=[('ps', pb_), 'rope'], writes=[k2])
                            P.op('vector', (lambda t1, t2, dstT, j, b, opf: lambda e: e.tensor_tensor(out=dstT[:, j, b.sl], in0=t1[:, 0:b.n], in1=t2[:, 0:b.n], op=opf))(t1, t2, dstT, j, b, opf),
                                 reads=[k1, k2], writes=[(dkey, j, b.key)])
                wv = [load_w(k, w_in[:, 2 * D + hd * 512 + hh * 256:2 * D + hd * 512 + (hh + 1) * 256], NCH, 256) for hh in range(2)]
                for kc18 in range(18):
                    pv = next_ps(k)
                    bkey = allb[kc18 // 4].key if kc18 < 16 else CTX_BLK.key
                    for hh in range(2):
                        s_, w_ = wv[hh]
                        for kc in range(NCH):
                            P.op('tensor', (lambda pv, kc, kc18, w_, hh: lambda e: e.matmul(
                                k.ps[pv][:, hh * 256:(hh + 1) * 256], lhsT=hb[:, kc, kc18 * 128:(kc18 + 1) * 128], rhs=w_[:, kc, :],
                                start=(kc == 0), stop=(kc == NCH - 1)))(pv, kc, kc18, w_, hh),
                                reads=[('ws', s_), ('h', kc, bkey)], writes=[('ps', pv)])
                    P.op('scalar', (lambda pv, kc18: lambda e: e.activation(out=vtm[:, kc18, :], in_=k.ps[pv][:], func=AF.Copy))(pv, kc18),
                         reads=[('ps', pv)], writes=[('v', kc18)])
            P.barrier()
            with ExitStack() as s2:
                masks = k.sb('rt_mask', [128, 4, 512], BF16, s2)
                Dq = k.sb('rt_dq', [128, 1024], F32, s2)
                pt = [k.sb('rt_p%d' % i, [128, 512], BF16, s2) for i in range(3)]
                obf = k.sb('rt_obf', [128, 4, 512], BF16, s2)
                sqf = k.sb('rt_sqf', [128, 4, 512], BF16, s2)
                rstd = k.sb('rt_rstd', [128, 512], F32, s2)
                ta = k.sb('rt_ta', [128, 512], F32, s2)
                tb_ = k.sb('rt_tb', [128, 512], F32, s2)
                tc_ = k.sb('rt_tc', [128, 512], F32, s2)
                zb = obf
                P.op('scalar', (lambda lgf: lambda e: e.activation(out=Dq[:, 0:512], in_=io_f, func=AF.Exp, scale=lgf))(lgf), reads=['rt_cst', 'rt_lg'], writes=['dq'])
                P.op('scalar', (lambda lgb: lambda e: e.activation(out=Dq[:, 512:1024], in_=io_b, func=AF.Exp, scale=lgb))(lgb), reads=['rt_cst', 'rt_lg'], writes=['dq'])
                for v in range(4):
                    ov = offs[:, v:v + 1]
                    P.op('vector', (lambda ov: lambda e: e.tensor_scalar(out=tc_[:], in0=io_f, scalar1=ov, scalar2=None, op0=ALU.subtract))(ov), reads=['rt_cst'], writes=['tc'])
                    P.op('vector', lambda e: e.tensor_scalar(out=ta[:], in0=tc_[:], scalar1=0.0, scalar2=None, op0=ALU.max), reads=['tc'], writes=['ta'])
                    P.op('scalar', (lambda lgf: lambda e: e.activation(out=ta[:], in_=ta[:], func=AF.Exp, scale=lgf))(lgf), reads=['ta', 'rt_lg'], writes=['ta'])
                    P.op('vector', lambda e: e.tensor_scalar(out=tb_[:], in0=tc_[:], scalar1=0.0, scalar2=None, op0=ALU.is_ge), reads=['tc'], writes=['tb'])
                    P.op('vector', lambda e: e.tensor_tensor(out=ta[:], in0=ta[:], in1=tb_[:], op=ALU.mult), reads=['ta', 'tb'], writes=['ta'])
                    P.op('vector', lambda e: e.tensor_scalar(out=tb_[:], in0=tc_[:], scalar1=-1.0, scalar2=0.0, op0=ALU.mult, op1=ALU.max), reads=['tc'], writes=['tb'])
                    P.op('scalar', (lambda lgb: lambda e: e.activation(out=tb_[:], in_=tb_[:], func=AF.Exp, scale=lgb))(lgb), reads=['tb', 'rt_lg'], writes=['tb'])
                    P.op('vector', lambda e: e.tensor_scalar(out=tc_[:], in0=tc_[:], scalar1=0.0, scalar2=None, op0=ALU.is_le), reads=['tc'], writes=['tc'])
                    P.op('vector', lambda e: e.tensor_tensor(out=tb_[:], in0=tb_[:], in1=tc_[:], op=ALU.mult), reads=['tb', 'tc'], writes=['tb'])
                    P.op('vector', lambda e: e.tensor_tensor(out=ta[:], in0=ta[:], in1=tb_[:], op=ALU.add), reads=['ta', 'tb'], writes=['ta'])
                    P.op('vector', (lambda v: lambda e: e.tensor_scalar(out=masks[:, v, :], in0=ta[:], scalar1=1.0 / 16.0, scalar2=None, op0=ALU.mult))(v), reads=['ta'], writes=[('mask', v)])
                pi = [0]
                for bi, b in enumerate(LAT_BLKS):
                    tiles = []
                    for cc in range(2):
                        tiles.append((16 + cc, 'f', 4 * bi + 2 - cc))
                    for kc in range(16):
                        if kc < 4 * bi:
                            tiles.append((kc, 'f', 4 * bi - kc))
                        elif kc > 4 * bi + 3:
                            tiles.append((kc, 'b', kc - 4 * bi))
                        else:
                            tiles.append((kc, 'd', kc - 4 * bi))
                    for cc in range(2):
                        tiles.append((16 + cc, 'b', 16 + cc - 4 * bi))
                    acc = [0, 1, 2, 3]
                    pend = None
                    for idx in range(len(tiles) + 1):
                        cur = None
                        if idx < len(tiles):
                            kc18, kind, par = tiles[idx]
                            kblk = allb[kc18 // 4].key if kc18 < 16 else CTX_BLK.key
                            if idx in (2, 7, 12):
                                bg_step(k, sbank())
                            elif idx == 17:
                                bg_drain(k, sbank())
                            sbk = sbank()
                            for c in range(2):
                                P.op('tensor', (lambda c, kc18, b, sbk: lambda e: e.matmul(
                                    k.ps[sbk][:], lhsT=kT[:, c, kc18 * 128:(kc18 + 1) * 128], rhs=qT[:, c, b.sl],
                                    start=(c == 0), stop=(c == 1)))(c, kc18, b, sbk),
                                    reads=[('k', c, kblk), ('q', c, b.key)], writes=[('ps', sbk)])
                            p_ = pt[pi[0] % 3]
                            pk = ('rt_p', pi[0] % 3)
                            pi[0] += 1
                            if kind == 'f':
                                P.op('vector', (lambda p_, sbk, par: lambda e: e.scalar_tensor_tensor(out=p_[:], in0=k.ps[sbk][:], scalar=ksf[:, par:par + 1], in1=Dq[:, 0:512], op0=ALU.mult, op1=ALU.mult))(p_, sbk, par),
                                     reads=[('ps', sbk), 'ksf', 'dq'], writes=[pk])
                            elif kind == 'b':
                                P.op('vector', (lambda p_, sbk, par: lambda e: e.scalar_tensor_tensor(out=p_[:], in0=k.ps[sbk][:], scalar=ksb[:, par:par + 1], in1=Dq[:, 512:1024], op0=ALU.mult, op1=ALU.mult))(p_, sbk, par),
                                     reads=[('ps', sbk), 'ksb', 'dq'], writes=[pk])
                            else:
                                P.op('vector', (lambda p_, sbk, par: lambda e: e.tensor_tensor(out=p_[:], in0=k.ps[sbk][:], in1=masks[:, par, :], op=ALU.mult))(p_, sbk, par),
                                     reads=[('ps', sbk), ('mask', par)], writes=[pk])
                            cur = (p_, pk, kc18, idx)
                        if pend is not None:
                            p_, pk, kc18, pidx = pend
                            for e_ in range(4):
                                P.op('tensor', (lambda p_, kc18, e_, pidx: lambda e: e.matmul(
                                    k.ps[acc[e_]][:], lhsT=vtm[:, kc18, e_ * 128:(e_ + 1) * 128], rhs=p_[:],
                                    start=(pidx == 0), stop=(pidx == len(tiles) - 1)))(p_, kc18, e_, pidx),
                                    reads=[pk, ('v', kc18)], writes=[('ps', acc[e_])])
                        pend = cur
                    for e_ in range(4):
                        P.op('scalar', (lambda e_: lambda e: e.activation(out=obf[:, e_, :], in_=k.ps[acc[e_]][:], func=AF.Copy))(e_), reads=[('ps', acc[e_])], writes=[('obf', e_)])
                        P.op('scalar', (lambda e_: lambda e: e.activation(out=sqf[:, e_, :], in_=k.ps[acc[e_]][:], func=AF.Square))(e_), reads=[('ps', acc[e_])], writes=[('sqf', e_)])
                    p1, p2 = sbank(), sbank()
                    for e_ in range(4):
                        P.op('tensor', (lambda e_, p1: lambda e: e.matmul(k.ps[p1][:], lhsT=o512[:], rhs=obf[:, e_, :], start=(e_ == 0), stop=(e_ == 3)))(e_, p1),
                             reads=[('obf', e_), 'o512'], writes=[('ps', p1)])
                    for e_ in range(4):
                        P.op('tensor', (lambda e_, p2: lambda e: e.matmul(k.ps[p2][:], lhsT=o512[:], rhs=sqf[:, e_, :], start=(e_ == 0), stop=(e_ == 3)))(e_, p2),
                             reads=[('sqf', e_), 'o512'], writes=[('ps', p2)])
                    P.op('scalar', (lambda p1: lambda e: e.activation(out=tc_[:], in_=k.ps[p1][:], func=AF.Copy))(p1), reads=[('ps', p1)], writes=['tc'])
                    P.op('vector', lambda e: e.tensor_tensor(out=ta[:], in0=tc_[:], in1=tc_[:], op=ALU.mult), reads=['tc'], writes=['ta'])
                    P.op('vector', (lambda p2: lambda e: e.tensor_tensor(out=ta[:], in0=k.ps[p2][:], in1=ta[:], op=ALU.subtract))(p2), reads=[('ps', p2), 'ta'], writes=['ta'])
                    P.op('scalar', lambda e: e.activation(out=ta[:], in_=ta[:], func=AF.Sqrt, bias=k.epsc[:, 0:1], scale=1.0), reads=['ta', 'epsc'], writes=['ta'])
                    P.op('vector', lambda e: e.reciprocal(out=rstd[:], in_=ta[:]), reads=['ta'], writes=['rt_rstd'])
                    P.op('vector', lambda e: e.scalar_tensor_tensor(out=tc_[:], in0=tc_[:], scalar=-1.0, in1=rstd[:], op0=ALU.mult, op1=ALU.mult), reads=['tc', 'rt_rstd'], writes=['tc'])
                    wg = [load_w(k, w_in[:, 4 * D + hd * 512 + hh * 256:4 * D + hd * 512 + (hh + 1) * 256], NCH, 256) for hh in range(2)]
                    for e_ in range(4):
                        pg = sbank()
                        s_, w_ = wg[e_ // 2]
                        for kc in range(NCH):
                            P.op('tensor', (lambda pg, kc, w_, e_, b: lambda e: e.matmul(
                                k.ps[pg][:], lhsT=w_[:, kc, (e_ % 2) * 128:(e_ % 2 + 1) * 128], rhs=hb[:, kc, b.sl],
                                start=(kc == 0), stop=(kc == NCH - 1)))(pg, kc, w_, e_, b),
                                reads=[('ws', s_), ('h', kc, b.key)], writes=[('ps', pg)])
                        P.op('scalar', (lambda pg: lambda e: e.activation(out=tb_[:], in_=k.ps[pg][:], func=AF.Silu))(pg), reads=[('ps', pg)], writes=['tb'])
                        P.op('vector', (lambda e_: lambda e: e.tensor_tensor(out=ta[:], in0=k.ps[acc[e_]][:], in1=rstd[:], op=ALU.mult))(e_), reads=[('ps', acc[e_]), 'rt_rstd'], writes=['ta'])
                        P.op('vector', lambda e: e.tensor_tensor(out=ta[:], in0=ta[:], in1=tc_[:], op=ALU.add), reads=['ta', 'tc'], writes=['ta'])
                        gcol = gng[:, hd * 4 + e_:hd * 4 + e_ + 1]
                        P.op('vector', (lambda e_, gcol: lambda e: e.scalar_tensor_tensor(out=zb[:, e_, :], in0=ta[:], scalar=gcol, in1=tb_[:], op0=ALU.mult, op1=ALU.mult))(e_, gcol),
                             reads=['ta', 'tb', 'rt_gng'], writes=[('obf', e_)])
                    out_proj_residual(k, li, sub, [b], k.dram['rt_w_o'][hd * 512:(hd + 1) * 512, :], 4,
                                      lambda kc, b: zb[:, kc, :], lambda kc, b: [('obf', kc)], None, do_ln=False)
            P.barrier()
    P.barrier()
    with ExitStack() as st:
        lnt = alloc_ln_tmps(k, st)
        for b in LAT_BLKS:
            if b.s == 0 and b.t0 < 1024:
                layer_norm_blk(k, b, lnt)
            else:
                k.ln_pending.append(b)


def hyena(k, li, use_ctx, ctx_next):
    P = k.P
    sub = 1
    allb = LAT_BLKS + [CTX_BLK]
    w_in = k.dram['hy_w_in']
    PI = math.pi
    UW = TT + 4

    def ucol(b):
        return 1 + b.t0 if b.s == 0 else SEQ + 3 + b.t0

    with ExitStack() as st:
        hb = k.sb('hy_h', [128, NCH, TT], BF16, st)
        h2T = k.sb('hy_h2T', [64, TT], BF16, st)
        cw = k.sb('hy_cw', [128, 72], F32, st)
        cb = k.sb('hy_cb', [128, 24], F32, st)
        tn = k.sb('hy_tn', [128, 18], F32, st)
        ident = k.sb('hy_ident', [128, 128], BF16, st)
        P.op('sync', lambda e: e.dma_start(out=cw[:], in_=k.dram['hy_cw']), writes=['hy_cw'], dma=True)
        P.op('sync', lambda e: e.dma_start(out=cb[:], in_=k.dram['hy_cb']), writes=['hy_cb'], dma=True)
        P.op('sync', lambda e: e.dma_start(out=tn[:], in_=k.dram['hy_tn']), writes=['hy_tn'], dma=True)
        P.op('sync', lambda e: e.dma_start(out=ident[:], in_=k.dram['ident']), writes=['hy_ident'], dma=True)
        make_h(k, li, sub, allb, hb, 0)
        make_xa(k, li, sub, allb)
        with ExitStack() as s0:
            fw1 = k.sb('hy_fw1', [33, 64], F32, s0)
            fw2 = k.sb('hy_fw2', [64, 64], F32, s0)
            v64 = k.sb('hy_v64', [64, 4], F32, s0)
            npi = k.sb('hy_npi', [64, 1], F32, s0)
            zt = [k.sb('hy_zt%d' % i, [33, 512], F32, s0) for i in range(2)]
            m1 = k.sb('hy_m1', [64, 512], F32, s0)
            m2 = k.sb('hy_m2', [64, 512], F32, s0)
            mw = k.sb('hy_mw', [64, 512], F32, s0)
            P.op('sync', lambda e: e.dma_start(out=fw1[:], in_=k.dram['hy_fw1']), writes=['fw1'], dma=True)
            P.op('sync', lambda e: e.dma_start(out=fw2[:], in_=k.dram['hy_fw2']), writes=['fw2'], dma=True)
            P.op('sync', lambda e: e.dma_start(out=v64[:], in_=k.dram['hy_vec64']), writes=['v64'], dma=True)
            P.op('vector', lambda e: e.memset(npi[:], -PI), writes=['npi'])
            for bi, b in enumerate(allb):
                z_ = zt[bi % 2]
                zk = ('zt', bi % 2)
                n = b.n
                P.op('sync', (lambda z_, b: lambda e: e.dma_start(out=z_[:, 0:b.n], in_=k.dram['hy_zT'][:, b.sl]))(z_, b), writes=[zk], dma=True)
                p1 = next_ps(k)
                P.op('tensor', (lambda p1, z_, n: lambda e: e.matmul(k.ps[p1][0:64, 0:n], lhsT=fw1[:], rhs=z_[:, 0:n], start=True, stop=True))(p1, z_, n),
                     reads=['fw1', zk], writes=[('ps', p1)])
                P.op('vector', (lambda p1, n: lambda e: e.tensor_scalar(out=m1[:, 0:n], in0=k.ps[p1][0:64, 0:n], scalar1=v64[:, 0:1], scalar2=v64[:, 1:2], op0=ALU.add, op1=ALU.mult))(p1, n),
                     reads=[('ps', p1), 'v64'], writes=['m1'])
                P.op('vector', (lambda n: lambda e: e.tensor_scalar(out=mw[:, 0:n], in0=m1[:, 0:n], scalar1=PI, scalar2=None, op0=ALU.is_gt))(n), reads=['m1'], writes=['mw'])
                P.op('vector', (lambda n: lambda e: e.scalar_tensor_tensor(out=m1[:, 0:n], in0=mw[:, 0:n], scalar=-2.0 * PI, in1=m1[:, 0:n], op0=ALU.mult, op1=ALU.add))(n), reads=['m1', 'mw'], writes=['m1'])
                P.op('vector', (lambda n: lambda e: e.tensor_scalar(out=mw[:, 0:n], in0=m1[:, 0:n], scalar1=-PI, scalar2=None, op0=ALU.is_lt))(n), reads=['m1'], writes=['mw'])
                P.op('vector', (lambda n: lambda e: e.scalar_tensor_tensor(out=m1[:, 0:n], in0=mw[:, 0:n], scalar=2.0 * PI, in1=m1[:, 0:n], op0=ALU.mult, op1=ALU.add))(n), reads=['m1', 'mw'], writes=['m1'])
                P.op('scalar', (lambda n: lambda e: e.activation(out=m1[:, 0:n], in_=m1[:, 0:n], func=AF.Sin))(n), reads=['m1'], writes=['m1'])
                p2 = next_ps(k)
                P.op('tensor', (lambda p2, n: lambda e: e.matmul(k.ps[p2][0:64, 0:n], lhsT=fw2[:], rhs=m1[:, 0:n], start=True, stop=True))(p2, n),
                     reads=['fw2', 'm1'], writes=[('ps', p2)])
                P.op('vector', (lambda p2, n: lambda e: e.tensor_scalar(out=m2[:, 0:n], in0=k.ps[p2][0:64, 0:n], scalar1=v64[:, 2:3], scalar2=v64[:, 3:4], op0=ALU.add, op1=ALU.mult))(p2, n),
                     reads=[('ps', p2), 'v64'], writes=['m2'])
                P.op('vector', (lambda n: lambda e: e.tensor_scalar(out=mw[:, 0:n], in0=m2[:, 0:n], scalar1=PI, scalar2=None, op0=ALU.is_gt))(n), reads=['m2'], writes=['mw'])
                P.op('vector', (lambda n: lambda e: e.scalar_tensor_tensor(out=m2[:, 0:n], in0=mw[:, 0:n], scalar=-2.0 * PI, in1=m2[:, 0:n], op0=ALU.mult, op1=ALU.add))(n), reads=['m2', 'mw'], writes=['m2'])
                P.op('vector', (lambda n: lambda e: e.tensor_scalar(out=mw[:, 0:n], in0=m2[:, 0:n], scalar1=-PI, scalar2=None, op0=ALU.is_lt))(n), reads=['m2'], writes=['mw'])
                P.op('vector', (lambda n: lambda e: e.scalar_tensor_tensor(out=m2[:, 0:n], in0=mw[:, 0:n], scalar=2.0 * PI, in1=m2[:, 0:n], op0=ALU.mult, op1=ALU.add))(n), reads=['m2', 'mw'], writes=['m2'])
                P.op('scalar', (lambda n, b: lambda e: e.activation(out=h2T[:, b.sl], in_=m2[:, 0:n], func=AF.Sin))(n, b), reads=['m2'], writes=[('h2T', b.key)])
        P.barrier()
        HS = os.environ.get('HYSTOP', '')

        def proj_conv(qi, fc, wslot, wcol, ub, ubk, emit_fn, tmps):
            s_, w_ = wslot
            for b in allb:
                pp = next_ps(k)
                for kc in range(NCH):
                    P.op('tensor', (lambda pp, kc, b, w_, wcol: lambda e: e.matmul(
                        k.ps[pp][:, 0:b.n], lhsT=w_[:, kc, wcol * 128:(wcol + 1) * 128], rhs=hb[:, kc, b.sl],
                        start=(kc == 0), stop=(kc == NCH - 1)))(pp, kc, b, w_, wcol),
                        reads=[('ws', s_), ('h', kc, b.key)], writes=[('ps', pp)])
                P.op('scalar', (lambda pp, b: lambda e: e.activation(out=ub[:, ucol(b):ucol(b) + b.n], in_=k.ps[pp][:, 0:b.n], func=AF.Copy))(pp, b),
                     reads=[('ps', pp)], writes=[(ubk, b.key)])
            w0 = cw[:, 0 * 24 + fc:0 * 24 + fc + 1]
            w1 = cw[:, 1 * 24 + fc:1 * 24 + fc + 1]
            w2 = cw[:, 2 * 24 + fc:2 * 24 + fc + 1]
            bia = cb[:, fc:fc + 1]
            for bi, b in enumerate(allb):
                t_, tk = tmps[bi % 2]
                o = ucol(b)
                nb_keys = [(ubk, bb.key) for bb in allb if bb.s == b.s and abs(bb.t0 - b.t0) <= 512] + [(ubk, 'pad'), 'hy_cw', 'hy_cb']
                P.op('scalar', (lambda t_, o, b: lambda e: e.activation(out=t_[:, 0:b.n], in_=ub[:, o:o + b.n], func=AF.Identity, bias=bia, scale=w1))(t_, o, b),
                     reads=nb_keys, writes=[tk])
                P.op('vector', (lambda t_, o, b: lambda e: e.scalar_tensor_tensor(out=t_[:, 0:b.n], in0=ub[:, o - 1:o - 1 + b.n], scalar=w0, in1=t_[:, 0:b.n], op0=ALU.mult, op1=ALU.add))(t_, o, b),
                     reads=nb_keys + [tk], writes=[tk])
                P.op('vector', (lambda t_, o, b: lambda e: e.scalar_tensor_tensor(out=t_[:, 0:b.n], in0=ub[:, o + 1:o + 1 + b.n], scalar=w2, in1=t_[:, 0:b.n], op0=ALU.mult, op1=ALU.add))(t_, o, b),
                     reads=nb_keys + [tk], writes=[tk])
                emit_fn(b, t_, tk)

        def zero_pads(ub, ubk):
            for c0 in (0, SEQ + 1, SEQ + 2, TT + 3):
                P.op('vector', (lambda c0: lambda e: e.memset(ub[:, c0:c0 + 1], 0.0))(c0), writes=[(ubk, 'pad')])

        for cg in range(4 if HS == '' else (0 if HS == 'mlp' else 1)):
            with ExitStack() as sY:
                Y = k.sb('hy_Y', [128, 18, 512], BF16, sY)
                with ExitStack() as sP:
                    ptm = k.sb('hy_ptm', [128, 18, 256], BF16, sP)
                    with ExitStack() as sA:
                        ub0 = k.sb('hy_ub0', [128, UW], F32, sA)
                        x1c = k.sb('hy_x1c', [128, TT], F32, sA)
                        pfm = k.sb('hy_pfm', [128, 2, TT], BF16, sA)
                        tA = [(k.sb('hy_ta%d' % i, [128, 512], F32, sA), ('hy_ta', i)) for i in range(2)]
                        zero_pads(ub0, 'ub0')
                        wx1 = load_w(k, w_in[:, D + cg * 256:D + (cg + 1) * 256], NCH, 256)
                        wv_ = load_w(k, w_in[:, 2 * D + cg * 256:2 * D + (cg + 1) * 256], NCH, 256)
                        for c2 in range(2):
                            def emit_x1(b, t_, tk):
                                P.op('scalar', (lambda b, t_: lambda e: e.activation(out=x1c[:, b.sl], in_=t_[:, 0:b.n], func=AF.Copy))(b, t_),
                                     reads=[tk], writes=[('x1c', b.key)])
                            proj_conv(1, 8 + cg * 2 + c2, wx1, c2, ub0, 'ub0', emit_x1, tA)

                            def emit_v(b, t_, tk, c2=c2):
                                P.op('vector', (lambda b, t_, c2: lambda e: e.tensor_tensor(out=pfm[:, c2, b.sl], in0=t_[:, 0:b.n], in1=x1c[:, b.sl], op=ALU.mult))(b, t_, c2),
                                     reads=[tk, ('x1c', b.key)], writes=[('pfm', c2, b.key)])
                            proj_conv(2, 16 + cg * 2 + c2, wv_, c2, ub0, 'ub0', emit_v, tA)
                        for tc in range(0, 18, 2):
                            pb = next_ps(k)
                            for dt_ in range(2):
                                bkey = allb[(tc + dt_) // 4].key if tc + dt_ < 16 else CTX_BLK.key
                                for c2 in range(2):
                                    P.op('tensor', (lambda pb, tc, dt_, c2: lambda e: e.matmul(
                                        k.ps[pb][:, dt_ * 256 + c2 * 128:dt_ * 256 + (c2 + 1) * 128], lhsT=pfm[:, c2, (tc + dt_) * 128:(tc + dt_ + 1) * 128], rhs=ident[:],
                                        start=True, stop=True))(pb, tc, dt_, c2),
                                        reads=[('pfm', c2, bkey), 'hy_ident'], writes=[('ps', pb)])
                            P.op('scalar', (lambda pb, tc: lambda e: e.activation(out=ptm[:, tc:tc + 2, :], in_=k.ps[pb][:].rearrange("p (a b) -> p a b", a=2), func=AF.Copy))(pb, tc),
                                 reads=[('ps', pb)], writes=[('ptm', tc), ('ptm', tc + 1)])
                    P.barrier()
                    if HS == 'A':
                        continue
                    with ExitStack() as sH:
                        hsum = k.sb('hy_hsum', [128, 18, 256], BF16, sH)
                        hdif = k.sb('hy_hdif', [128, 18, 256], BF16, sH)
                        with ExitStack() as sF:
                            w3c = k.sb('hy_w3c', [64, 512], BF16, sF)
                            dlc = k.sb('hy_dlc', [128, 256], F32, sF)
                            decs = [k.sb('hy_dec%d' % i, [128, 256], F32, sF) for i in range(2)]
                            fas = [k.sb('hy_fa%d' % i, [128, 256], F32, sF) for i in range(2)]
                            fbs = [k.sb('hy_fb%d' % i, [128, 256], F32, sF) for i in range(2)]
                            fcs = [k.sb('hy_fc%d' % i, [128, 256], F32, sF) for i in range(2)]
                            dsk = k.sb('hy_dsk', [1, 256], F32, sF)
                            P.op('sync', (lambda cg: lambda e: e.dma_start(out=dsk[:], in_=k.dram['hy_dskip'][:, cg * 256:(cg + 1) * 256]))(cg), writes=['hy_dsk'], dma=True)
                            w3f = k.sb('hy_w3f', [64, 512], F32, sF)
                            P.op('sync', (lambda cg: lambda e: e.dma_start(out=w3f[:, 0:256], in_=k.dram['hy_fw3'][:, cg * 256:(cg + 1) * 256]))(cg), writes=['w3f_f'], dma=True)
                            P.op('sync', (lambda cg: lambda e: e.dma_start(out=w3f[:, 256:512], in_=k.dram['hy_fw3'][:, D + cg * 256:D + (cg + 1) * 256]))(cg), writes=['w3f_b'], dma=True)
                            P.op('scalar', lambda e: e.activation(out=w3c[:, 0:256], in_=w3f[:, 0:256], func=AF.Copy), reads=['w3f_f'], writes=['w3c_f'])
                            P.op('scalar', lambda e: e.activation(out=w3c[:, 256:512], in_=w3f[:, 256:512], func=AF.Copy), reads=['w3f_b'], writes=['w3c_b'])
                            P.op('sync', (lambda cg: lambda e: e.dma_start(out=dlc[:], in_=k.dram['hy_delta'][:, cg * 256:(cg + 1) * 256]))(cg), writes=['dlc'], dma=True)
                            for tix in range(18):
                                pcol = tix * 128
                                bkey = allb[tix // 4].key if tix < 16 else CTX_BLK.key
                                first = tix in (0, 16)
                                par = tix % 2
                                dec, fa, fb_, fc_ = decs[par], fas[par], fbs[par], fcs[par]
                                kd, ka, kb, kc_ = ('dec', par), ('fa', par), ('fb', par), ('fc', par)
                                pf = next_ps(k)
                                for hh, wk in ((0, 'w3c_f'), (1, 'w3c_b')):
                                    P.op('tensor', (lambda pf, hh, pcol: lambda e: e.matmul(
                                        k.ps[pf][:, hh * 256:(hh + 1) * 256], lhsT=h2T[:, pcol:pcol + 128], rhs=w3c[:, hh * 256:(hh + 1) * 256],
                                        start=True, stop=True))(pf, hh, pcol),
                                        reads=[('h2T', bkey), wk], writes=[('ps', pf)])
                                P.op('scalar', (lambda tix, dec: lambda e: e.activation(out=dec[:], in_=dlc[:], func=AF.Exp, scale=tn[:, tix:tix + 1]))(tix, dec),
                                     reads=['dlc', 'hy_tn'], writes=[kd])
                                P.op('vector', (lambda pf, fa, dec: lambda e: e.tensor_tensor(out=fa[:], in0=k.ps[pf][:, 0:256], in1=dec[:], op=ALU.mult))(pf, fa, dec), reads=[('ps', pf), kd], writes=[ka])
                                P.op('vector', (lambda pf, fb_, dec: lambda e: e.tensor_tensor(out=fb_[:], in0=k.ps[pf][:, 256:512], in1=dec[:], op=ALU.mult))(pf, fb_, dec), reads=[('ps', pf), kd], writes=[kb])
                                if first:
                                    P.op('vector', (lambda fb_: lambda e: e.memset(fb_[0:1, :], 0.0))(fb_), reads=[kb], writes=[kb])
                                P.op('vector', (lambda fa, fb_, fc_: lambda e: e.tensor_tensor(out=fc_[:], in0=fa[:], in1=fb_[:], op=ALU.add))(fa, fb_, fc_), reads=[ka, kb], writes=[kc_])
                                if first:
                                    P.op('vector', (lambda fc_: lambda e: e.tensor_tensor(out=fc_[0:1, :], in0=fc_[0:1, :], in1=dsk[0:1, :], op=ALU.add))(fc_),
                                         reads=[kc_, 'hy_dsk'], writes=[kc_])
                                P.op('scalar', (lambda tix, fc_: lambda e: e.activation(out=hsum[:, tix, :], in_=fc_[:], func=AF.Copy))(tix, fc_), reads=[kc_], writes=[('hsum', tix)])
                                P.op('vector', (lambda tix, fa, fb_: lambda e: e.tensor_tensor(out=hdif[:, tix, :], in0=fa[:], in1=fb_[:], op=ALU.subtract))(tix, fa, fb_), reads=[ka, kb], writes=[('hdif', tix)])
                        P.barrier()
                        if HS == 'F':
                            continue
                        with ExitStack() as sB:
                            fs = [k.sb('hy_fs%d' % i, [128, 2048], BF16, sB) for i in range(3)]
                            Gs = k.sb('hy_Gs', [128, 512], F32, sB)
                            tq = [k.sb('hy_tq%d' % i, [128, 256], F32, sB) for i in range(2)]
                            fsi = 0
                            for (nkch, ntch, t0x, src) in ((16, 16, 0, 'dft_f'), (2, 2, 16, 'dft_fc')):
                                for kch in range(nkch):
                                    bg_step(k, next_ps(k))
                                    pu, pg = next_ps(k, 2)
                                    nel = ntch * 128
                                    for m in range(2):
                                        f_ = fs[fsi % 3]
                                        fk = ('fs', fsi % 3)
                                        fsi += 1
                                        P.op('sync', (lambda f_, kch, nel, src, m: lambda e: e.dma_start(out=f_[:, 0:nel], in_=k.dram[src][kch][:, m * nel:(m + 1) * nel]))(f_, kch, nel, src, m), writes=[fk], dma=True)
                                        fv = f_[:, 0:nel].rearrange("p (t q) -> p t q", t=ntch)
                                        col = 256 * m
                                        for tch in range(ntch):
                                            tix = t0x + tch
                                            for (pb, rhs, rk) in ((pu, ptm, 'ptm'), (pg, hsum if m == 0 else hdif, 'hsum' if m == 0 else 'hdif')):
                                                P.op('tensor', (lambda pb, col, rhs, tch, tix, fv: lambda e: e.matmul(
                                                    k.ps[pb][:, col:col + 256], lhsT=fv[:, tch, :], rhs=rhs[:, tix, :],
                                                    start=(tch == 0), stop=(tch == ntch - 1)))(pb, col, rhs, tch, tix, fv),
                                                    reads=[fk, (rk, tix)], writes=[('ps', pb)])
                                    kix = t0x + kch
                                    P.op('scalar', (lambda pg: lambda e: e.activation(out=Gs[:], in_=k.ps[pg][:], func=AF.Copy))(pg), reads=[('ps', pg)], writes=['Gs'])
                                    P.op('vector', (lambda pu: lambda e: e.tensor_tensor(out=tq[0][:], in0=k.ps[pu][:, 0:256], in1=Gs[:, 0:256], op=ALU.mult))(pu), reads=[('ps', pu), 'Gs'], writes=['tq0'])
                                    P.op('vector', (lambda pu: lambda e: e.tensor_tensor(out=tq[1][:], in0=k.ps[pu][:, 256:512], in1=Gs[:, 256:512], op=ALU.mult))(pu), reads=[('ps', pu), 'Gs'], writes=['tq1'])
                                    P.op('vector', (lambda kix: lambda e: e.tensor_tensor(out=Y[:, kix, 0:256], in0=tq[0][:], in1=tq[1][:], op=ALU.subtract))(kix), reads=['tq0', 'tq1'], writes=[('Y', kix)])
                                    P.op('vector', (lambda pu: lambda e: e.tensor_tensor(out=tq[0][:], in0=k.ps[pu][:, 0:256], in1=Gs[:, 256:512], op=ALU.mult))(pu), reads=[('ps', pu), 'Gs'], writes=['tq0'])
                                    P.op('vector', (lambda pu: lambda e: e.tensor_tensor(out=tq[1][:], in0=k.ps[pu][:, 256:512], in1=Gs[:, 0:256], op=ALU.mult))(pu), reads=[('ps', pu), 'Gs'], writes=['tq1'])
                                    P.op('vector', (lambda kix: lambda e: e.tensor_tensor(out=Y[:, kix, 256:512], in0=tq[0][:], in1=tq[1][:], op=ALU.add))(kix), reads=['tq0', 'tq1'], writes=[('Yi', kix)])
                            bg_drain(k, next_ps(k))
                        P.barrier()
                P.barrier()
                if HS == 'B':
                    continue
                with ExitStack() as sC:
                    isl = [k.sb('hy_is%d' % i, [128, 2048], BF16, sC) for i in range(3)]
                    ub0 = k.sb('hy_ubc', [128, UW], F32, sC)
                    x0c = k.sb('hy_x0c', [128, 2, TT], BF16, sC)
                    zb = k.sb('hy_z', [128, 2, TT], BF16, sC)
                    tA = [(k.sb('hy_tc%d' % i, [128, 512], F32, sC), ('hy_ta', i)) for i in range(2)]
                    zero_pads(ub0, 'ub0')
                    wx0 = load_w(k, w_in[:, cg * 256:(cg + 1) * 256], NCH, 256)
                    for c2 in range(2):
                        def emit_x0(b, t_, tk, c2=c2):
                            P.op('scalar', (lambda b, t_, c2: lambda e: e.activation(out=x0c[:, c2, b.sl], in_=t_[:, 0:b.n], func=AF.Copy))(b, t_, c2),
                                 reads=[tk], writes=[('x0c', c2, b.key)])
                        proj_conv(0, cg * 2 + c2, wx0, c2, ub0, 'ub0', emit_x0, tA)
                    isi = 0
                    for b in allb:
                        lat = (b.s == 0)
                        nkg = 4 if lat else 1
                        kpg = 4 if lat else 2
                        scale = 2.0 / (2 * SEQ) if lat else 2.0 / (2 * CTX)
                        acc = next_ps(k, 2)
                        for kg in range(nkg):
                            for m in range(2):
                                i_ = isl[isi % 3]
                                ik = ('is', isi % 3)
                                isi += 1
                                if lat:
                                    src = k.dram['dft_i'][(b.t0 // 512) * 4 + kg][:, m * 2048:(m + 1) * 2048]
                                    nel = 2048
                                else:
                                    src = k.dram['dft_ic'][:, m * 512:(m + 1) * 512]
                                    nel = 512
                                P.op('sync', (lambda i_, nel, src: lambda e: e.dma_start(out=i_[:, 0:nel], in_=src))(i_, nel, src), writes=[ik], dma=True)
                                iv = i_[:, 0:nel].rearrange("p (q t) -> p q t", q=kpg)
                                for kq in range(kpg):
                                    kix = (kg * 4 + kq) if lat else (16 + kq)
                                    for c2 in range(2):
                                        st_ = (kg == 0 and kq == 0 and m == 0)
                                        sp_ = (kg == nkg - 1 and kq == kpg - 1 and m == 1)
                                        pacc = acc[c2]
                                        P.op('tensor', (lambda c2, m, kix, kq, iv, b, st_, sp_, pacc: lambda e: e.matmul(
                                            k.ps[pacc][:, 0:b.n], lhsT=Y[:, kix, m * 256 + c2 * 128:m * 256 + (c2 + 1) * 128], rhs=iv[:, kq, :],
                                            start=st_, stop=sp_))(c2, m, kix, kq, iv, b, st_, sp_, pacc),
                                            reads=[ik, ('Y', kix), ('Yi', kix)], writes=[('ps', acc[c2])])
                        for c2 in range(2):
                            P.op('vector', (lambda c2, b, scale, pacc: lambda e: e.scalar_tensor_tensor(out=zb[:, c2, b.sl], in0=k.ps[pacc][:, 0:b.n], scalar=scale, in1=x0c[:, c2, b.sl], op0=ALU.mult, op1=ALU.mult))(c2, b, scale, acc[c2]),
                                 reads=[('ps', acc[c2]), ('x0c', c2, b.key)], writes=[('z', c2, b.key)])
                    for grp in ([LAT_BLKS[0], LAT_BLKS[1]], [LAT_BLKS[2], LAT_BLKS[3]], [CTX_BLK]):
                        out_proj_residual(k, li, sub, grp, k.dram['hy_w_o'][cg * 256:(cg + 1) * 256, :], 2,
                                          lambda kc, b: zb[:, kc, b.sl], lambda kc, b: [('z', kc, b.key)], None, do_ln=False)
                P.barrier()
    P.barrier()
    with ExitStack() as st:
        lnt = alloc_ln_tmps(k, st)
        for b in allb:
            if b.s == 0 and b.t0 < 1024:
                layer_norm_blk(k, b, lnt)
            else:
                k.ln_pending.append(b)


MIXERS = {0: diff_attn, 1: hyena, 2: retention}


def pvec(v):
    v = np.asarray(v, np.float32)
    lead = v.shape[:-1]
    v = v.reshape(lead + (v.shape[-1] // 128, 128))
    v = np.moveaxis(v, -1, 0)
    return np.ascontiguousarray(v.reshape(128, -1))


def axial_angles(n_tokens, dim):
    rows = n_tokens // 64
    row = np.repeat(np.arange(rows), 64).astype(np.float32)
    col = np.tile(np.arange(64), rows).astype(np.float32)
    n_freq = dim // 4
    inv = (np.float32(10000.0) ** (-np.arange(n_freq, dtype=np.float32) / np.float32(n_freq))).astype(np.float32)
    return np.concatenate([row[:, None] * inv, col[:, None] * inv], axis=-1).astype(np.float32)


def rope_table_da():
    ang = axial_angles(SEQ, 64)
    p = np.arange(128)
    d = p % 64
    fi = d % 32
    sign = np.where(d < 32, -1.0, 1.0).astype(np.float32)
    cosT = np.cos(ang)[:, fi].T
    sinS = (np.sin(ang)[:, fi] * sign[None, :]).T
    return np.ascontiguousarray(np.concatenate([cosT, sinS], axis=1).astype(ml_dtypes.bfloat16))


def rt_tables():
    ang = axial_angles(SEQ, 256)
    rope = np.concatenate([np.cos(ang).T, np.sin(ang).T], axis=1).astype(ml_dtypes.bfloat16)
    p = np.arange(128, dtype=np.float32)[:, None]
    t = np.arange(512, dtype=np.float32)[None, :]
    io_f = np.broadcast_to(t, (128, 512))
    io_b = np.broadcast_to(511.0 - t, (128, 512))
    offs = 128.0 * np.arange(4, dtype=np.float32)[None, :] + p
    E_f = 128.0 * np.arange(16, dtype=np.float32)[None, :] - p
    E_b = 128.0 * np.arange(18, dtype=np.float32)[None, :] - 511.0 + p
    cst = np.concatenate([io_f, io_b, offs, E_f, E_b], axis=1).astype(np.float32)
    return np.ascontiguousarray(rope), np.ascontiguousarray(cst)


_HY_CACHE = {}


def hy_tables():
    if _HY_CACHE:
        return _HY_CACHE
    f32 = np.float32
    zs = []
    tns = []
    for n in (SEQ, CTX):
        t = np.linspace(0.0, 1.0, n, dtype=f32)[:, None]
        fr = np.linspace(1e-4, 15.0, 16, dtype=f32)[None, :]
        w = (f32(2.0 * math.pi) * np.arange(n, dtype=f32)[:, None] / f32(n)).astype(f32)
        z = np.concatenate([t, np.cos(fr * w), -np.sin(fr * w)], axis=-1).astype(f32)
        zs.append(z.T)
        tns.append((-t[:, 0]).reshape(n // 128, 128).T)
    _HY_CACHE['hy_zT'] = np.ascontiguousarray(np.concatenate(zs, axis=1))
    _HY_CACHE['hy_tn'] = np.ascontiguousarray(np.concatenate(tns, axis=1).astype(f32))
    max_decay = math.log(1e-2) / 0.3
    min_decay = math.log(1e-2) / 1.5
    deltas = np.abs(np.linspace(min_decay, max_decay, D, dtype=f32))
    _HY_CACHE['hy_delta'] = np.ascontiguousarray(np.broadcast_to(deltas[None, :], (128, D)).astype(f32))
    _HY_CACHE['ident'] = np.eye(128, dtype=f32).astype(ml_dtypes.bfloat16)
    bf = ml_dtypes.bfloat16

    def cs(n):
        N = 2 * n
        t = np.arange(n, dtype=np.float64)[:, None]
        kk = np.arange(n, dtype=np.float64)[None, :] + 0.5
        ang = 2.0 * np.pi * t * kk / N
        return np.cos(ang), np.sin(ang)

    C, S = cs(SEQ)
    CS = np.stack([C, S], 0)
    a = CS.reshape(2, 16, 128, 16, 128)
    _HY_CACHE['dft_f'] = np.ascontiguousarray(a.transpose(3, 2, 0, 1, 4).reshape(16, 128, 4096).astype(bf))
    a = CS.reshape(2, 4, 512, 4, 4, 128)
    _HY_CACHE['dft_i'] = np.ascontiguousarray(a.transpose(1, 3, 5, 0, 4, 2).reshape(16, 128, 4096).astype(bf))
    C, S = cs(CTX)
    CS = np.stack([C, S], 0)
    a = CS.reshape(2, 2, 128, 2, 128)
    _HY_CACHE['dft_fc'] = np.ascontiguousarray(a.transpose(3, 2, 0, 1, 4).reshape(2, 128, 512).astype(bf))
    a = CS.reshape(2, 256, 2, 128)
    _HY_CACHE['dft_ic'] = np.ascontiguousarray(a.transpose(3, 0, 2, 1).reshape(128, 1024).astype(bf))
    return _HY_CACHE


def make_in_maps(inputs, n_cores=8):
    f = lambda a: np.ascontiguousarray(np.asarray(a, np.float32))
    shared = {
        'ada_w': f(inputs['ada_w']),
        'ada_b': pvec(np.asarray(inputs['ada_b']).reshape(DEPTH, 9, D).reshape(DEPTH, 9 * D).reshape(DEPTH, 72, 128).reshape(DEPTH, 72 * 128)) if False else None,
    }
    ada_b = np.asarray(inputs['ada_b'], np.float32).reshape(DEPTH, 72, 128)
    shared['ada_b'] = np.ascontiguousarray(ada_b.transpose(2, 0, 1).reshape(128, DEPTH * 72))
    shared['ln_g'] = pvec(inputs['ln_g'])
    shared['ln_b'] = pvec(inputs['ln_b'])
    for nm in ('ffa_wi', 'ffa_wo', 'ffb_wi', 'ffb_wo'):
        shared[nm] = f(inputs[nm])
    wqkv = np.asarray(inputs['da_w_qkv'][0], np.float32)
    shared['da_w_qkv'] = f(wqkv)
    swp = np.arange(D).reshape(D // 64, 2, 32)[:, ::-1, :].reshape(D)
    shared['da_wq_sw'] = f(wqkv[:, 0:D][:, swp])
    shared['da_wk_sw'] = f(wqkv[:, D:2 * D][:, swp])
    shared['da_w_o'] = f(inputs['da_w_o'][0])
    shared['da_lamT'] = f(np.asarray(inputs['da_lambda'][0], np.float32).T)
    shared['da_subg'] = f(np.asarray(inputs['da_subln_g'][0], np.float32).reshape(128, 1))
    shared['rope_da'] = rope_table_da()
    shared['rt_w_in'] = f(inputs['rt_w_in'][0])
    shared['rt_w_o'] = f(inputs['rt_w_o'][0])
    shared['rt_decay'] = np.ascontiguousarray(np.tile(np.asarray(inputs['rt_decay_logit'][0], np.float32).reshape(1, 8), (128, 1)))
    shared['rt_gng'] = pvec(inputs['rt_gn_g'][0])
    shared['rope_rt'], shared['rt_const'] = rt_tables()
    shared['hy_w_in'] = f(inputs['hy_w_in'][0])
    shared['hy_w_o'] = f(inputs['hy_w_o'][0])
    shared['hy_cw'] = pvec(inputs['hy_conv_w'][0])
    shared['hy_cb'] = pvec(inputs['hy_conv_b'][0])
    shared['hy_fw1'] = f(inputs['hy_fw1'][0])
    shared['hy_fw2'] = f(inputs['hy_fw2'][0])
    shared['hy_vec64'] = f(np.stack([np.asarray(inputs[n_][0], np.float32) for n_ in ('hy_fb1', 'hy_ff1', 'hy_fb2', 'hy_ff2')], 1))
    shared['hy_fw3'] = f(inputs['hy_fw3'][0])
    shared['hy_dskip'] = f(np.asarray(inputs['hy_d_skip'][0], np.float32).reshape(1, D))
    shared.update(hy_tables())
    shared['sc_w_in'] = f(inputs['sc_w_in'][0])
    shared['sc_conv_w'] = pvec(inputs['sc_conv_w'][0])
    shared['sc_w_o'] = f(inputs['sc_w_o'][0])
    maps = []
    for b in range(n_cores):
        m = dict(shared)
        m['xT'] = np.ascontiguousarray(np.asarray(inputs['x'][b], np.float32).T)
        m['ctxT'] = np.ascontiguousarray(np.asarray(inputs['ctx'][b], np.float32).T)
        cv = np.stack([np.asarray(inputs['c'][b], np.float32), np.asarray(inputs['c_ctx'], np.float32)], 0)
        m['cvec'] = pvec(cv)
        maps.append(m)
    return maps


def kernel(**inputs):
    nc, k = build_program()
    maps = make_in_maps(inputs, 8)
    res = run_bass_kernel_spmd(nc, maps, core_ids=list(range(8)))
    out = np.stack([np.ascontiguousarray(r['outT'].T) for r in res.results], 0)
    return out.astype(np.float32)
```

```python
import math
import os
from contextlib import ExitStack

import numpy as np
import ml_dtypes
import concourse.bass as bass
import concourse.mybir as mybir
from concourse.bass_utils import run_bass_kernel_spmd

F32 = mybir.dt.float32
BF16 = mybir.dt.bfloat16
AF = mybir.ActivationFunctionType
ALU = mybir.AluOpType

D = 1024
NCH = 8
SEQ = 2048
CTX = 256
TT = SEQ + CTX
DEPTH = 4
DFF = 2816
NF = 22
ALPHA = (2.0 * DEPTH) ** 0.25
EPS = 1e-5
ENGS = ['tensor', 'vector', 'scalar', 'gpsimd', 'sync']


class Op:
    __slots__ = ('eng', 'fn', 'deps', 'signals', 'dma', 'semkey', 'semval')

    def __init__(self, eng, fn, dma, semkey):
        self.eng = eng
        self.fn = fn
        self.deps = []
        self.signals = False
        self.dma = dma
        self.semkey = semkey
        self.semval = None


class Prog:
    def __init__(self, nc):
        self.nc = nc
        self.ops = {e: [] for e in ENGS}
        self.last_w = {}
        self.readers = {}
        self.nops = 0
        self.last_op = {}
        self.scoped_dma = []

    def op(self, eng, fn, reads=(), writes=(), dma=False, semkey=None, scoped=True):
        if dma and semkey is None:
            semkey = writes[0]
        o = Op(eng, fn, dma, semkey)
        deps = set()
        for k in reads:
            w = self.last_w.get(k)
            if w is not None:
                deps.add(w)
        for k in writes:
            w = self.last_w.get(k)
            if w is not None:
                deps.add(w)
            for r in self.readers.get(k, ()):
                deps.add(r)
        for d in deps:
            if (not d.dma) and d.eng == 'tensor' and eng == 'tensor' and not dma:
                continue
            d.signals = True
            o.deps.append(d)
        for k in writes:
            self.last_w[k] = o
            self.readers[k] = []
        for k in reads:
            self.readers.setdefault(k, []).append(o)
        self.ops[eng].append(o)
        self.nops += 1
        if dma:
            if scoped:
                self.scoped_dma.append(o)
        else:
            self.last_op[eng] = o
        return o

    def barrier(self, engs=('tensor', 'vector', 'scalar', 'sync')):
        targets = [o for o in self.last_op.values()] + list(self.scoped_dma)
        self.scoped_dma = []
        for e in engs:
            o = Op(e, None, False, None)
            for d in targets:
                d.signals = True
                o.deps.append(d)
            self.ops[e].append(o)

    def emit(self, final_waits=()):
        nc = self.nc
        eng_sem = {}
        dma_sem = {}
        dma_cnt = {}
        with ExitStack() as es:
            for e in ENGS:
                cnt = 0
                for o in self.ops[e]:
                    if o.dma:
                        if o.semkey not in dma_sem:
                            dma_sem[o.semkey] = es.enter_context(nc.semaphore('d%d' % len(dma_sem)))
                            dma_cnt[o.semkey] = 0
                        dma_cnt[o.semkey] += 16
                        o.semval = dma_cnt[o.semkey]
                    elif o.signals:
                        cnt += 1
                        o.semval = cnt
                eng_sem[e] = es.enter_context(nc.semaphore('e_' + e))
            self.n_dma_sems = len(dma_sem)
            block = es.enter_context(nc.Block())
            fw = {}
            for (e, o) in final_waits:
                fw.setdefault(e, []).append((dma_sem[o.semkey], o.semval))

            def run(e, engobj):
                waited = {}
                for o in self.ops[e]:
                    need = {}
                    for d in o.deps:
                        if d.dma:
                            key = ('d', d.semkey)
                            sem = dma_sem[d.semkey]
                        else:
                            key = ('e', d.eng)
                            sem = eng_sem[d.eng]
                        if waited.get(key, 0) >= d.semval:
                            continue
                        if key not in need or need[key][1] < d.semval:
                            need[key] = (sem, d.semval)
                    for key, (sem, val) in need.items():
                        engobj.wait_ge(sem, val)
                        waited[key] = val
                    if o.fn is None:
                        continue
                    ins = o.fn(engobj)
                    if o.dma:
                        ins.then_inc(dma_sem[o.semkey], 16)
                    elif o.signals:
                        ins.then_inc(eng_sem[e], 1)
                for (sem, val) in fw.get(e, ()):
                    engobj.wait_ge(sem, val)

            block.tensor(lambda eng: run('tensor', eng))
            block.vector(lambda eng: run('vector', eng))
            block.scalar(lambda eng: run('scalar', eng))
            block.gpsimd(lambda eng: run('gpsimd', eng))
            block.sync(lambda eng: run('sync', eng))


class Blk:
    def __init__(self, stream, t0, n):
        self.s = stream
        self.t0 = t0
        self.n = n
        self.g0 = t0 if stream == 0 else SEQ + t0
        self.key = (stream, t0)

    @property
    def sl(self):
        return slice(self.g0, self.g0 + self.n)


LAT_BLKS = [Blk(0, i * 512, 512) for i in range(4)]
CTX_BLK = Blk(1, 0, 256)


class K:
    pass


def build_program(layers=(0, 1, 2, 3), first_affine_identity=True, dbg=None):
    nc = bass.Bass("TRN2", target_bir_lowering=False)
    k = K()
    k.nc = nc
    k.P = Prog(nc)
    k.es = ExitStack()
    k.dram = {}
    k.layers = layers

    def din(name, shape, dt=F32):
        k.dram[name] = nc.dram_tensor(name, list(shape), dt, kind="ExternalInput").ap()
        return k.dram[name]

    din('xT', [D, SEQ]); din('ctxT', [D, CTX]); din('cvec', [128, 2 * NCH])
    din('ada_w', [DEPTH, D, 9 * D]); din('ada_b', [128, DEPTH * 72])
    din('ln_g', [128, DEPTH * 3 * NCH]); din('ln_b', [128, DEPTH * 3 * NCH])
    din('ffa_wi', [DEPTH, D, 2 * DFF]); din('ffa_wo', [DEPTH, DFF, D])
    din('ffb_wi', [DEPTH, D, 2 * DFF]); din('ffb_wo', [DEPTH, DFF, D])
    din('da_w_qkv', [D, 3 * D]); din('da_wq_sw', [D, D]); din('da_wk_sw', [D, D]); din('da_w_o', [D, D])
    din('da_lamT', [64, 4]); din('da_subg', [128, 1]); din('rope_da', [128, 2 * SEQ], BF16)
    din('rt_w_in', [D, 6 * D]); din('rt_w_o', [2 * D, D]); din('rt_decay', [128, 8]); din('rt_gng', [128, 16])
    din('rope_rt', [128, 2 * SEQ], BF16); din('rt_const', [128, 1024 + 4 + 16 + 18])
    din('hy_w_in', [D, 3 * D]); din('hy_w_o', [D, D]); din('hy_cw', [128, 72]); din('hy_cb', [128, 24])
    din('hy_fw1', [33, 64]); din('hy_fw2', [64, 64]); din('hy_vec64', [64, 4]); din('hy_fw3', [64, 2 * D])
    din('hy_dskip', [1, D]); din('hy_delta', [128, D]); din('hy_tn', [128, 18]); din('hy_zT', [33, TT])
    din('ident', [128, 128], BF16)
    din('dft_f', [16, 128, 4096], BF16); din('dft_i', [16, 128, 4096], BF16)
    din('dft_fc', [2, 128, 512], BF16); din('dft_ic', [128, 1024], BF16)
    din('sc_w_in', [D, 3 * D]); din('sc_conv_w', [128, 3 * NCH]); din('sc_w_o', [D, D])
    k.out = nc.dram_tensor('outT', [D, SEQ], F32, kind="ExternalOutput").ap()
    if dbg:
        k.dbgc = nc.dram_tensor('dbgc', [D, CTX], F32, kind="ExternalOutput").ap()

    with k.es:
        es = k.es
        P = k.P

        k.sbcnt = 0

        def sb(name, shape, dt=F32, stack=None):
            k.sbcnt += 1
            return (stack or es).enter_context(nc.sbuf_tensor('%s_%d' % (name, k.sbcnt), list(shape), dt))

        k.sb = sb
        k.nbuf = sb('nbuf', [128, NCH, TT], F32)
        k.NS = 6
        k.WSZ = 2048
        k.wslots = [sb('ws%d' % i, [128, k.WSZ], BF16) for i in range(k.NS)]
        k.ws_next = 0
        k.mod = sb('mod', [128, DEPTH * 2 * 72], F32)
        k.adab = sb('adab', [128, DEPTH * 72], F32)
        k.lng = sb('lng', [128, DEPTH * 3 * NCH], F32)
        k.lnb = sb('lnb', [128, DEPTH * 3 * NCH], F32)
        k.cv = sb('cv', [128, 2 * NCH], F32)
        k.scb = sb('scb', [128, NCH, 2], BF16)
        k.sg = sb('sg', [128, 2 * NCH], F32)
        k.bg = []
        k.bg_loaded = None
        k.ln_pending = []
        k.coef = sb('coef', [128, DEPTH * 3 * 2 * 5 * NCH], F32)
        k.ones = sb('ones', [128, 128], BF16)
        k.one1 = sb('one1', [128, 128], BF16)
        k.epsc = sb('epsc', [128, 1], F32)
        k.scw = sb('scw', [128, 3 * NCH], F32)
        k.ps = [es.enter_context(nc.psum_tensor('ps%d' % i, [128, 512], F32)) for i in range(8)]
        k.ps_rr = 0

        P.op('vector', lambda e: e.memset(k.ones[:], 1.0 / 1024.0), writes=['ones'])
        P.op('vector', lambda e: e.memset(k.one1[:], 1.0), writes=['one1'])
        P.op('vector', lambda e: e.memset(k.epsc[:], EPS), writes=['epsc'])

        def small_load(dst, src, key):
            P.op('sync', lambda e: e.dma_start(out=dst[:], in_=src), writes=[key], dma=True)

        small_load(k.adab, k.dram['ada_b'], 'adab')
        small_load(k.lng, k.dram['ln_g'], 'lng')
        small_load(k.lnb, k.dram['ln_b'], 'lnb')
        small_load(k.cv, k.dram['cvec'], 'cv')
        small_load(k.scw, k.dram['sc_conv_w'], 'scw')
        for c in range(NCH):
            P.op('sync', (lambda c: lambda e: e.dma_start(out=k.nbuf[:, c, 0:SEQ], in_=k.dram['xT'][c * 128:(c + 1) * 128, :]))(c),
                 writes=[('n', c, b.key) for b in LAT_BLKS], dma=True, semkey=('nload', c))
        P.op('sync', lambda e: e.dma_start(out=k.nbuf[:, :, SEQ:TT], in_=k.dram['ctxT'].rearrange("(c p) t -> p c t", p=128)),
             writes=[('n', c, CTX_BLK.key) for c in range(NCH)], dma=True, semkey=('nload', 'c'))

        compute_mods(k)
        for lidx, li in enumerate(layers):
            use_ctx = li <= 2
            ctx_next = li < 2
            compute_coefs(k, li, first_affine_identity and li == layers[0])
            ffn(k, li, 0, use_ctx)
            P.barrier()
            if lidx >= 1 and lidx + 1 < len(layers) and li in (1, 2):
                k.bg = [(layers[lidx + 1], p) for p in range(36)]
            if li == 3:
                shortconv(k, li)
            else:
                MIXERS[li](k, li, use_ctx, ctx_next)
            bg_flush(k)
            P.barrier()
            ffn(k, li, 2, ctx_next, defer_tail=(lidx + 1 < len(layers)))
            P.barrier()
        final_out(k, layers[-1], dbg)
        P.emit(final_waits=k.final_waits)
    k.nops = P.nops
    return nc, k


def next_ps(k, n=1):
    r = []
    for _ in range(n):
        r.append(k.ps_rr)
        k.ps_rr = (k.ps_rr + 1) % 8
    return r if n > 1 else r[0]


def load_w(k, src_ap, nk, ncols):
    assert nk * ncols <= k.WSZ
    s = k.ws_next
    k.ws_next = (k.ws_next + 1) % k.NS
    dst = k.wslots[s][:, 0:nk * ncols].rearrange("p (a b) -> p a b", b=ncols)
    k.P.op('gpsimd', lambda e: e.dma_start(out=dst, in_=src_ap.rearrange("(a p) n -> p a n", p=128)),
           writes=[('ws', s)], dma=True, scoped=False)
    return s, dst


def coef_ap(k, li, sub, stream, which, c):
    idx = ((((li * 3 + sub) * 2 + stream) * 5 + which) * NCH) + c
    return k.coef[:, idx:idx + 1]


def mod_ap(k, li, stream, j, c=None):
    base = (li * 2 + stream) * 72 + j * NCH
    if c is None:
        return k.mod[:, base:base + NCH]
    return k.mod[:, base + c:base + c + 1]


def ada_load(k, li, piece):
    return load_w(k, k.dram['ada_w'][li, :, piece * 256:(piece + 1) * 256], NCH, 256)


def ada_compute(k, li, piece, pb, handle):
    P = k.P
    s, wv = handle
    for q in range(2):
        for kc in range(NCH):
            P.op('tensor', (lambda wv, q, kc, pb: lambda e: e.matmul(
                k.ps[pb][:, 2 * q:2 * q + 2], lhsT=wv[:, kc, q * 128:(q + 1) * 128], rhs=k.scb[:, kc, :],
                start=(kc == 0), stop=(kc == NCH - 1)))(wv, q, kc, pb),
                reads=[('ws', s), 'scb'], writes=[('ps', pb)])
    j0 = 2 * piece
    for s_ in range(2):
        base = (li * 2 + s_) * 72 + j0
        P.op('vector', (lambda pb, s_, base, li, j0: lambda e: e.tensor_tensor(
            out=k.mod[:, base:base + 2], in0=k.ps[pb][:, s_:4:2], in1=k.adab[:, li * 72 + j0:li * 72 + j0 + 2], op=ALU.add))(pb, s_, base, li, j0),
            reads=[('ps', pb), 'adab'], writes=['mod'])


def bg_drain(k, pb):
    if k.bg_loaded is not None:
        li, piece, handle = k.bg_loaded
        k.bg_loaded = None
        k.ln_pending = []
        ada_compute(k, li, piece, pb, handle)


def bg_step(k, pb):
    bg_drain(k, pb)
    if k.bg:
        li, piece = k.bg.pop(0)
        k.bg_loaded = (li, piece, ada_load(k, li, piece))


def bg_flush(k):
    while k.bg or k.bg_loaded is not None:
        bg_step(k, next_ps(k))


def compute_mods(k):
    P = k.P
    P.op('scalar', lambda e: e.activation(out=k.sg[:], in_=k.cv[:], func=AF.Silu), reads=['cv'], writes=['sg'])
    P.op('vector', lambda e: e.tensor_copy(out=k.scb[:, :, 0], in_=k.sg[:, 0:NCH]), reads=['sg'], writes=['scb'])
    P.op('vector', lambda e: e.tensor_copy(out=k.scb[:, :, 1], in_=k.sg[:, NCH:2 * NCH]), reads=['sg'], writes=['scb'])
    k.bg = [(li, p) for li in k.layers[0:2] for p in range(36)]
    bg_flush(k)
    P.barrier()


def compute_coefs(k, li, identity_first):
    P = k.P
    for sub in range(3):
        for s_ in range(2):
            shift = mod_ap(k, li, s_, 3 * sub + 0)
            scale = mod_ap(k, li, s_, 3 * sub + 1)
            gate = mod_ap(k, li, s_, 3 * sub + 2)
            i0 = (((li * 3 + sub) * 2 + s_) * 5) * NCH
            Ah = k.coef[:, i0:i0 + NCH]
            Bh = k.coef[:, i0 + NCH:i0 + 2 * NCH]
            Ar = k.coef[:, i0 + 2 * NCH:i0 + 3 * NCH]
            Br = k.coef[:, i0 + 3 * NCH:i0 + 4 * NCH]
            G = k.coef[:, i0 + 4 * NCH:i0 + 5 * NCH]
            ident = identity_first and sub == 0
            if not ident:
                pl, psub = (li - 1, 2) if sub == 0 else (li, sub - 1)
                gp = k.lng[:, (pl * 3 + psub) * NCH:(pl * 3 + psub + 1) * NCH]
                bp = k.lnb[:, (pl * 3 + psub) * NCH:(pl * 3 + psub + 1) * NCH]
            rk = ['mod', 'lng', 'lnb', 'coef']
            if ident:
                P.op('vector', (lambda Ah, scale: lambda e: e.tensor_scalar(out=Ah, in0=scale, scalar1=1.0, scalar2=None, op0=ALU.add))(Ah, scale), reads=rk, writes=['coef'])
                P.op('vector', (lambda Bh, shift: lambda e: e.tensor_copy(out=Bh, in_=shift))(Bh, shift), reads=rk, writes=['coef'])
                P.op('vector', (lambda Ar: lambda e: e.memset(Ar, ALPHA))(Ar), reads=rk, writes=['coef'])
                P.op('vector', (lambda Br: lambda e: e.memset(Br, 0.0))(Br), reads=rk, writes=['coef'])
            else:
                P.op('vector', (lambda Ah, scale, gp: lambda e: e.scalar_tensor_tensor(out=Ah, in0=scale, scalar=1.0, in1=gp, op0=ALU.add, op1=ALU.mult))(Ah, scale, gp), reads=rk, writes=['coef'])
                P.op('vector', (lambda Bh, scale, bp: lambda e: e.scalar_tensor_tensor(out=Bh, in0=scale, scalar=1.0, in1=bp, op0=ALU.add, op1=ALU.mult))(Bh, scale, bp), reads=rk, writes=['coef'])
                P.op('vector', (lambda Bh, shift: lambda e: e.tensor_tensor(out=Bh, in0=Bh, in1=shift, op=ALU.add))(Bh, shift), reads=rk, writes=['coef'])
                P.op('vector', (lambda Ar, gp: lambda e: e.tensor_scalar(out=Ar, in0=gp, scalar1=ALPHA, scalar2=None, op0=ALU.mult))(Ar, gp), reads=rk, writes=['coef'])
                P.op('vector', (lambda Br, bp: lambda e: e.tensor_scalar(out=Br, in0=bp, scalar1=ALPHA, scalar2=None, op0=ALU.mult))(Br, bp), reads=rk, writes=['coef'])
            gsc = 1.0 if sub == 1 else 0.5
            P.op('vector', (lambda G, gate, gsc: lambda e: e.tensor_scalar(out=G, in0=gate, scalar1=gsc, scalar2=None, op0=ALU.mult))(G, gate, gsc), reads=rk, writes=['coef'])


def make_h(k, li, sub, blks, hbuf, hcol0, eng='scalar', hkey=None):
    P = k.P
    for b in blks:
        for c in range(NCH):
            wkey = ('h', c, b.key) if hkey is None else ('h', c, hkey(b))
            Ah = coef_ap(k, li, sub, b.s, 0, c)
            Bh = coef_ap(k, li, sub, b.s, 1, c)
            dst = hbuf[:, c, b.g0 - hcol0:b.g0 - hcol0 + b.n]
            src = k.nbuf[:, c, b.sl]
            P.op('scalar', (lambda dst, src, Ah, Bh: lambda e: e.activation(out=dst, in_=src, func=AF.Identity, bias=Bh, scale=Ah))(dst, src, Ah, Bh),
                 reads=[('n', c, b.key), 'coef'], writes=[wkey])


def make_xa(k, li, sub, blks, eng='gpsimd'):
    P = k.P
    for b in blks:
        for c in range(NCH):
            Ar = coef_ap(k, li, sub, b.s, 2, c)
            Br = coef_ap(k, li, sub, b.s, 3, c)
            v = k.nbuf[:, c, b.sl]
            P.op('scalar', (lambda v, Ar, Br: lambda e: e.activation(out=v, in_=v, func=AF.Identity, bias=Br, scale=Ar))(v, Ar, Br),
                 reads=[('n', c, b.key), 'coef'], writes=[('n', c, b.key)])


def layer_norm_steps(k, b, lnt):
    P = k.P
    n = b.n
    rb, sq, mean, rstd, nmr, tmp = lnt
    steps = []

    def s_cs(c):
        src = k.nbuf[:, c, b.sl]
        P.op('scalar', lambda e: e.activation(out=rb[:, c, 0:n], in_=src, func=AF.Copy), reads=[('n', c, b.key)], writes=[('rb', c)])
        P.op('scalar', lambda e: e.activation(out=sq[:, c, 0:n], in_=src, func=AF.Square), reads=[('n', c, b.key)], writes=[('sq', c)])
    for c in range(NCH):
        steps.append((lambda c: lambda: s_cs(c))(c))
    pp = {}

    def s_mm(which):
        pb = next_ps(k)
        pp[which] = pb
        src_, key_ = (rb, 'rb') if which == 0 else (sq, 'sq')
        for c in range(NCH):
            P.op('tensor', (lambda c: lambda e: e.matmul(k.ps[pb][:, 0:n], lhsT=k.ones[:], rhs=src_[:, c, 0:n], start=(c == 0), stop=(c == NCH - 1)))(c),
                 reads=[(key_, c), 'ones'], writes=[('ps', pb)])
    steps.append(lambda: s_mm(0))
    steps.append(lambda: s_mm(1))

    def s_small():
        p1, p2 = pp[0], pp[1]
        P.op('scalar', lambda e: e.activation(out=mean[:, 0:n], in_=k.ps[p1][:, 0:n], func=AF.Copy), reads=[('ps', p1)], writes=['ln_mean'])
        P.op('vector', lambda e: e.tensor_tensor(out=tmp[:, 0:n], in0=mean[:, 0:n], in1=mean[:, 0:n], op=ALU.mult), reads=['ln_mean'], writes=['ln_tmp'])
        P.op('vector', lambda e: e.tensor_tensor(out=tmp[:, 0:n], in0=k.ps[p2][:, 0:n], in1=tmp[:, 0:n], op=ALU.subtract), reads=[('ps', p2), 'ln_tmp'], writes=['ln_tmp'])
        P.op('scalar', lambda e: e.activation(out=tmp[:, 0:n], in_=tmp[:, 0:n], func=AF.Sqrt, bias=k.epsc[:, 0:1], scale=1.0), reads=['ln_tmp', 'epsc'], writes=['ln_tmp'])
        P.op('vector', lambda e: e.reciprocal(out=rstd[:, 0:n], in_=tmp[:, 0:n]), reads=['ln_tmp'], writes=['ln_rstd'])
        P.op('vector', lambda e: e.scalar_tensor_tensor(out=nmr[:, 0:n], in0=mean[:, 0:n], scalar=-1.0, in1=rstd[:, 0:n], op0=ALU.mult, op1=ALU.mult),
             reads=['ln_mean', 'ln_rstd'], writes=['ln_nmr'])
    steps.append(s_small)

    def s_norm(c):
        v = k.nbuf[:, c, b.sl]
        P.op('vector', lambda e: e.tensor_tensor(out=v, in0=v, in1=rstd[:, 0:n], op=ALU.mult), reads=[('n', c, b.key), 'ln_rstd'], writes=[('n', c, b.key)])
        P.op('vector', lambda e: e.tensor_tensor(out=v, in0=v, in1=nmr[:, 0:n], op=ALU.add), reads=[('n', c, b.key), 'ln_nmr'], writes=[('n', c, b.key)])
    for c in range(NCH):
        steps.append((lambda c: lambda: s_norm(c))(c))
    return steps


def layer_norm_blk(k, b, lnt):
    for st_ in layer_norm_steps(k, b, lnt):
        st_()


def alloc_ln_tmps(k, st):
    rb = k.sb('ln_rb', [128, NCH, 512], BF16, st)
    sq = k.sb('ln_sq', [128, NCH, 512], BF16, st)
    mean = k.sb('ln_mean', [128, 512], F32, st)
    rstd = k.sb('ln_rstd', [128, 512], F32, st)
    nmr = k.sb('ln_nmr', [128, 512], F32, st)
    tmp = k.sb('ln_tmp', [128, 512], F32, st)
    return (rb, sq, mean, rstd, nmr, tmp)


def out_proj_residual(k, li, sub, blks, w_dram, nkc, rhs_fn, rhs_keys_fn, lnt, do_ln=True):
    P = k.P
    nb = len(blks)
    dper = max(1, min(NCH, 6 // nb))
    d0 = 0
    while d0 < NCH:
        dn = min(dper, NCH - d0)
        banks = {(dc, bi): next_ps(k) for dc in range(dn) for bi in range(nb)}
        kc0 = 0
        while kc0 < nkc:
            kn = min(k.WSZ // (dn * 128), nkc - kc0)
            s, wv = load_w(k, w_dram[kc0 * 128:(kc0 + kn) * 128, d0 * 128:(d0 + dn) * 128], kn, dn * 128)
            for kk in range(kn):
                kc = kc0 + kk
                for dc in range(dn):
                    for bi, b in enumerate(blks):
                        pb = banks[(dc, bi)]
                        rhs = rhs_fn(kc, b)
                        P.op('tensor', (lambda pb, b, wv, kk, dc, kc, rhs: lambda e: e.matmul(
                            k.ps[pb][:, 0:b.n], lhsT=wv[:, kk, dc * 128:(dc + 1) * 128], rhs=rhs,
                            start=(kc == 0), stop=(kc == nkc - 1)))(pb, b, wv, kk, dc, kc, rhs),
                            reads=[('ws', s)] + rhs_keys_fn(kc, b), writes=[('ps', pb)])
            kc0 += kn
        for dc in range(dn):
            c = d0 + dc
            for bi, b in enumerate(blks):
                pb = banks[(dc, bi)]
                G = coef_ap(k, li, sub, b.s, 4, c)
                v = k.nbuf[:, c, b.sl]
                P.op('vector', (lambda pb, b, G, v: lambda e: e.scalar_tensor_tensor(
                    out=v, in0=k.ps[pb][:, 0:b.n], scalar=G, in1=v, op0=ALU.mult, op1=ALU.add))(pb, b, G, v),
                    reads=[('ps', pb), ('n', c, b.key), 'coef'], writes=[('n', c, b.key)])
        d0 += dn
    if do_ln:
        for b in blks:
            layer_norm_blk(k, b, lnt)


def ffn(k, li, sub, with_ctx, defer_tail=False):
    P = k.P
    wi = k.dram['ffa_wi' if sub == 0 else 'ffb_wi'][li]
    wo = k.dram['ffa_wo' if sub == 0 else 'ffb_wo'][li]
    groups = [[LAT_BLKS[0], LAT_BLKS[1]], [LAT_BLKS[2], LAT_BLKS[3]]]
    gw = 1024
    if with_ctx:
        groups[1] = groups[1] + [CTX_BLK]
        gw = 1280
    with ExitStack() as st:
        hb = k.sb('ffn_h', [128, NCH, gw], BF16, st)
        gb = k.sb('ffn_g', [128, NF, gw], BF16, st)
        stmp = [k.sb('ffn_s%d' % i, [128, 512], F32, st) for i in range(2)]
        lnt = alloc_ln_tmps(k, st)
        si = 0
        pending = []

        def hloc(gcol0):
            return lambda b: ('loc', (b.g0 - gcol0) // 512)
        make_h(k, li, sub, groups[0], hb, groups[0][0].g0, hkey=hloc(groups[0][0].g0))
        make_xa(k, li, sub, groups[0])
        for b in k.ln_pending:
            pending.extend(layer_norm_steps(k, b, lnt))
        k.ln_pending = []
        for gi, grp in enumerate(groups):
            gcol0 = grp[0].g0
            hk = hloc(gcol0)
            for f0 in range(0, NF, 2):
                fn_ = min(2, NF - f0)
                sa, wa = load_w(k, wi[:, f0 * 128:(f0 + fn_) * 128], NCH, fn_ * 128)
                su, wu = load_w(k, wi[:, DFF + f0 * 128:DFF + (f0 + fn_) * 128], NCH, fn_ * 128)
                for ff in range(fn_):
                    f = f0 + ff
                    for b in grp:
                        lc = b.g0 - gcol0
                        pa, pu = next_ps(k, 2)
                        for (pb, wv, s) in ((pa, wa, sa), (pu, wu, su)):
                            for kc in range(NCH):
                                P.op('tensor', (lambda pb, wv, kc, ff, lc, b: lambda e: e.matmul(
                                    k.ps[pb][:, 0:b.n], lhsT=wv[:, kc, ff * 128:(ff + 1) * 128], rhs=hb[:, kc, lc:lc + b.n],
                                    start=(kc == 0), stop=(kc == NCH - 1)))(pb, wv, kc, ff, lc, b),
                                    reads=[('ws', s), ('h', kc, hk(b))], writes=[('ps', pb)])
                        tmp = stmp[si % 2]
                        tk = ('ffn_s', si % 2)
                        si += 1
                        P.op('scalar', (lambda tmp, pa, b: lambda e: e.activation(out=tmp[:, 0:b.n], in_=k.ps[pa][:, 0:b.n], func=AF.Silu))(tmp, pa, b),
                             reads=[('ps', pa)], writes=[tk])
                        P.op('vector', (lambda tmp, pu, b, f, lc: lambda e: e.tensor_tensor(
                            out=gb[:, f, lc:lc + b.n], in0=k.ps[pu][:, 0:b.n], in1=tmp[:, 0:b.n], op=ALU.mult))(tmp, pu, b, f, lc),
                            reads=[('ps', pu), tk], writes=[('g', f, b.key)])
                        for _ in range(2):
                            if pending:
                                pending.pop(0)()
            while pending:
                pending.pop(0)()
            if gi + 1 < len(groups):
                nxt = groups[gi + 1]
                make_h(k, li, sub, nxt, hb, nxt[0].g0, hkey=hloc(nxt[0].g0))
                make_xa(k, li, sub, nxt)
            out_proj_residual(k, li, sub, grp, wo, NF,
                              lambda kc, b: gb[:, kc, b.g0 - gcol0:b.g0 - gcol0 + b.n],
                              lambda kc, b: [('g', kc, b.key)], lnt, do_ln=False)
            if gi + 1 == len(groups) and defer_tail:
                k.ln_pending = list(grp)
            else:
                for b in grp:
                    pending.extend(layer_norm_steps(k, b, lnt))
            if gi + 1 == len(groups):
                while pending:
                    pending.pop(0)()


def shortconv(k, li):
    P = k.P
    sub = 1
    blks = LAT_BLKS
    w_in = k.dram['sc_w_in']
    with ExitStack() as st0:
      zb = k.sb('sc_z', [128, NCH, SEQ], BF16, st0)
      with ExitStack() as st:
        hb = k.sb('sc_h', [128, NCH, SEQ], BF16, st)
        vb = [k.sb('sc_v%d' % i, [128, SEQ + 2], F32, st) for i in range(2)]
        tb = [k.sb('sc_t%d' % i, [128, 512], F32, st) for i in range(2)]
        make_h(k, li, sub, blks, hb, 0)
        make_xa(k, li, sub, blks)
        for i in range(2):
            P.op('vector', (lambda i: lambda e: e.memset(vb[i][:, 0:1], 0.0))(i), writes=[('sc_v', i, 'pad')])
            P.op('vector', (lambda i: lambda e: e.memset(vb[i][:, SEQ + 1:SEQ + 2], 0.0))(i), writes=[('sc_v', i, 'pad')])
        ti = 0
        for c in range(NCH):
            vi = c % 2
            v = vb[vi]
            w3s = [load_w(k, w_in[:, j * D + c * 128:j * D + (c + 1) * 128], NCH, 128) for j in range(3)]
            for b in blks:
                pbg, pcg, pu = next_ps(k, 3)
                for j, pb in enumerate((pbg, pcg, pu)):
                    s, wv = w3s[j]
                    for kc in range(NCH):
                        P.op('tensor', (lambda pb, kc, b, wv: lambda e: e.matmul(
                            k.ps[pb][:, 0:b.n], lhsT=wv[:, kc, :], rhs=hb[:, kc, b.sl],
                            start=(kc == 0), stop=(kc == NCH - 1)))(pb, kc, b, wv),
                            reads=[('ws', s), ('h', kc, b.key)], writes=[('ps', pb)])
                t = tb[ti % 2]
                tk = ('sc_t', ti % 2)
                ti += 1
                P.op('scalar', (lambda t, pcg, b: lambda e: e.activation(out=t[:, 0:b.n], in_=k.ps[pcg][:, 0:b.n], func=AF.Copy))(t, pcg, b),
                     reads=[('ps', pcg)], writes=[tk])
                P.op('vector', (lambda t, pu, b, v: lambda e: e.tensor_tensor(out=v[:, 1 + b.t0:1 + b.t0 + b.n], in0=k.ps[pu][:, 0:b.n], in1=t[:, 0:b.n], op=ALU.mult))(t, pu, b, v),
                     reads=[('ps', pu), tk], writes=[('sc_v', vi, b.key)])
                P.op('scalar', (lambda pbg, b, c: lambda e: e.activation(out=zb[:, c, b.sl], in_=k.ps[pbg][:, 0:b.n], func=AF.Copy))(pbg, b, c),
                     reads=[('ps', pbg)], writes=[('z', c, b.key)])
            for bi, b in enumerate(blks):
                t = tb[ti % 2]
                tk = ('sc_t', ti % 2)
                ti += 1
                rk = [('sc_v', vi, bb.key) for bb in blks[max(0, bi - 1):bi + 2]] + [('sc_v', vi, 'pad'), 'scw']
                w0 = k.scw[:, 0 * NCH + c:0 * NCH + c + 1]
                w1 = k.scw[:, 1 * NCH + c:1 * NCH + c + 1]
                w2 = k.scw[:, 2 * NCH + c:2 * NCH + c + 1]
                o = 1 + b.t0
                P.op('scalar', (lambda t, v, o, b, w1: lambda e: e.activation(out=t[:, 0:b.n], in_=v[:, o:o + b.n], func=AF.Copy, scale=w1))(t, v, o, b, w1),
                     reads=rk, writes=[tk])
                P.op('vector', (lambda t, v, o, b, w0: lambda e: e.scalar_tensor_tensor(out=t[:, 0:b.n], in0=v[:, o - 1:o - 1 + b.n], scalar=w0, in1=t[:, 0:b.n], op0=ALU.mult, op1=ALU.add))(t, v, o, b, w0),
                     reads=rk + [tk], writes=[tk])
                P.op('vector', (lambda t, v, o, b, w2: lambda e: e.scalar_tensor_tensor(out=t[:, 0:b.n], in0=v[:, o + 1:o + 1 + b.n], scalar=w2, in1=t[:, 0:b.n], op0=ALU.mult, op1=ALU.add))(t, v, o, b, w2),
                     reads=rk + [tk], writes=[tk])
                P.op('vector', (lambda t, b, c: lambda e: e.tensor_tensor(out=zb[:, c, b.sl], in0=zb[:, c, b.sl], in1=t[:, 0:b.n], op=ALU.mult))(t, b, c),
                     reads=[tk, ('z', c, b.key)], writes=[('z', c, b.key)])
      P.barrier()
      with ExitStack() as st:
        lnt = alloc_ln_tmps(k, st)
        for gi_, grp in enumerate(([blks[0], blks[1]], [blks[2], blks[3]])):
            out_proj_residual(k, li, sub, grp, k.dram['sc_w_o'], NCH,
                              lambda kc, b: zb[:, kc, b.sl], lambda kc, b: [('z', kc, b.key)], lnt, do_ln=(gi_ == 0))
            if gi_ == 1:
                k.ln_pending = list(grp)


def final_out(k, li, dbg):
    P = k.P
    k.final_waits = []
    with ExitStack() as st:
        ob = [k.sb('fo%d' % i, [128, SEQ], F32, st) for i in range(2)]
        for c in range(NCH):
            o = ob[c % 2]
            g = k.lng[:, (li * 3 + 2) * NCH + c:(li * 3 + 2) * NCH + c + 1]
            bb = k.lnb[:, (li * 3 + 2) * NCH + c:(li * 3 + 2) * NCH + c + 1]
            if c % 2 == 0:
                P.op('vector', (lambda o, c, g, bb: lambda e: e.tensor_scalar(out=o[:], in0=k.nbuf[:, c, 0:SEQ], scalar1=g, scalar2=bb, op0=ALU.mult, op1=ALU.add))(o, c, g, bb),
                     reads=[('n', c, b.key) for b in LAT_BLKS] + ['lng', 'lnb'], writes=[('fo', c % 2)])
            else:
                P.op('scalar', (lambda o, c, g, bb: lambda e: e.activation(out=o[:], in_=k.nbuf[:, c, 0:SEQ], func=AF.Identity, bias=bb, scale=g))(o, c, g, bb),
                     reads=[('n', c, b.key) for b in LAT_BLKS] + ['lng', 'lnb'], writes=[('fo', c % 2)])
            d = P.op('sync', (lambda o, c: lambda e: e.dma_start(out=k.out[c * 128:(c + 1) * 128, :], in_=o[:]))(o, c),
                     reads=[('fo', c % 2)], writes=[('out', c)], dma=True)
            k.final_waits.append(('sync', d))
        if dbg:
            oc = k.sb('foc', [128, NCH, CTX], F32, st)
            for c in range(NCH):
                g = k.lng[:, (li * 3 + 2) * NCH + c:(li * 3 + 2) * NCH + c + 1]
                bb = k.lnb[:, (li * 3 + 2) * NCH + c:(li * 3 + 2) * NCH + c + 1]
                P.op('vector', (lambda c, g, bb: lambda e: e.tensor_scalar(out=oc[:, c, :], in0=k.nbuf[:, c, SEQ:TT], scalar1=g, scalar2=bb, op0=ALU.mult, op1=ALU.add))(c, g, bb),
                     reads=[('n', c, CTX_BLK.key), 'lng', 'lnb'], writes=[('foc', c)])
            d = P.op('sync', lambda e: e.dma_start(out=k.dbgc.rearrange("(c p) t -> p c t", p=128), in_=oc[:]),
                     reads=[('foc', c) for c in range(NCH)], writes=['dbgc'], dma=True)
            k.final_waits.append(('sync', d))


def diff_attn(k, li, use_ctx, ctx_next):
    P = k.P
    nc = k.nc
    sub = 1
    lam_init = 0.8 - 0.6 * math.exp(-0.3 * li)
    allb = LAT_BLKS + [CTX_BLK]
    wqkv = k.dram['da_w_qkv']
    with ExitStack() as st:
        hb = k.sb('da_h', [128, NCH, TT], BF16, st)
        rope = k.sb('da_rope', [128, 2 * SEQ], BF16, st)
        qT = k.sb('da_q', [128, 2, TT], BF16, st)
        kT = k.sb('da_k', [128, 2, TT], BF16, st)
        vtm = k.sb('da_v', [128, 18, 256], BF16, st)
        ob = k.sb('da_o', [128, 2, TT], BF16, st)
        pt = [k.sb('da_p%d' % i, [128, 2, 512], BF16, st) for i in range(3)]
        zacc = k.sb('da_zacc', [128, 512], F32, st)
        zhl = k.sb('da_zhl', [128, 2, 512], BF16, st)
        onesf = k.sb('da_onesf', [128, 128], F32, st)
        tt = [k.sb('da_t%d' % i, [128, 512], F32, st) for i in range(4)]
        sqb = k.sb('da_sq', [128, 512], BF16, st)
        lam = k.sb('da_lam', [128, 8], F32, st)
        lamT = k.sb('da_lamT', [64, 4], F32, st)
        subg = k.sb('da_subg', [128, 1], F32, st)
        onef = k.sb('da_onef', [64, 128], F32, st)
        P.op('sync', lambda e: e.dma_start(out=rope[:], in_=k.dram['rope_da']), writes=['rope'], dma=True)
        P.op('sync', lambda e: e.dma_start(out=lamT[:], in_=k.dram['da_lamT']), writes=['lamT'], dma=True)
        P.op('sync', lambda e: e.dma_start(out=subg[:], in_=k.dram['da_subg']), writes=['subg'], dma=True)
        P.op('vector', lambda e: e.memset(onef[:], 1.0), writes=['onef'])
        P.op('vector', lambda e: e.memset(onesf[:], 1.0), writes=['onesf'])
        P.op('vector', lambda e: e.tensor_tensor(out=lamT[:, 0:1], in0=lamT[:, 0:1], in1=lamT[:, 1:2], op=ALU.mult), reads=['lamT'], writes=['lamT'])
        P.op('vector', lambda e: e.tensor_tensor(out=lamT[:, 1:2], in0=lamT[:, 2:3], in1=lamT[:, 3:4], op=ALU.mult), reads=['lamT'], writes=['lamT'])
        pl = next_ps(k)
        P.op('tensor', lambda e: e.matmul(k.ps[pl][:, 0:2], lhsT=onef[:], rhs=lamT[:, 0:2], start=True, stop=True), reads=['lamT', 'onef'], writes=[('ps', pl)])
        P.op('scalar', lambda e: e.activation(out=lam[:, 0:2], in_=k.ps[pl][:, 0:2], func=AF.Exp), reads=[('ps', pl)], writes=['lam'])
        P.op('vector', lambda e: e.tensor_tensor(out=lam[:, 2:3], in0=lam[:, 1:2], in1=lam[:, 0:1], op=ALU.subtract), reads=['lam'], writes=['lam'])
        P.op('vector', lambda e: e.tensor_scalar(out=lam[:, 2:3], in0=lam[:, 2:3], scalar1=-lam_init, scalar2=None, op0=ALU.add), reads=['lam'], writes=['lam'])
        P.op('vector', lambda e: e.tensor_scalar(out=lam[:, 3:4], in0=subg[:, 0:1], scalar1=1.0 - lam_init, scalar2=None, op0=ALU.mult), reads=['subg', 'lam'], writes=['lam'])
        neg_lam = lam[:, 2:3]
        gsub = lam[:, 3:4]

        make_h(k, li, sub, allb, hb, 0)
        make_xa(k, li, sub, allb)
        cosT = rope[:, 0:SEQ]
        sinS = rope[:, SEQ:2 * SEQ]
        pi = [0]
        ti = [0]
        sr = [0]
        fin = []

        def nxt_p():
            i = pi[0] % len(pt)
            pi[0] += 1
            return pt[i], ('da_p', i)

        def nxt_t():
            i = ti[0] % len(tt)
            ti[0] += 1
            return tt[i], ('da_t', i)

        for hp in range(4):
            for (dstT, c0, wsw, dkey) in ((qT, hp * 256, k.dram['da_wq_sw'], 'q'), (kT, D + hp * 256, k.dram['da_wk_sw'], 'k')):
                s1, w1 = load_w(k, wqkv[:, c0:c0 + 256], NCH, 256)
                s2, w2 = load_w(k, wsw[:, hp * 256:(hp + 1) * 256], NCH, 256)
                for j in range(2):
                    for b in allb:
                        pa = next_ps(k)
                        for kc in range(NCH):
                            P.op('tensor', (lambda pa, w1, kc, j, b: lambda e: e.matmul(
                                k.ps[pa][:, 0:b.n], lhsT=w1[:, kc, j * 128:(j + 1) * 128], rhs=hb[:, kc, b.sl],
                                start=(kc == 0), stop=(kc == NCH - 1)))(pa, w1, kc, j, b),
                                reads=[('ws', s1), ('h', kc, b.key)], writes=[('ps', pa)])
                        if b.s == 1:
                            P.op('vector', (lambda pa, dstT, j, b: lambda e: e.tensor_copy(out=dstT[:, j, b.sl], in_=k.ps[pa][:, 0:b.n]))(pa, dstT, j, b),
                                 reads=[('ps', pa)], writes=[(dkey, j, b.key)])
                            continue
                        pbk = next_ps(k)
                        for kc in range(NCH):
                            P.op('tensor', (lambda pbk, w2, kc, j, b: lambda e: e.matmul(
                                k.ps[pbk][:, 0:b.n], lhsT=w2[:, kc, j * 128:(j + 1) * 128], rhs=hb[:, kc, b.sl],
                                start=(kc == 0), stop=(kc == NCH - 1)))(pbk, w2, kc, j, b),
                                reads=[('ws', s2), ('h', kc, b.key)], writes=[('ps', pbk)])
                        t1, k1 = nxt_t()
                        t2, k2 = nxt_t()
                        P.op('vector', (lambda t1, pa, b: lambda e: e.tensor_tensor(out=t1[:, 0:b.n], in0=k.ps[pa][:, 0:b.n], in1=cosT[:, b.sl], op=ALU.mult))(t1, pa, b),
                             reads=[('ps', pa), 'rope'], writes=[k1])
                        P.op('vector', (lambda t2, pbk, b: lambda e: e.tensor_tensor(out=t2[:, 0:b.n], in0=k.ps[pbk][:, 0:b.n], in1=sinS[:, b.sl], op=ALU.mult))(t2, pbk, b),
                             reads=[('ps', pbk), 'rope'], writes=[k2])
                        P.op('vector', (lambda t1, t2, dstT, j, b: lambda e: e.tensor_tensor(out=dstT[:, j, b.sl], in0=t1[:, 0:b.n], in1=t2[:, 0:b.n], op=ALU.add))(t1, t2, dstT, j, b),
                             reads=[k1, k2], writes=[(dkey, j, b.key)])
            s3, w3 = load_w(k, wqkv[:, 2 * D + hp * 256:2 * D + (hp + 1) * 256], NCH, 256)
            for kc18 in range(18):
                pv = next_ps(k)
                bkey = allb[kc18 // 4].key if kc18 < 16 else CTX_BLK.key
                for kc in range(NCH):
                    P.op('tensor', (lambda pv, kc, kc18, w3: lambda e: e.matmul(
                        k.ps[pv][:, 0:256], lhsT=hb[:, kc, kc18 * 128:(kc18 + 1) * 128], rhs=w3[:, kc, :],
                        start=(kc == 0), stop=(kc == NCH - 1)))(pv, kc, kc18, w3),
                        reads=[('ws', s3), ('h', kc, bkey)], writes=[('ps', pv)])
                P.op('vector', (lambda pv, kc18: lambda e: e.tensor_copy(out=vtm[:, kc18, :], in_=k.ps[pv][:, 0:256]))(pv, kc18),
                     reads=[('ps', pv)], writes=[('v', kc18)])
            qblks = allb if ctx_next else LAT_BLKS
            spairs = [(4, 5), (6, 7)]
            for j in range(2):
                for b in qblks:
                    kcs = list(range(18)) if b.s == 0 else [16, 17]
                    n = b.n
                    pend = []
                    nk = len(kcs)
                    for idx in range(nk + 1):
                        if idx < nk:
                            kc18 = kcs[idx]
                            kblk = allb[kc18 // 4].key if kc18 < 16 else CTX_BLK.key
                            pair = spairs[sr[0] % 2]
                            sr[0] += 1
                            for m in range(2):
                                pr = slice(64 * m, 64 * m + 64)
                                P.op('tensor', (lambda m, pr, kc18, j, b, sbm: lambda e: e.matmul(
                                    k.ps[sbm][:, 0:b.n], lhsT=kT[pr, j, kc18 * 128:(kc18 + 1) * 128], rhs=qT[pr, j, b.sl],
                                    start=True, stop=True))(m, pr, kc18, j, b, pair[m]),
                                    reads=[('k', j, kblk), ('q', j, b.key)], writes=[('ps', pair[m])])
                            pt_, pk = nxt_p()
                            for m in range(2):
                                P.op('scalar', (lambda pt_, pm, n, m: lambda e: e.activation(out=pt_[:, m, 0:n], in_=k.ps[pm][:, 0:n], func=AF.Exp, scale=0.125))(pt_, pair[m], n, m),
                                     reads=[('ps', pair[m])], writes=[pk])
                            if idx == 0:
                                P.op('vector', (lambda pt_, n: lambda e: e.tensor_copy(out=zacc[:, 0:n], in_=pt_[:, 0, 0:n]))(pt_, n), reads=[pk], writes=['zacc'])
                            else:
                                P.op('vector', (lambda pt_, n: lambda e: e.tensor_tensor(out=zacc[:, 0:n], in0=zacc[:, 0:n], in1=pt_[:, 0, 0:n], op=ALU.add))(pt_, n), reads=[pk, 'zacc'], writes=['zacc'])
                            pend.append((pt_, pk, kc18, idx))
                            if fin and idx >= 1:
                                fin.pop(0)()
                        if idx >= 1:
                            pt_, pk, kc18, pidx = pend.pop(0)
                            first = (pidx == 0)
                            last = (pidx == nk - 1)
                            for m in range(2):
                                P.op('tensor', (lambda pt_, kc18, j, m, n, first, last: lambda e: e.matmul(
                                    k.ps[m][:, 0:n], lhsT=vtm[:, kc18, j * 128:(j + 1) * 128], rhs=pt_[:, m, 0:n], start=first, stop=last))(pt_, kc18, j, m, n, first, last),
                                    reads=[pk, ('v', kc18)], writes=[('ps', m)])
                            P.op('tensor', (lambda pt_, n, first, last: lambda e: e.matmul(
                                k.ps[2][:, 0:n], lhsT=k.one1[:], rhs=pt_[:, 1, 0:n], start=first, stop=last))(pt_, n, first, last),
                                reads=[pk, 'one1'], writes=[('ps', 2)])
                    while fin:
                        fin.pop(0)()
                    P.op('vector', (lambda n: lambda e: e.tensor_copy(out=zhl[:, 0, 0:n], in_=zacc[:, 0:n]))(n), reads=['zacc'], writes=['zhl'])
                    P.op('vector', (lambda n: lambda e: e.tensor_tensor(out=zhl[:, 1, 0:n], in0=zacc[:, 0:n], in1=zhl[:, 0, 0:n], op=ALU.subtract))(n), reads=['zacc', 'zhl'], writes=['zhl'])
                    for hl in range(2):
                        P.op('tensor', (lambda n, hl: lambda e: e.matmul(k.ps[3][:, 0:n], lhsT=k.one1[:], rhs=zhl[:, hl, 0:n], start=(hl == 0), stop=(hl == 1)))(n, hl),
                             reads=['zhl', 'one1'], writes=[('ps', 3)])
                    while fin:
                        fin.pop(0)()
                    c0, c1, c2, c3 = tt
                    P.op('vector', (lambda n: lambda e: e.tensor_copy(out=c0[:, 0:n], in_=k.ps[0][:, 0:n]))(n), reads=[('ps', 0)], writes=[('da_t', 0)])
                    P.op('vector', (lambda n: lambda e: e.tensor_copy(out=c1[:, 0:n], in_=k.ps[1][:, 0:n]))(n), reads=[('ps', 1)], writes=[('da_t', 1)])
                    P.op('scalar', (lambda n: lambda e: e.activation(out=c2[:, 0:n], in_=k.ps[2][:, 0:n], func=AF.Ln))(n), reads=[('ps', 2)], writes=[('da_t', 2)])

                    def mk(n, j, b):
                        return [
                            lambda: P.op('scalar', lambda e: e.activation(out=c3[:, 0:n], in_=k.ps[3][:, 0:n], func=AF.Ln), reads=[('ps', 3)], writes=[('da_t', 3)]),
                            lambda: P.op('scalar', lambda e: e.activation(out=c2[:, 0:n], in_=c2[:, 0:n], func=AF.Exp, scale=-1.0), reads=[('da_t', 2)], writes=[('da_t', 2)]),
                            lambda: P.op('scalar', lambda e: e.activation(out=c3[:, 0:n], in_=c3[:, 0:n], func=AF.Exp, scale=-1.0), reads=[('da_t', 3)], writes=[('da_t', 3)]),
                            lambda: P.op('vector', lambda e: e.tensor_tensor(out=c0[:, 0:n], in0=c0[:, 0:n], in1=c3[:, 0:n], op=ALU.mult), reads=[('da_t', 0), ('da_t', 3)], writes=[('da_t', 0)]),
                            lambda: P.op('vector', lambda e: e.tensor_tensor(out=c1[:, 0:n], in0=c1[:, 0:n], in1=c2[:, 0:n], op=ALU.mult), reads=[('da_t', 1), ('da_t', 2)], writes=[('da_t', 1)]),
                            lambda: P.op('vector', lambda e: e.scalar_tensor_tensor(out=c0[:, 0:n], in0=c1[:, 0:n], scalar=neg_lam, in1=c0[:, 0:n], op0=ALU.mult, op1=ALU.add),
                                         reads=[('da_t', 0), ('da_t', 1), 'lam'], writes=[('da_t', 0)]),
                            lambda: P.op('scalar', lambda e: e.activation(out=sqb[:, 0:n], in_=c0[:, 0:n], func=AF.Square), reads=[('da_t', 0)], writes=['da_sq']),
                            lambda: P.op('tensor', lambda e: e.matmul(k.ps[3][:, 0:n], lhsT=k.one1[:], rhs=sqb[:, 0:n], start=True, stop=True), reads=['da_sq', 'one1'], writes=[('ps', 3)]),
                            lambda: P.op('scalar', lambda e: e.activation(out=c1[:, 0:n], in_=k.ps[3][:, 0:n], func=AF.Ln, bias=k.epsc[:, 0:1], scale=1.0 / 128.0),
                                         reads=[('ps', 3), 'epsc'], writes=[('da_t', 1)]),
                            lambda: P.op('scalar', lambda e: e.activation(out=c1[:, 0:n], in_=c1[:, 0:n], func=AF.Exp, scale=-0.5), reads=[('da_t', 1)], writes=[('da_t', 1)]),
                            lambda: P.op('vector', lambda e: e.scalar_tensor_tensor(out=ob[:, j, b.sl], in0=c0[:, 0:n], scalar=gsub, in1=c1[:, 0:n], op0=ALU.mult, op1=ALU.mult),
                                         reads=[('da_t', 0), ('da_t', 1), 'lam'], writes=[('o', j, b.key)]),
                        ]
                    fin.extend(mk(n, j, b))
            while fin:
                fin.pop(0)()
            wo_part = k.dram['da_w_o'][hp * 256:(hp + 1) * 256, :]
            for grp in ([LAT_BLKS[0], LAT_BLKS[1]], [LAT_BLKS[2], LAT_BLKS[3]]) + (([CTX_BLK],) if ctx_next else ()):
                out_proj_residual(k, li, sub, grp, wo_part, 2,
                                  lambda kc, b: ob[:, kc, b.sl], lambda kc, b: [('o', kc, b.key)], None, do_ln=False)
    P.barrier()
    with ExitStack() as st:
        lnt = alloc_ln_tmps(k, st)
        for b in (allb if ctx_next else LAT_BLKS):
            if b.s == 0 and b.t0 < 1024:
                layer_norm_blk(k, b, lnt)
            else:
                k.ln_pending.append(b)


def retention(k, li, use_ctx, ctx_next):
    P = k.P
    sub = 1
    allb = LAT_BLKS + [CTX_BLK]
    w_in = k.dram['rt_w_in']
    with ExitStack() as st:
        hb = k.sb('rt_h', [128, NCH, TT], BF16, st)
        cst = k.sb('rt_cst', [128, 1024 + 4 + 16 + 18], F32, st)
        qT = k.sb('rt_q', [128, 2, SEQ], BF16, st)
        kT = k.sb('rt_k', [128, 2, TT], BF16, st)
        vtm = k.sb('rt_v', [128, 18, 512], BF16, st)
        lg = k.sb('rt_lg', [128, 8], F32, st)
        gng = k.sb('rt_gng', [128, 16], F32, st)
        ksf = k.sb('rt_ksf', [128, 16], F32, st)
        ksb = k.sb('rt_ksb', [128, 18], F32, st)
        o512 = k.sb('rt_o512', [128, 128], BF16, st)
        P.op('sync', lambda e: e.dma_start(out=cst[:], in_=k.dram['rt_const']), writes=['rt_cst'], dma=True)
        P.op('sync', lambda e: e.dma_start(out=lg[:], in_=k.dram['rt_decay']), writes=['rt_lg'], dma=True)
        P.op('sync', lambda e: e.dma_start(out=gng[:], in_=k.dram['rt_gng']), writes=['rt_gng'], dma=True)
        P.op('vector', lambda e: e.memset(o512[:], 1.0 / 512.0), writes=['o512'])
        io_f = cst[:, 0:512]
        io_b = cst[:, 512:1024]
        offs = cst[:, 1024:1028]
        E_f = cst[:, 1028:1044]
        E_b = cst[:, 1044:1062]
        P.op('scalar', lambda e: e.activation(out=lg[:], in_=lg[:], func=AF.Exp, scale=-1.0), reads=['rt_lg'], writes=['rt_lg'])
        P.op('scalar', lambda e: e.activation(out=lg[:], in_=lg[:], func=AF.Ln, bias=1.0, scale=1.0), reads=['rt_lg'], writes=['rt_lg'])
        P.op('vector', lambda e: e.tensor_scalar(out=lg[:], in0=lg[:], scalar1=-1.0, scalar2=None, op0=ALU.mult), reads=['rt_lg'], writes=['rt_lg'])
        make_h(k, li, sub, allb, hb, 0)
        make_xa(k, li, sub, LAT_BLKS)
        sr = [0]

        def sbank():
            b_ = 4 + (sr[0] % 4)
            sr[0] += 1
            return b_

        for hd in range(4):
            lgf = lg[:, hd:hd + 1]
            lgb = lg[:, 4 + hd:5 + hd]
            P.op('scalar', (lambda lgf: lambda e: e.activation(out=ksf[:], in_=E_f, func=AF.Exp, scale=lgf))(lgf), reads=['rt_cst', 'rt_lg'], writes=['ksf'])
            P.op('scalar', (lambda lgb: lambda e: e.activation(out=ksb[:], in_=E_b, func=AF.Exp, scale=lgb))(lgb), reads=['rt_cst', 'rt_lg'], writes=['ksb'])
            P.op('vector', lambda e: e.tensor_scalar(out=ksf[:], in0=ksf[:], scalar1=1.0 / 16.0, scalar2=None, op0=ALU.mult), reads=['ksf'], writes=['ksf'])
            P.op('vector', lambda e: e.tensor_scalar(out=ksb[:], in0=ksb[:], scalar1=1.0 / 16.0, scalar2=None, op0=ALU.mult), reads=['ksb'], writes=['ksb'])
            with ExitStack() as s1:
                rope = k.sb('rt_rope', [128, 2 * SEQ], BF16, s1)
                tt = [k.sb('rt_t%d' % i, [128, 512], F32, s1) for i in range(4)]
                P.op('sync', lambda e: e.dma_start(out=rope[:], in_=k.dram['rope_rt']), writes=['rope'], dma=True)
                cosT = rope[:, 0:SEQ]
                sinT = rope[:, SEQ:2 * SEQ]
                for (dstT, c0, dkey, blks_) in ((qT, hd * 256, 'q', LAT_BLKS), (kT, D + hd * 256, 'k', allb)):
                    s_, w_ = load_w(k, w_in[:, c0:c0 + 256], NCH, 256)
                    for b in blks_:
                        pa, pb_ = next_ps(k, 2)
                        for j, pp in enumerate((pa, pb_)):
                            for kc in range(NCH):
                                P.op('tensor', (lambda pp, w_, kc, j, b: lambda e: e.matmul(
                                    k.ps[pp][:, 0:b.n], lhsT=w_[:, kc, j * 128:(j + 1) * 128], rhs=hb[:, kc, b.sl],
                                    start=(kc == 0), stop=(kc == NCH - 1)))(pp, w_, kc, j, b),
                                    reads=[('ws', s_), ('h', kc, b.key)], writes=[('ps', pp)])
                        if b.s == 1:
                            for j, pp in enumerate((pa, pb_)):
                                P.op('vector', (lambda pp, dstT, j, b: lambda e: e.tensor_copy(out=dstT[:, j, b.sl], in_=k.ps[pp][:, 0:b.n]))(pp, dstT, j, b),
                                     reads=[('ps', pp)], writes=[(dkey, j, b.key)])
                            continue
                        for j in range(2):
                            t1, t2 = tt[2 * j], tt[2 * j + 1]
                            k1, k2 = ('rt_t', 2 * j), ('rt_t', 2 * j + 1)
                            tabA = cosT if j == 0 else sinT
                            tabB = sinT if j == 0 else cosT
                            opf = ALU.subtract if j == 0 else ALU.add
                            P.op('vector', (lambda t1, pa, b, tabA: lambda e: e.tensor_tensor(out=t1[:, 0:b.n], in0=k.ps[pa][:, 0:b.n], in1=tabA[:, b.sl], op=ALU.mult))(t1, pa, b, tabA),
                                 reads=[('ps', pa), 'rope'], writes=[k1])
                            P.op('vector', (lambda t2, pb_, b, tabB: lambda e: e.tensor_tensor(out=t2[:, 0:b.n], in0=k.ps[pb_][:, 0:b.n], in1=tabB[:, b.sl], op=ALU.mult))(t2, pb_, b, tabB),
                                 reads=[('ps', pb_), 'rope'], writes=[k2])
                            P.op('vector', (lambda t1, t2, dstT, j, b, opf: lambda e: e.tensor_tensor(out=dstT[:, j, b.sl], in0=t1[:, 0:b.n], in1=t2[:, 0:b.n], op=opf))(t1, t2, dstT, j, b, opf),
                                 reads=[k1, k2], writes=[(dkey, j, b.key)])
                wv = [load_w(k, w_in[:, 2 * D + hd * 512 + hh * 256:2 * D + hd * 512 + (hh + 1) * 256], NCH, 256) for hh in range(2)]
                for kc18 in range(18):
                    pv = next_ps(k)
                    bkey = allb[kc18 // 4].key if kc18 < 16 else CTX_BLK.key
                    for hh in range(2):
                        s_, w_ = wv[hh]
                        for kc in range(NCH):
                            P.op('tensor', (lambda pv, kc, kc18, w_, hh: lambda e: e.matmul(
                                k.ps[pv][:, hh * 256:(hh + 1) * 256], lhsT=hb[:, kc, kc18 * 128:(kc18 + 1) * 128], rhs=w_[:, kc, :],
                                start=(kc == 0), stop=(kc == NCH - 1)))(pv, kc, kc18, w_, hh),
                                reads=[('ws', s_), ('h', kc, bkey)], writes=[('ps', pv)])
                    P.op('scalar', (lambda pv, kc18: lambda e: e.activation(out=vtm[:, kc18, :], in_=k.ps[pv][:], func=AF.Copy))(pv, kc18),
                         reads=[('ps', pv)], writes=[('v', kc18)])
            P.barrier()
            with ExitStack() as s2:
                masks = k.sb('rt_mask', [128, 4, 512], BF16, s2)
                Dq = k.sb('rt_dq', [128, 1024], F32, s2)
                pt = [k.sb('rt_p%d' % i, [128, 512], BF16, s2) for i in range(3)]
                obf = k.sb('rt_obf', [128, 4, 512], BF16, s2)
                sqf = k.sb('rt_sqf', [128, 4, 512], BF16, s2)
                rstd = k.sb('rt_rstd', [128, 512], F32, s2)
                ta = k.sb('rt_ta', [128, 512], F32, s2)
                tb_ = k.sb('rt_tb', [128, 512], F32, s2)
                tc_ = k.sb('rt_tc', [128, 512], F32, s2)
                zb = obf
                P.op('scalar', (lambda lgf: lambda e: e.activation(out=Dq[:, 0:512], in_=io_f, func=AF.Exp, scale=lgf))(lgf), reads=['rt_cst', 'rt_lg'], writes=['dq'])
                P.op('scalar', (lambda lgb: lambda e: e.activation(out=Dq[:, 512:1024], in_=io_b, func=AF.Exp, scale=lgb))(lgb), reads=['rt_cst', 'rt_lg'], writes=['dq'])
                for v in range(4):
                    ov = offs[:, v:v + 1]
                    P.op('vector', (lambda ov: lambda e: e.tensor_scalar(out=tc_[:], in0=io_f, scalar1=ov, scalar2=None, op0=ALU.subtract))(ov), reads=['rt_cst'], writes=['tc'])
                    P.op('vector', lambda e: e.tensor_scalar(out=ta[:], in0=tc_[:], scalar1=0.0, scalar2=None, op0=ALU.max), reads=['tc'], writes=['ta'])
                    P.op('scalar', (lambda lgf: lambda e: e.activation(out=ta[:], in_=ta[:], func=AF.Exp, scale=lgf))(lgf), reads=['ta', 'rt_lg'], writes=['ta'])
                    P.op('vector', lambda e: e.tensor_scalar(out=tb_[:], in0=tc_[:], scalar1=0.0, scalar2=None, op0=ALU.is_ge), reads=['tc'], writes=['tb'])
                    P.op('vector', lambda e: e.tensor_tensor(out=ta[:], in0=ta[:], in1=tb_[:], op=ALU.mult), reads=['ta', 'tb'], writes=['ta'])
                    P.op('vector', lambda e: e.tensor_scalar(out=tb_[:], in0=tc_[:], scalar1=-1.0, scalar2=0.0, op0=ALU.mult, op1=ALU.max), reads=['tc'], writes=['tb'])
                    P.op('scalar', (lambda lgb: lambda e: e.activation(out=tb_[:], in_=tb_[:], func=AF.Exp, scale=lgb))(lgb), reads=['tb', 'rt_lg'], writes=['tb'])
                    P.op('vector', lambda e: e.tensor_scalar(out=tc_[:], in0=tc_[:], scalar1=0.0, scalar2=None, op0=ALU.is_le), reads=['tc'], writes=['tc'])
                    P.op('vector', lambda e: e.tensor_tensor(out=tb_[:], in0=tb_[:], in1=tc_[:], op=ALU.mult), reads=['tb', 'tc'], writes=['tb'])
                    P.op('vector', lambda e: e.tensor_tensor(out=ta[:], in0=ta[:], in1=tb_[:], op=ALU.add), reads=['ta', 'tb'], writes=['ta'])
                    P.op('vector', (lambda v: lambda e: e.tensor_scalar(out=masks[:, v, :], in0=ta[:], scalar1=1.0 / 16.0, scalar2=None, op0=ALU.mult))(v), reads=['ta'], writes=[('mask', v)])
                pi = [0]
                for bi, b in enumerate(LAT_BLKS):
                    tiles = []
                    for cc in range(2):
                        tiles.append((16 + cc, 'f', 4 * bi + 2 - cc))
                    for kc in range(16):
                        if kc < 4 * bi:
                            tiles.append((kc, 'f', 4 * bi - kc))
                        elif kc > 4 * bi + 3:
                            tiles.append((kc, 'b', kc - 4 * bi))
                        else:
                            tiles.append((kc, 'd', kc - 4 * bi))
                    for cc in range(2):
                        tiles.append((16 + cc, 'b', 16 + cc - 4 * bi))
                    acc = [0, 1, 2, 3]
                    pend = None
                    for idx in range(len(tiles) + 1):
                        cur = None
                        if idx < len(tiles):
                            kc18, kind, par = tiles[idx]
                            kblk = allb[kc18 // 4].key if kc18 < 16 else CTX_BLK.key
                            if idx in (2, 7, 12):
                                bg_step(k, sbank())
                            elif idx == 17:
                                bg_drain(k, sbank())
                            sbk = sbank()
                            for c in range(2):
                                P.op('tensor', (lambda c, kc18, b, sbk: lambda e: e.matmul(
                                    k.ps[sbk][:], lhsT=kT[:, c, kc18 * 128:(kc18 + 1) * 128], rhs=qT[:, c, b.sl],
                                    start=(c == 0), stop=(c == 1)))(c, kc18, b, sbk),
                                    reads=[('k', c, kblk), ('q', c, b.key)], writes=[('ps', sbk)])
                            p_ = pt[pi[0] % 3]
                            pk = ('rt_p', pi[0] % 3)
                            pi[0] += 1
                            if kind == 'f':
                                P.op('vector', (lambda p_, sbk, par: lambda e: e.scalar_tensor_tensor(out=p_[:], in0=k.ps[sbk][:], scalar=ksf[:, par:par + 1], in1=Dq[:, 0:512], op0=ALU.mult, op1=ALU.mult))(p_, sbk, par),
                                     reads=[('ps', sbk), 'ksf', 'dq'], writes=[pk])
                            elif kind == 'b':
                                P.op('vector', (lambda p_, sbk, par: lambda e: e.scalar_tensor_tensor(out=p_[:], in0=k.ps[sbk][:], scalar=ksb[:, par:par + 1], in1=Dq[:, 512:1024], op0=ALU.mult, op1=ALU.mult))(p_, sbk, par),
                                     reads=[('ps', sbk), 'ksb', 'dq'], writes=[pk])
                            else:
                                P.op('vector', (lambda p_, sbk, par: lambda e: e.tensor_tensor(out=p_[:], in0=k.ps[sbk][:], in1=masks[:, par, :], op=ALU.mult))(p_, sbk, par),
                                     reads=[('ps', sbk), ('mask', par)], writes=[pk])
                            cur = (p_, pk, kc18, idx)
                        if pend is not None:
                            p_, pk, kc18, pidx = pend
                            for e_ in range(4):
                                P.op('tensor', (lambda p_, kc18, e_, pidx: lambda e: e.matmul(
                                    k.ps[acc[e_]][:], lhsT=vtm[:, kc18, e_ * 128:(e_ + 1) * 128], rhs=p_[:],
                                    start=(pidx == 0), stop=(pidx == len(tiles) - 1)))(p_, kc18, e_, pidx),
                                    reads=[pk, ('v', kc18)], writes=[('ps', acc[e_])])
                        pend = cur
                    for e_ in range(4):
                        P.op('scalar', (lambda e_: lambda e: e.activation(out=obf[:, e_, :], in_=k.ps[acc[e_]][:], func=AF.Copy))(e_), reads=[('ps', acc[e_])], writes=[('obf', e_)])
                        P.op('scalar', (lambda e_: lambda e: e.activation(out=sqf[:, e_, :], in_=k.ps[acc[e_]][:], func=AF.Square))(e_), reads=[('ps', acc[e_])], writes=[('sqf', e_)])
                    p1, p2 = sbank(), sbank()
                    for e_ in range(4):
                        P.op('tensor', (lambda e_, p1: lambda e: e.matmul(k.ps[p1][:], lhsT=o512[:], rhs=obf[:, e_, :], start=(e_ == 0), stop=(e_ == 3)))(e_, p1),
                             reads=[('obf', e_), 'o512'], writes=[('ps', p1)])
                    for e_ in range(4):
                        P.op('tensor', (lambda e_, p2: lambda e: e.matmul(k.ps[p2][:], lhsT=o512[:], rhs=sqf[:, e_, :], start=(e_ == 0), stop=(e_ == 3)))(e_, p2),
                             reads=[('sqf', e_), 'o512'], writes=[('ps', p2)])
                    P.op('scalar', (lambda p1: lambda e: e.activation(out=tc_[:], in_=k.ps[p1][:], func=AF.Copy))(p1), reads=[('ps', p1)], writes=['tc'])
                    P.op('vector', lambda e: e.tensor_tensor(out=ta[:], in0=tc_[:], in1=tc_[:], op=ALU.mult), reads=['tc'], writes=['ta'])
                    P.op('vector', (lambda p2: lambda e: e.tensor_tensor(out=ta[:], in0=k.ps[p2][:], in1=ta[:], op=ALU.subtract))(p2), reads=[('ps', p2), 'ta'], writes=['ta'])
                    P.op('scalar', lambda e: e.activation(out=ta[:], in_=ta[:], func=AF.Sqrt, bias=k.epsc[:, 0:1], scale=1.0), reads=['ta', 'epsc'], writes=['ta'])
                    P.op('vector', lambda e: e.reciprocal(out=rstd[:], in_=ta[:]), reads=['ta'], writes=['rt_rstd'])
                    P.op('vector', lambda e: e.scalar_tensor_tensor(out=tc_[:], in0=tc_[:], scalar=-1.0, in1=rstd[:], op0=ALU.mult, op1=ALU.mult), reads=['tc', 'rt_rstd'], writes=['tc'])
                    wg = [load_w(k, w_in[:, 4 * D + hd * 512 + hh * 256:4 * D + hd * 512 + (hh + 1) * 256], NCH, 256) for hh in range(2)]
                    for e_ in range(4):
                        pg = sbank()
                        s_, w_ = wg[e_ // 2]
                        for kc in range(NCH):
                            P.op('tensor', (lambda pg, kc, w_, e_, b: lambda e: e.matmul(
                                k.ps[pg][:], lhsT=w_[:, kc, (e_ % 2) * 128:(e_ % 2 + 1) * 128], rhs=hb[:, kc, b.sl],
                                start=(kc == 0), stop=(kc == NCH - 1)))(pg, kc, w_, e_, b),
                                reads=[('ws', s_), ('h', kc, b.key)], writes=[('ps', pg)])
                        P.op('scalar', (lambda pg: lambda e: e.activation(out=tb_[:], in_=k.ps[pg][:], func=AF.Silu))(pg), reads=[('ps', pg)], writes=['tb'])
                        P.op('vector', (lambda e_: lambda e: e.tensor_tensor(out=ta[:], in0=k.ps[acc[e_]][:], in1=rstd[:], op=ALU.mult))(e_), reads=[('ps', acc[e_]), 'rt_rstd'], writes=['ta'])
                        P.op('vector', lambda e: e.tensor_tensor(out=ta[:], in0=ta[:], in1=tc_[:], op=ALU.add), reads=['ta', 'tc'], writes=['ta'])
                        gcol = gng[:, hd * 4 + e_:hd * 4 + e_ + 1]
                        P.op('vector', (lambda e_, gcol: lambda e: e.scalar_tensor_tensor(out=zb[:, e_, :], in0=ta[:], scalar=gcol, in1=tb_[:], op0=ALU.mult, op1=ALU.mult))(e_, gcol),
                             reads=['ta', 'tb', 'rt_gng'], writes=[('obf', e_)])
                    out_proj_residual(k, li, sub, [b], k.dram['rt_w_o'][hd * 512:(hd + 1) * 512, :], 4,
                                      lambda kc, b: zb[:, kc, :], lambda kc, b: [('obf', kc)], None, do_ln=False)
            P.barrier()
    P.barrier()
    with ExitStack() as st:
        lnt = alloc_ln_tmps(k, st)
        for b in LAT_BLKS:
            if b.s == 0 and b.t0 < 1024:
                layer_norm_blk(k, b, lnt)
            else:
                k.ln_pending.append(b)


def hyena(k, li, use_ctx, ctx_next):
    P = k.P
    sub = 1
    allb = LAT_BLKS + [CTX_BLK]
    w_in = k.dram['hy_w_in']
    PI = math.pi
    UW = TT + 4

    def ucol(b):
        return 1 + b.t0 if b.s == 0 else SEQ + 3 + b.t0

    with ExitStack() as st:
        hb = k.sb('hy_h', [128, NCH, TT], BF16, st)
        h2T = k.sb('hy_h2T', [64, TT], BF16, st)
        cw = k.sb('hy_cw', [128, 72], F32, st)
        cb = k.sb('hy_cb', [128, 24], F32, st)
        tn = k.sb('hy_tn', [128, 18], F32, st)
        ident = k.sb('hy_ident', [128, 128], BF16, st)
        P.op('sync', lambda e: e.dma_start(out=cw[:], in_=k.dram['hy_cw']), writes=['hy_cw'], dma=True)
        P.op('sync', lambda e: e.dma_start(out=cb[:], in_=k.dram['hy_cb']), writes=['hy_cb'], dma=True)
        P.op('sync', lambda e: e.dma_start(out=tn[:], in_=k.dram['hy_tn']), writes=['hy_tn'], dma=True)
        P.op('sync', lambda e: e.dma_start(out=ident[:], in_=k.dram['ident']), writes=['hy_ident'], dma=True)
        make_h(k, li, sub, allb, hb, 0)
        make_xa(k, li, sub, allb)
        with ExitStack() as s0:
            fw1 = k.sb('hy_fw1', [33, 64], F32, s0)
            fw2 = k.sb('hy_fw2', [64, 64], F32, s0)
            v64 = k.sb('hy_v64', [64, 4], F32, s0)
            npi = k.sb('hy_npi', [64, 1], F32, s0)
            zt = [k.sb('hy_zt%d' % i, [33, 512], F32, s0) for i in range(2)]
            m1 = k.sb('hy_m1', [64, 512], F32, s0)
            m2 = k.sb('hy_m2', [64, 512], F32, s0)
            mw = k.sb('hy_mw', [64, 512], F32, s0)
            P.op('sync', lambda e: e.dma_start(out=fw1[:], in_=k.dram['hy_fw1']), writes=['fw1'], dma=True)
            P.op('sync', lambda e: e.dma_start(out=fw2[:], in_=k.dram['hy_fw2']), writes=['fw2'], dma=True)
            P.op('sync', lambda e: e.dma_start(out=v64[:], in_=k.dram['hy_vec64']), writes=['v64'], dma=True)
            P.op('vector', lambda e: e.memset(npi[:], -PI), writes=['npi'])
            for bi, b in enumerate(allb):
                z_ = zt[bi % 2]
                zk = ('zt', bi % 2)
                n = b.n
                P.op('sync', (lambda z_, b: lambda e: e.dma_start(out=z_[:, 0:b.n], in_=k.dram['hy_zT'][:, b.sl]))(z_, b), writes=[zk], dma=True)
                p1 = next_ps(k)
                P.op('tensor', (lambda p1, z_, n: lambda e: e.matmul(k.ps[p1][0:64, 0:n], lhsT=fw1[:], rhs=z_[:, 0:n], start=True, stop=True))(p1, z_, n),
                     reads=['fw1', zk], writes=[('ps', p1)])
                P.op('vector', (lambda p1, n: lambda e: e.tensor_scalar(out=m1[:, 0:n], in0=k.ps[p1][0:64, 0:n], scalar1=v64[:, 0:1], scalar2=v64[:, 1:2], op0=ALU.add, op1=ALU.mult))(p1, n),
                     reads=[('ps', p1), 'v64'], writes=['m1'])
                P.op('vector', (lambda n: lambda e: e.tensor_scalar(out=mw[:, 0:n], in0=m1[:, 0:n], scalar1=PI, scalar2=None, op0=ALU.is_gt))(n), reads=['m1'], writes=['mw'])
                P.op('vector', (lambda n: lambda e: e.scalar_tensor_tensor(out=m1[:, 0:n], in0=mw[:, 0:n], scalar=-2.0 * PI, in1=m1[:, 0:n], op0=ALU.mult, op1=ALU.add))(n), reads=['m1', 'mw'], writes=['m1'])
                P.op('vector', (lambda n: lambda e: e.tensor_scalar(out=mw[:, 0:n], in0=m1[:, 0:n], scalar1=-PI, scalar2=None, op0=ALU.is_lt))(n), reads=['m1'], writes=['mw'])
                P.op('vector', (lambda n: lambda e: e.scalar_tensor_tensor(out=m1[:, 0:n], in0=mw[:, 0:n], scalar=2.0 * PI, in1=m1[:, 0:n], op0=ALU.mult, op1=ALU.add))(n), reads=['m1', 'mw'], writes=['m1'])
                P.op('scalar', (lambda n: lambda e: e.activation(out=m1[:, 0:n], in_=m1[:, 0:n], func=AF.Sin))(n), reads=['m1'], writes=['m1'])
                p2 = next_ps(k)
                P.op('tensor', (lambda p2, n: lambda e: e.matmul(k.ps[p2][0:64, 0:n], lhsT=fw2[:], rhs=m1[:, 0:n], start=True, stop=True))(p2, n),
                     reads=['fw2', 'm1'], writes=[('ps', p2)])
                P.op('vector', (lambda p2, n: lambda e: e.tensor_scalar(out=m2[:, 0:n], in0=k.ps[p2][0:64, 0:n], scalar1=v64[:, 2:3], scalar2=v64[:, 3:4], op0=ALU.add, op1=ALU.mult))(p2, n),
                     reads=[('ps', p2), 'v64'], writes=['m2'])
                P.op('vector', (lambda n: lambda e: e.tensor_scalar(out=mw[:, 0:n], in0=m2[:, 0:n], scalar1=PI, scalar2=None, op0=ALU.is_gt))(n), reads=['m2'], writes=['mw'])
                P.op('vector', (lambda n: lambda e: e.scalar_tensor_tensor(out=m2[:, 0:n], in0=mw[:, 0:n], scalar=-2.0 * PI, in1=m2[:, 0:n], op0=ALU.mult, op1=ALU.add))(n), reads=['m2', 'mw'], writes=['m2'])
                P.op('vector', (lambda n: lambda e: e.tensor_scalar(out=mw[:, 0:n], in0=m2[:, 0:n], scalar1=-PI, scalar2=None, op0=ALU.is_lt))(n), reads=['m2'], writes=['mw'])
                P.op('vector', (lambda n: lambda e: e.scalar_tensor_tensor(out=m2[:, 0:n], in0=mw[:, 0:n], scalar=2.0 * PI, in1=m2[:, 0:n], op0=ALU.mult, op1=ALU.add))(n), reads=['m2', 'mw'], writes=['m2'])
                P.op('scalar', (lambda n, b: lambda e: e.activation(out=h2T[:, b.sl], in_=m2[:, 0:n], func=AF.Sin))(n, b), reads=['m2'], writes=[('h2T', b.key)])
        P.barrier()
        HS = os.environ.get('HYSTOP', '')

        def proj_conv(qi, fc, wslot, wcol, ub, ubk, emit_fn, tmps):
            s_, w_ = wslot
            for b in allb:
                pp = next_ps(k)
                for kc in range(NCH):
                    P.op('tensor', (lambda pp, kc, b, w_, wcol: lambda e: e.matmul(
                        k.ps[pp][:, 0:b.n], lhsT=w_[:, kc, wcol * 128:(wcol + 1) * 128], rhs=hb[:, kc, b.sl],
                        start=(kc == 0), stop=(kc == NCH - 1)))(pp, kc, b, w_, wcol),
                        reads=[('ws', s_), ('h', kc, b.key)], writes=[('ps', pp)])
                P.op('scalar', (lambda pp, b: lambda e: e.activation(out=ub[:, ucol(b):ucol(b) + b.n], in_=k.ps[pp][:, 0:b.n], func=AF.Copy))(pp, b),
                     reads=[('ps', pp)], writes=[(ubk, b.key)])
            w0 = cw[:, 0 * 24 + fc:0 * 24 + fc + 1]
            w1 = cw[:, 1 * 24 + fc:1 * 24 + fc + 1]
            w2 = cw[:, 2 * 24 + fc:2 * 24 + fc + 1]
            bia = cb[:, fc:fc + 1]
            for bi, b in enumerate(allb):
                t_, tk = tmps[bi % 2]
                o = ucol(b)
                nb_keys = [(ubk, bb.key) for bb in allb if bb.s == b.s and abs(bb.t0 - b.t0) <= 512] + [(ubk, 'pad'), 'hy_cw', 'hy_cb']
                P.op('scalar', (lambda t_, o, b: lambda e: e.activation(out=t_[:, 0:b.n], in_=ub[:, o:o + b.n], func=AF.Identity, bias=bia, scale=w1))(t_, o, b),
                     reads=nb_keys, writes=[tk])
                P.op('vector', (lambda t_, o, b: lambda e: e.scalar_tensor_tensor(out=t_[:, 0:b.n], in0=ub[:, o - 1:o - 1 + b.n], scalar=w0, in1=t_[:, 0:b.n], op0=ALU.mult, op1=ALU.add))(t_, o, b),
                     reads=nb_keys + [tk], writes=[tk])
                P.op('vector', (lambda t_, o, b: lambda e: e.scalar_tensor_tensor(out=t_[:, 0:b.n], in0=ub[:, o + 1:o + 1 + b.n], scalar=w2, in1=t_[:, 0:b.n], op0=ALU.mult, op1=ALU.add))(t_, o, b),
                     reads=nb_keys + [tk], writes=[tk])
                emit_fn(b, t_, tk)

        def zero_pads(ub, ubk):
            for c0 in (0, SEQ + 1, SEQ + 2, TT + 3):
                P.op('vector', (lambda c0: lambda e: e.memset(ub[:, c0:c0 + 1], 0.0))(c0), writes=[(ubk, 'pad')])

        for cg in range(4 if HS == '' else (0 if HS == 'mlp' else 1)):
            with ExitStack() as sY:
                Y = k.sb('hy_Y', [128, 18, 512], BF16, sY)
                with ExitStack() as sP:
                    ptm = k.sb('hy_ptm', [128, 18, 256], BF16, sP)
                    with ExitStack() as sA:
                        ub0 = k.sb('hy_ub0', [128, UW], F32, sA)
                        x1c = k.sb('hy_x1c', [128, TT], F32, sA)
                        pfm = k.sb('hy_pfm', [128, 2, TT], BF16, sA)
                        tA = [(k.sb('hy_ta%d' % i, [128, 512], F32, sA), ('hy_ta', i)) for i in range(2)]
                        zero_pads(ub0, 'ub0')
                        wx1 = load_w(k, w_in[:, D + cg * 256:D + (cg + 1) * 256], NCH, 256)
                        wv_ = load_w(k, w_in[:, 2 * D + cg * 256:2 * D + (cg + 1) * 256], NCH, 256)
                        for c2 in range(2):
                            def emit_x1(b, t_, tk):
                                P.op('scalar', (lambda b, t_: lambda e: e.activation(out=x1c[:, b.sl], in_=t_[:, 0:b.n], func=AF.Copy))(b, t_),
                                     reads=[tk], writes=[('x1c', b.key)])
                            proj_conv(1, 8 + cg * 2 + c2, wx1, c2, ub0, 'ub0', emit_x1, tA)

                            def emit_v(b, t_, tk, c2=c2):
                                P.op('vector', (lambda b, t_, c2: lambda e: e.tensor_tensor(out=pfm[:, c2, b.sl], in0=t_[:, 0:b.n], in1=x1c[:, b.sl], op=ALU.mult))(b, t_, c2),
                                     reads=[tk, ('x1c', b.key)], writes=[('pfm', c2, b.key)])
                            proj_conv(2, 16 + cg * 2 + c2, wv_, c2, ub0, 'ub0', emit_v, tA)
                        for tc in range(0, 18, 2):
                            pb = next_ps(k)
                            for dt_ in range(2):
                                bkey = allb[(tc + dt_) // 4].key if tc + dt_ < 16 else CTX_BLK.key
                                for c2 in range(2):
                                    P.op('tensor', (lambda pb, tc, dt_, c2: lambda e: e.matmul(
                                        k.ps[pb][:, dt_ * 256 + c2 * 128:dt_ * 256 + (c2 + 1) * 128], lhsT=pfm[:, c2, (tc + dt_) * 128:(tc + dt_ + 1) * 128], rhs=ident[:],
                                        start=True, stop=True))(pb, tc, dt_, c2),
                                        reads=[('pfm', c2, bkey), 'hy_ident'], writes=[('ps', pb)])
                            P.op('scalar', (lambda pb, tc: lambda e: e.activation(out=ptm[:, tc:tc + 2, :], in_=k.ps[pb][:].rearrange("p (a b) -> p a b", a=2), func=AF.Copy))(pb, tc),
                                 reads=[('ps', pb)], writes=[('ptm', tc), ('ptm', tc + 1)])
                    P.barrier()
                    if HS == 'A':
                        continue
                    with ExitStack() as sH:
                        hsum = k.sb('hy_hsum', [128, 18, 256], BF16, sH)
                        hdif = k.sb('hy_hdif', [128, 18, 256], BF16, sH)
                        with ExitStack() as sF:
                            w3c = k.sb('hy_w3c', [64, 512], BF16, sF)
                            dlc = k.sb('hy_dlc', [128, 256], F32, sF)
                            decs = [k.sb('hy_dec%d' % i, [128, 256], F32, sF) for i in range(2)]
                            fas = [k.sb('hy_fa%d' % i, [128, 256], F32, sF) for i in range(2)]
                            fbs = [k.sb('hy_fb%d' % i, [128, 256], F32, sF) for i in range(2)]
                            fcs = [k.sb('hy_fc%d' % i, [128, 256], F32, sF) for i in range(2)]
                            dsk = k.sb('hy_dsk', [1, 256], F32, sF)
                            P.op('sync', (lambda cg: lambda e: e.dma_start(out=dsk[:], in_=k.dram['hy_dskip'][:, cg * 256:(cg + 1) * 256]))(cg), writes=['hy_dsk'], dma=True)
                            w3f = k.sb('hy_w3f', [64, 512], F32, sF)
                            P.op('sync', (lambda cg: lambda e: e.dma_start(out=w3f[:, 0:256], in_=k.dram['hy_fw3'][:, cg * 256:(cg + 1) * 256]))(cg), writes=['w3f_f'], dma=True)
                            P.op('sync', (lambda cg: lambda e: e.dma_start(out=w3f[:, 256:512], in_=k.dram['hy_fw3'][:, D + cg * 256:D + (cg + 1) * 256]))(cg), writes=['w3f_b'], dma=True)
                            P.op('scalar', lambda e: e.activation(out=w3c[:, 0:256], in_=w3f[:, 0:256], func=AF.Copy), reads=['w3f_f'], writes=['w3c_f'])
                            P.op('scalar', lambda e: e.activation(out=w3c[:, 256:512], in_=w3f[:, 256:512], func=AF.Copy), reads=['w3f_b'], writes=['w3c_b'])
                            P.op('sync', (lambda cg: lambda e: e.dma_start(out=dlc[:], in_=k.dram['hy_delta'][:, cg * 256:(cg + 1) * 256]))(cg), writes=['dlc'], dma=True)
                            for tix in range(18):
                                pcol = tix * 128
                                bkey = allb[tix // 4].key if tix < 16 else CTX_BLK.key
                                first = tix in (0, 16)
                                par = tix % 2
                                dec, fa, fb_, fc_ = decs[par], fas[par], fbs[par], fcs[par]
                                kd, ka, kb, kc_ = ('dec', par), ('fa', par), ('fb', par), ('fc', par)
                                pf = next_ps(k)
                                for hh, wk in ((0, 'w3c_f'), (1, 'w3c_b')):
                                    P.op('tensor', (lambda pf, hh, pcol: lambda e: e.matmul(
                                        k.ps[pf][:, hh * 256:(hh + 1) * 256], lhsT=h2T[:, pcol:pcol + 128], rhs=w3c[:, hh * 256:(hh + 1) * 256],
                                        start=True, stop=True))(pf, hh, pcol),
                                        reads=[('h2T', bkey), wk], writes=[('ps', pf)])
                                P.op('scalar', (lambda tix, dec: lambda e: e.activation(out=dec[:], in_=dlc[:], func=AF.Exp, scale=tn[:, tix:tix + 1]))(tix, dec),
                                     reads=['dlc', 'hy_tn'], writes=[kd])
                                P.op('vector', (lambda pf, fa, dec: lambda e: e.tensor_tensor(out=fa[:], in0=k.ps[pf][:, 0:256], in1=dec[:], op=ALU.mult))(pf, fa, dec), reads=[('ps', pf), kd], writes=[ka])
                                P.op('vector', (lambda pf, fb_, dec: lambda e: e.tensor_tensor(out=fb_[:], in0=k.ps[pf][:, 256:512], in1=dec[:], op=ALU.mult))(pf, fb_, dec), reads=[('ps', pf), kd], writes=[kb])
                                if first:
                                    P.op('vector', (lambda fb_: lambda e: e.memset(fb_[0:1, :], 0.0))(fb_), reads=[kb], writes=[kb])
                                P.op('vector', (lambda fa, fb_, fc_: lambda e: e.tensor_tensor(out=fc_[:], in0=fa[:], in1=fb_[:], op=ALU.add))(fa, fb_, fc_), reads=[ka, kb], writes=[kc_])
                                if first:
                                    P.op('vector', (lambda fc_: lambda e: e.tensor_tensor(out=fc_[0:1, :], in0=fc_[0:1, :], in1=dsk[0:1, :], op=ALU.add))(fc_),
                                         reads=[kc_, 'hy_dsk'], writes=[kc_])
                                P.op('scalar', (lambda tix, fc_: lambda e: e.activation(out=hsum[:, tix, :], in_=fc_[:], func=AF.Copy))(tix, fc_), reads=[kc_], writes=[('hsum', tix)])
                                P.op('vector', (lambda tix, fa, fb_: lambda e: e.tensor_tensor(out=hdif[:, tix, :], in0=fa[:], in1=fb_[:], op=ALU.subtract))(tix, fa, fb_), reads=[ka, kb], writes=[('hdif', tix)])
                        P.barrier()
                        if HS == 'F':
                            continue
                        with ExitStack() as sB:
                            fs = [k.sb('hy_fs%d' % i, [128, 2048], BF16, sB) for i in range(3)]
                            Gs = k.sb('hy_Gs', [128, 512], F32, sB)
                            tq = [k.sb('hy_tq%d' % i, [128, 256], F32, sB) for i in range(2)]
                            fsi = 0
                            for (nkch, ntch, t0x, src) in ((16, 16, 0, 'dft_f'), (2, 2, 16, 'dft_fc')):
                                for kch in range(nkch):
                                    bg_step(k, next_ps(k))
                                    pu, pg = next_ps(k, 2)
                                    nel = ntch * 128
                                    for m in range(2):
                                        f_ = fs[fsi % 3]
                                        fk = ('fs', fsi % 3)
                                        fsi += 1
                                        P.op('sync', (lambda f_, kch, nel, src, m: lambda e: e.dma_start(out=f_[:, 0:nel], in_=k.dram[src][kch][:, m * nel:(m + 1) * nel]))(f_, kch, nel, src, m), writes=[fk], dma=True)
                                        fv = f_[:, 0:nel].rearrange("p (t q) -> p t q", t=ntch)
                                        col = 256 * m
                                        for tch in range(ntch):
                                            tix = t0x + tch
                                            for (pb, rhs, rk) in ((pu, ptm, 'ptm'), (pg, hsum if m == 0 else hdif, 'hsum' if m == 0 else 'hdif')):
                                                P.op('tensor', (lambda pb, col, rhs, tch, tix, fv: lambda e: e.matmul(
                                                    k.ps[pb][:, col:col + 256], lhsT=fv[:, tch, :], rhs=rhs[:, tix, :],
                                                    start=(tch == 0), stop=(tch == ntch - 1)))(pb, col, rhs, tch, tix, fv),
                                                    reads=[fk, (rk, tix)], writes=[('ps', pb)])
                                    kix = t0x + kch
                                    P.op('scalar', (lambda pg: lambda e: e.activation(out=Gs[:], in_=k.ps[pg][:], func=AF.Copy))(pg), reads=[('ps', pg)], writes=['Gs'])
                                    P.op('vector', (lambda pu: lambda e: e.tensor_tensor(out=tq[0][:], in0=k.ps[pu][:, 0:256], in1=Gs[:, 0:256], op=ALU.mult))(pu), reads=[('ps', pu), 'Gs'], writes=['tq0'])
                                    P.op('vector', (lambda pu: lambda e: e.tensor_tensor(out=tq[1][:], in0=k.ps[pu][:, 256:512], in1=Gs[:, 256:512], op=ALU.mult))(pu), reads=[('ps', pu), 'Gs'], writes=['tq1'])
                                    P.op('vector', (lambda kix: lambda e: e.tensor_tensor(out=Y[:, kix, 0:256], in0=tq[0][:], in1=tq[1][:], op=ALU.subtract))(kix), reads=['tq0', 'tq1'], writes=[('Y', kix)])
                                    P.op('vector', (lambda pu: lambda e: e.tensor_tensor(out=tq[0][:], in0=k.ps[pu][:, 0:256], in1=Gs[:, 256:512], op=ALU.mult))(pu), reads=[('ps', pu), 'Gs'], writes=['tq0'])
                                    P.op('vector', (lambda pu: lambda e: e.tensor_tensor(out=tq[1][:], in0=k.ps[pu][:, 256:512], in1=Gs[:, 0:256], op=ALU.mult))(pu), reads=[('ps', pu), 'Gs'], writes=['tq1'])
                                    P.op('vector', (lambda kix: lambda e: e.tensor_tensor(out=Y[:, kix, 256:512], in0=tq[0][:], in1=tq[1][:], op=ALU.add))(kix), reads=['tq0', 'tq1'], writes=[('Yi', kix)])
                            bg_drain(k, next_ps(k))
                        P.barrier()
                P.barrier()
                if HS == 'B':
                    continue
                with ExitStack() as sC:
                    isl = [k.sb('hy_is%d' % i, [128, 2048], BF16, sC) for i in range(3)]
                    ub0 = k.sb('hy_ubc', [128, UW], F32, sC)
                    x0c = k.sb('hy_x0c', [128, 2, TT], BF16, sC)
                    zb = k.sb('hy_z', [128, 2, TT], BF16, sC)
                    tA = [(k.sb('hy_tc%d' % i, [128, 512], F32, sC), ('hy_ta', i)) for i in range(2)]
                    zero_pads(ub0, 'ub0')
                    wx0 = load_w(k, w_in[:, cg * 256:(cg + 1) * 256], NCH, 256)
                    for c2 in range(2):
                        def emit_x0(b, t_, tk, c2=c2):
                            P.op('scalar', (lambda b, t_, c2: lambda e: e.activation(out=x0c[:, c2, b.sl], in_=t_[:, 0:b.n], func=AF.Copy))(b, t_, c2),
                                 reads=[tk], writes=[('x0c', c2, b.key)])
                        proj_conv(0, cg * 2 + c2, wx0, c2, ub0, 'ub0', emit_x0, tA)
                    isi = 0
                    for b in allb:
                        lat = (b.s == 0)
                        nkg = 4 if lat else 1
                        kpg = 4 if lat else 2
                        scale = 2.0 / (2 * SEQ) if lat else 2.0 / (2 * CTX)
                        acc = next_ps(k, 2)
                        for kg in range(nkg):
                            for m in range(2):
                                i_ = isl[isi % 3]
                                ik = ('is', isi % 3)
                                isi += 1
                                if lat:
                                    src = k.dram['dft_i'][(b.t0 // 512) * 4 + kg][:, m * 2048:(m + 1) * 2048]
                                    nel = 2048
                                else:
                                    src = k.dram['dft_ic'][:, m * 512:(m + 1) * 512]
                                    nel = 512
                                P.op('sync', (lambda i_, nel, src: lambda e: e.dma_start(out=i_[:, 0:nel], in_=src))(i_, nel, src), writes=[ik], dma=True)
                                iv = i_[:, 0:nel].rearrange("p (q t) -> p q t", q=kpg)
                                for kq in range(kpg):
                                    kix = (kg * 4 + kq) if lat else (16 + kq)
                                    for c2 in range(2):
                                        st_ = (kg == 0 and kq == 0 and m == 0)
                                        sp_ = (kg == nkg - 1 and kq == kpg - 1 and m == 1)
                                        pacc = acc[c2]
                                        P.op('tensor', (lambda c2, m, kix, kq, iv, b, st_, sp_, pacc: lambda e: e.matmul(
                                            k.ps[pacc][:, 0:b.n], lhsT=Y[:, kix, m * 256 + c2 * 128:m * 256 + (c2 + 1) * 128], rhs=iv[:, kq, :],
                                            start=st_, stop=sp_))(c2, m, kix, kq, iv, b, st_, sp_, pacc),
                                            reads=[ik, ('Y', kix), ('Yi', kix)], writes=[('ps', acc[c2])])
                        for c2 in range(2):
                            P.op('vector', (lambda c2, b, scale, pacc: lambda e: e.scalar_tensor_tensor(out=zb[:, c2, b.sl], in0=k.ps[pacc][:, 0:b.n], scalar=scale, in1=x0c[:, c2, b.sl], op0=ALU.mult, op1=ALU.mult))(c2, b, scale, acc[c2]),
                                 reads=[('ps', acc[c2]), ('x0c', c2, b.key)], writes=[('z', c2, b.key)])
                    for grp in ([LAT_BLKS[0], LAT_BLKS[1]], [LAT_BLKS[2], LAT_BLKS[3]], [CTX_BLK]):
                        out_proj_residual(k, li, sub, grp, k.dram['hy_w_o'][cg * 256:(cg + 1) * 256, :], 2,
                                          lambda kc, b: zb[:, kc, b.sl], lambda kc, b: [('z', kc, b.key)], None, do_ln=False)
                P.barrier()
    P.barrier()
    with ExitStack() as st:
        lnt = alloc_ln_tmps(k, st)
        for b in allb:
            if b.s == 0 and b.t0 < 1024:
                layer_norm_blk(k, b, lnt)
            else:
                k.ln_pending.append(b)


MIXERS = {0: diff_attn, 1: hyena, 2: retention}


def pvec(v):
    v = np.asarray(v, np.float32)
    lead = v.shape[:-1]
    v = v.reshape(lead + (v.shape[-1] // 128, 128))
    v = np.moveaxis(v, -1, 0)
    return np.ascontiguousarray(v.reshape(128, -1))


def axial_angles(n_tokens, dim):
    rows = n_tokens // 64
    row = np.repeat(np.arange(rows), 64).astype(np.float32)
    col = np.tile(np.arange(64), rows).astype(np.float32)
    n_freq = dim // 4
    inv = (np.float32(10000.0) ** (-np.arange(n_freq, dtype=np.float32) / np.float32(n_freq))).astype(np.float32)
    return np.concatenate([row[:, None] * inv, col[:, None] * inv], axis=-1).astype(np.float32)


def rope_table_da():
    ang = axial_angles(SEQ, 64)
    p = np.arange(128)
    d = p % 64
    fi = d % 32
    sign = np.where(d < 32, -1.0, 1.0).astype(np.float32)
    cosT = np.cos(ang)[:, fi].T
    sinS = (np.sin(ang)[:, fi] * sign[None, :]).T
    return np.ascontiguousarray(np.concatenate([cosT, sinS], axis=1).astype(ml_dtypes.bfloat16))


def rt_tables():
    ang = axial_angles(SEQ, 256)
    rope = np.concatenate([np.cos(ang).T, np.sin(ang).T], axis=1).astype(ml_dtypes.bfloat16)
    p = np.arange(128, dtype=np.float32)[:, None]
    t = np.arange(512, dtype=np.float32)[None, :]
    io_f = np.broadcast_to(t, (128, 512))
    io_b = np.broadcast_to(511.0 - t, (128, 512))
    offs = 128.0 * np.arange(4, dtype=np.float32)[None, :] + p
    E_f = 128.0 * np.arange(16, dtype=np.float32)[None, :] - p
    E_b = 128.0 * np.arange(18, dtype=np.float32)[None, :] - 511.0 + p
    cst = np.concatenate([io_f, io_b, offs, E_f, E_b], axis=1).astype(np.float32)
    return np.ascontiguousarray(rope), np.ascontiguousarray(cst)


_HY_CACHE = {}


def hy_tables():
    if _HY_CACHE:
        return _HY_CACHE
    f32 = np.float32
    zs = []
    tns = []
    for n in (SEQ, CTX):
        t = np.linspace(0.0, 1.0, n, dtype=f32)[:, None]
        fr = np.linspace(1e-4, 15.0, 16, dtype=f32)[None, :]
        w = (f32(2.0 * math.pi) * np.arange(n, dtype=f32)[:, None] / f32(n)).astype(f32)
        z = np.concatenate([t, np.cos(fr * w), -np.sin(fr * w)], axis=-1).astype(f32)
        zs.append(z.T)
        tns.append((-t[:, 0]).reshape(n // 128, 128).T)
    _HY_CACHE['hy_zT'] = np.ascontiguousarray(np.concatenate(zs, axis=1))
    _HY_CACHE['hy_tn'] = np.ascontiguousarray(np.concatenate(tns, axis=1).astype(f32))
    max_decay = math.log(1e-2) / 0.3
    min_decay = math.log(1e-2) / 1.5
    deltas = np.abs(np.linspace(min_decay, max_decay, D, dtype=f32))
    _HY_CACHE['hy_delta'] = np.ascontiguousarray(np.broadcast_to(deltas[None, :], (128, D)).astype(f32))
    _HY_CACHE['ident'] = np.eye(128, dtype=f32).astype(ml_dtypes.bfloat16)
    bf = ml_dtypes.bfloat16

    def cs(n):
        N = 2 * n
        t = np.arange(n, dtype=np.float64)[:, None]
        kk = np.arange(n, dtype=np.float64)[None, :] + 0.5
        ang = 2.0 * np.pi * t * kk / N
        return np.cos(ang), np.sin(ang)

    C, S = cs(SEQ)
    CS = np.stack([C, S], 0)
    a = CS.reshape(2, 16, 128, 16, 128)
    _HY_CACHE['dft_f'] = np.ascontiguousarray(a.transpose(3, 2, 0, 1, 4).reshape(16, 128, 4096).astype(bf))
    a = CS.reshape(2, 4, 512, 4, 4, 128)
    _HY_CACHE['dft_i'] = np.ascontiguousarray(a.transpose(1, 3, 5, 0, 4, 2).reshape(16, 128, 4096).astype(bf))
    C, S = cs(CTX)
    CS = np.stack([C, S], 0)
    a = CS.reshape(2, 2, 128, 2, 128)
    _HY_CACHE['dft_fc'] = np.ascontiguousarray(a.transpose(3, 2, 0, 1, 4).reshape(2, 128, 512).astype(bf))
    a = CS.reshape(2, 256, 2, 128)
    _HY_CACHE['dft_ic'] = np.ascontiguousarray(a.transpose(3, 0, 2, 1).reshape(128, 1024).astype(bf))
    return _HY_CACHE


def make_in_maps(inputs, n_cores=8):
    f = lambda a: np.ascontiguousarray(np.asarray(a, np.float32))
    shared = {
        'ada_w': f(inputs['ada_w']),
        'ada_b': pvec(np.asarray(inputs['ada_b']).reshape(DEPTH, 9, D).reshape(DEPTH, 9 * D).reshape(DEPTH, 72, 128).reshape(DEPTH, 72 * 128)) if False else None,
    }
    ada_b = np.asarray(inputs['ada_b'], np.float32).reshape(DEPTH, 72, 128)
    shared['ada_b'] = np.ascontiguousarray(ada_b.transpose(2, 0, 1).reshape(128, DEPTH * 72))
    shared['ln_g'] = pvec(inputs['ln_g'])
    shared['ln_b'] = pvec(inputs['ln_b'])
    for nm in ('ffa_wi', 'ffa_wo', 'ffb_wi', 'ffb_wo'):
        shared[nm] = f(inputs[nm])
    wqkv = np.asarray(inputs['da_w_qkv'][0], np.float32)
    shared['da_w_qkv'] = f(wqkv)
    swp = np.arange(D).reshape(D // 64, 2, 32)[:, ::-1, :].reshape(D)
    shared['da_wq_sw'] = f(wqkv[:, 0:D][:, swp])
    shared['da_wk_sw'] = f(wqkv[:, D:2 * D][:, swp])
    shared['da_w_o'] = f(inputs['da_w_o'][0])
    shared['da_lamT'] = f(np.asarray(inputs['da_lambda'][0], np.float32).T)
    shared['da_subg'] = f(np.asarray(inputs['da_subln_g'][0], np.float32).reshape(128, 1))
    shared['rope_da'] = rope_table_da()
    shared['rt_w_in'] = f(inputs['rt_w_in'][0])
    shared['rt_w_o'] = f(inputs['rt_w_o'][0])
    shared['rt_decay'] = np.ascontiguousarray(np.tile(np.asarray(inputs['rt_decay_logit'][0], np.float32).reshape(1, 8), (128, 1)))
    shared['rt_gng'] = pvec(inputs['rt_gn_g'][0])
    shared['rope_rt'], shared['rt_const'] = rt_tables()
    shared['hy_w_in'] = f(inputs['hy_w_in'][0])
    shared['hy_w_o'] = f(inputs['hy_w_o'][0])
    shared['hy_cw'] = pvec(inputs['hy_conv_w'][0])
    shared['hy_cb'] = pvec(inputs['hy_conv_b'][0])
    shared['hy_fw1'] = f(inputs['hy_fw1'][0])
    shared['hy_fw2'] = f(inputs['hy_fw2'][0])
    shared['hy_vec64'] = f(np.stack([np.asarray(inputs[n_][0], np.float32) for n_ in ('hy_fb1', 'hy_ff1', 'hy_fb2', 'hy_ff2')], 1))
    shared['hy_fw3'] = f(inputs['hy_fw3'][0])
    shared['hy_dskip'] = f(np.asarray(inputs['hy_d_skip'][0], np.float32).reshape(1, D))
    shared.update(hy_tables())
    shared['sc_w_in'] = f(inputs['sc_w_in'][0])
    shared['sc_conv_w'] = pvec(inputs['sc_conv_w'][0])
    shared['sc_w_o'] = f(inputs['sc_w_o'][0])
    maps = []
    for b in range(n_cores):
        m = dict(shared)
        m['xT'] = np.ascontiguousarray(np.asarray(inputs['x'][b], np.float32).T)
        m['ctxT'] = np.ascontiguousarray(np.asarray(inputs['ctx'][b], np.float32).T)
        cv = np.stack([np.asarray(inputs['c'][b], np.float32), np.asarray(inputs['c_ctx'], np.float32)], 0)
        m['cvec'] = pvec(cv)
        maps.append(m)
    return maps


def kernel(**inputs):
    nc, k = build_program()
    maps = make_in_maps(inputs, 8)
    res = run_bass_kernel_spmd(nc, maps, core_ids=list(range(8)))
    out = np.stack([np.ascontiguousarray(r['outT'].T) for r in res.results], 0)
    return out.astype(np.float32)
```

```python
import math
import os
from contextlib import ExitStack

import numpy as np
import ml_dtypes
import concourse.bass as bass
import concourse.mybir as mybir
from concourse.bass_utils import run_bass_kernel_spmd

F32 = mybir.dt.float32
BF16 = mybir.dt.bfloat16
AF = mybir.ActivationFunctionType
ALU = mybir.AluOpType

D = 1024
NCH = 8
SEQ = 2048
CTX = 256
TT = SEQ + CTX
DEPTH = 4
DFF = 2816
NF = 22
ALPHA = (2.0 * DEPTH) ** 0.25
EPS = 1e-5
ENGS = ['tensor', 'vector', 'scalar', 'gpsimd', 'sync']


class Op:
    __slots__ = ('eng', 'fn', 'deps', 'signals', 'dma', 'semkey', 'semval')

    def __init__(self, eng, fn, dma, semkey):
        self.eng = eng
        self.fn = fn
        self.deps = []
        self.signals = False
        self.dma = dma
        self.semkey = semkey
        self.semval = None


class Prog:
    def __init__(self, nc):
        self.nc = nc
        self.ops = {e: [] for e in ENGS}
        self.last_w = {}
        self.readers = {}
        self.nops = 0
        self.last_op = {}
        self.scoped_dma = []

    def op(self, eng, fn, reads=(), writes=(), dma=False, semkey=None, scoped=True):
        if dma and semkey is None:
            semkey = writes[0]
        o = Op(eng, fn, dma, semkey)
        deps = set()
        for k in reads:
            w = self.last_w.get(k)
            if w is not None:
                deps.add(w)
        for k in writes:
            w = self.last_w.get(k)
            if w is not None:
                deps.add(w)
            for r in self.readers.get(k, ()):
                deps.add(r)
        for d in deps:
            if (not d.dma) and d.eng == 'tensor' and eng == 'tensor' and not dma:
                continue
            d.signals = True
            o.deps.append(d)
        for k in writes:
            self.last_w[k] = o
            self.readers[k] = []
        for k in reads:
            self.readers.setdefault(k, []).append(o)
        self.ops[eng].append(o)
        self.nops += 1
        if dma:
            if scoped:
                self.scoped_dma.append(o)
        else:
            self.last_op[eng] = o
        return o

    def barrier(self, engs=('tensor', 'vector', 'scalar', 'sync')):
        targets = [o for o in self.last_op.values()] + list(self.scoped_dma)
        self.scoped_dma = []
        for e in engs:
            o = Op(e, None, False, None)
            for d in targets:
                d.signals = True
                o.deps.append(d)
            self.ops[e].append(o)

    def emit(self, final_waits=()):
        nc = self.nc
        eng_sem = {}
        dma_sem = {}
        dma_cnt = {}
        with ExitStack() as es:
            for e in ENGS:
                cnt = 0
                for o in self.ops[e]:
                    if o.dma:
                        if o.semkey not in dma_sem:
                            dma_sem[o.semkey] = es.enter_context(nc.semaphore('d%d' % len(dma_sem)))
                            dma_cnt[o.semkey] = 0
                        dma_cnt[o.semkey] += 16
                        o.semval = dma_cnt[o.semkey]
                    elif o.signals:
                        cnt += 1
                        o.semval = cnt
                eng_sem[e] = es.enter_context(nc.semaphore('e_' + e))
            self.n_dma_sems = len(dma_sem)
            block = es.enter_context(nc.Block())
            fw = {}
            for (e, o) in final_waits:
                fw.setdefault(e, []).append((dma_sem[o.semkey], o.semval))

            def run(e, engobj):
                waited = {}
                for o in self.ops[e]:
                    need = {}
                    for d in o.deps:
                        if d.dma:
                            key = ('d', d.semkey)
                            sem = dma_sem[d.semkey]
                        else:
                            key = ('e', d.eng)
                            sem = eng_sem[d.eng]
                        if waited.get(key, 0) >= d.semval:
                            continue
                        if key not in need or need[key][1] < d.semval:
                            need[key] = (sem, d.semval)
                    for key, (sem, val) in need.items():
                        engobj.wait_ge(sem, val)
                        waited[key] = val
                    if o.fn is None:
                        continue
                    ins = o.fn(engobj)
                    if o.dma:
                        ins.then_inc(dma_sem[o.semkey], 16)
                    elif o.signals:
                        ins.then_inc(eng_sem[e], 1)
                for (sem, val) in fw.get(e, ()):
                    engobj.wait_ge(sem, val)

            block.tensor(lambda eng: run('tensor', eng))
            block.vector(lambda eng: run('vector', eng))
            block.scalar(lambda eng: run('scalar', eng))
            block.gpsimd(lambda eng: run('gpsimd', eng))
            block.sync(lambda eng: run('sync', eng))


class Blk:
    def __init__(self, stream, t0, n):
        self.s = stream
        self.t0 = t0
        self.n = n
        self.g0 = t0 if stream == 0 else SEQ + t0
        self.key = (stream, t0)

    @property
    def sl(self):
        return slice(self.g0, self.g0 + self.n)


LAT_BLKS = [Blk(0, i * 512, 512) for i in range(4)]
CTX_BLK = Blk(1, 0, 256)


class K:
    pass


def build_program(layers=(0, 1, 2, 3), first_affine_identity=True, dbg=None):
    nc = bass.Bass("TRN2", target_bir_lowering=False)
    k = K()
    k.nc = nc
    k.P = Prog(nc)
    k.es = ExitStack()
    k.dram = {}
    k.layers = layers

    def din(name, shape, dt=F32):
        k.dram[name] = nc.dram_tensor(name, list(shape), dt, kind="ExternalInput").ap()
        return k.dram[name]

    din('xT', [D, SEQ]); din('ctxT', [D, CTX]); din('cvec', [128, 2 * NCH])
    din('ada_w', [DEPTH, D, 9 * D]); din('ada_b', [128, DEPTH * 72])
    din('ln_g', [128, DEPTH * 3 * NCH]); din('ln_b', [128, DEPTH * 3 * NCH])
    din('ffa_wi', [DEPTH, D, 2 * DFF]); din('ffa_wo', [DEPTH, DFF, D])
    din('ffb_wi', [DEPTH, D, 2 * DFF]); din('ffb_wo', [DEPTH, DFF, D])
    din('da_w_qkv', [D, 3 * D]); din('da_wq_sw', [D, D]); din('da_wk_sw', [D, D]); din('da_w_o', [D, D])
    din('da_lamT', [64, 4]); din('da_subg', [128, 1]); din('rope_da', [128, 2 * SEQ], BF16)
    din('rt_w_in', [D, 6 * D]); din('rt_w_o', [2 * D, D]); din('rt_decay', [128, 8]); din('rt_gng', [128, 16])
    din('rope_rt', [128, 2 * SEQ], BF16); din('rt_const', [128, 1024 + 4 + 16 + 18])
    din('hy_w_in', [D, 3 * D]); din('hy_w_o', [D, D]); din('hy_cw', [128, 72]); din('hy_cb', [128, 24])
    din('hy_fw1', [33, 64]); din('hy_fw2', [64, 64]); din('hy_vec64', [64, 4]); din('hy_fw3', [64, 2 * D])
    din('hy_dskip', [1, D]); din('hy_delta', [128, D]); din('hy_tn', [128, 18]); din('hy_zT', [33, TT])
    din('ident', [128, 128], BF16)
    din('dft_f', [16, 128, 4096], BF16); din('dft_i', [16, 128, 4096], BF16)
    din('dft_fc', [2, 128, 512], BF16); din('dft_ic', [128, 1024], BF16)
    din('sc_w_in', [D, 3 * D]); din('sc_conv_w', [128, 3 * NCH]); din('sc_w_o', [D, D])
    k.out = nc.dram_tensor('outT', [D, SEQ], F32, kind="ExternalOutput").ap()
    if dbg:
        k.dbgc = nc.dram_tensor('dbgc', [D, CTX], F32, kind="ExternalOutput").ap()

    with k.es:
        es = k.es
        P = k.P

        k.sbcnt = 0

        def sb(name, shape, dt=F32, stack=None):
            k.sbcnt += 1
            return (stack or es).enter_context(nc.sbuf_tensor('%s_%d' % (name, k.sbcnt), list(shape), dt))

        k.sb = sb
        k.nbuf = sb('nbuf', [128, NCH, TT], F32)
        k.NS = 6
        k.WSZ = 2048
        k.wslots = [sb('ws%d' % i, [128, k.WSZ], BF16) for i in range(k.NS)]
        k.ws_next = 0
        k.mod = sb('mod', [128, DEPTH * 2 * 72], F32)
        k.adab = sb('adab', [128, DEPTH * 72], F32)
        k.lng = sb('lng', [128, DEPTH * 3 * NCH], F32)
        k.lnb = sb('lnb', [128, DEPTH * 3 * NCH], F32)
        k.cv = sb('cv', [128, 2 * NCH], F32)
        k.scb = sb('scb', [128, NCH, 2], BF16)
        k.sg = sb('sg', [128, 2 * NCH], F32)
        k.bg = []
        k.bg_loaded = None
        k.ln_pending = []
        k.coef = sb('coef', [128, DEPTH * 3 * 2 * 5 * NCH], F32)
        k.ones = sb('ones', [128, 128], BF16)
        k.one1 = sb('one1', [128, 128], BF16)
        k.epsc = sb('epsc', [128, 1], F32)
        k.scw = sb('scw', [128, 3 * NCH], F32)
        k.ps = [es.enter_context(nc.psum_tensor('ps%d' % i, [128, 512], F32)) for i in range(8)]
        k.ps_rr = 0

        P.op('vector', lambda e: e.memset(k.ones[:], 1.0 / 1024.0), writes=['ones'])
        P.op('vector', lambda e: e.memset(k.one1[:], 1.0), writes=['one1'])
        P.op('vector', lambda e: e.memset(k.epsc[:], EPS), writes=['epsc'])

        def small_load(dst, src, key):
            P.op('sync', lambda e: e.dma_start(out=dst[:], in_=src), writes=[key], dma=True)

        small_load(k.adab, k.dram['ada_b'], 'adab')
        small_load(k.lng, k.dram['ln_g'], 'lng')
        small_load(k.lnb, k.dram['ln_b'], 'lnb')
        small_load(k.cv, k.dram['cvec'], 'cv')
        small_load(k.scw, k.dram['sc_conv_w'], 'scw')
        for c in range(NCH):
            P.op('sync', (lambda c: lambda e: e.dma_start(out=k.nbuf[:, c, 0:SEQ], in_=k.dram['xT'][c * 128:(c + 1) * 128, :]))(c),
                 writes=[('n', c, b.key) for b in LAT_BLKS], dma=True, semkey=('nload', c))
        P.op('sync', lambda e: e.dma_start(out=k.nbuf[:, :, SEQ:TT], in_=k.dram['ctxT'].rearrange("(c p) t -> p c t", p=128)),
             writes=[('n', c, CTX_BLK.key) for c in range(NCH)], dma=True, semkey=('nload', 'c'))

        compute_mods(k)
        for lidx, li in enumerate(layers):
            use_ctx = li <= 2
            ctx_next = li < 2
            compute_coefs(k, li, first_affine_identity and li == layers[0])
            ffn(k, li, 0, use_ctx)
            P.barrier()
            if lidx >= 1 and lidx + 1 < len(layers) and li in (1, 2):
                k.bg = [(layers[lidx + 1], p) for p in range(36)]
            if li == 3:
                shortconv(k, li)
            else:
                MIXERS[li](k, li, use_ctx, ctx_next)
            bg_flush(k)
            P.barrier()
            ffn(k, li, 2, ctx_next, defer_tail=(lidx + 1 < len(layers)))
            P.barrier()
        final_out(k, layers[-1], dbg)
        P.emit(final_waits=k.final_waits)
    k.nops = P.nops
    return nc, k


def next_ps(k, n=1):
    r = []
    for _ in range(n):
        r.append(k.ps_rr)
        k.ps_rr = (k.ps_rr + 1) % 8
    return r if n > 1 else r[0]


def load_w(k, src_ap, nk, ncols):
    assert nk * ncols <= k.WSZ
    s = k.ws_next
    k.ws_next = (k.ws_next + 1) % k.NS
    dst = k.wslots[s][:, 0:nk * ncols].rearrange("p (a b) -> p a b", b=ncols)
    k.P.op('gpsimd', lambda e: e.dma_start(out=dst, in_=src_ap.rearrange("(a p) n -> p a n", p=128)),
           writes=[('ws', s)], dma=True, scoped=False)
    return s, dst


def coef_ap(k, li, sub, stream, which, c):
    idx = ((((li * 3 + sub) * 2 + stream) * 5 + which) * NCH) + c
    return k.coef[:, idx:idx + 1]


def mod_ap(k, li, stream, j, c=None):
    base = (li * 2 + stream) * 72 + j * NCH
    if c is None:
        return k.mod[:, base:base + NCH]
    return k.mod[:, base + c:base + c + 1]


def ada_load(k, li, piece):
    return load_w(k, k.dram['ada_w'][li, :, piece * 256:(piece + 1) * 256], NCH, 256)


def ada_compute(k, li, piece, pb, handle):
    P = k.P
    s, wv = handle
    for q in range(2):
        for kc in range(NCH):
            P.op('tensor', (lambda wv, q, kc, pb: lambda e: e.matmul(
                k.ps[pb][:, 2 * q:2 * q + 2], lhsT=wv[:, kc, q * 128:(q + 1) * 128], rhs=k.scb[:, kc, :],
                start=(kc == 0), stop=(kc == NCH - 1)))(wv, q, kc, pb),
                reads=[('ws', s), 'scb'], writes=[('ps', pb)])
    j0 = 2 * piece
    for s_ in range(2):
        base = (li * 2 + s_) * 72 + j0
        P.op('vector', (lambda pb, s_, base, li, j0: lambda e: e.tensor_tensor(
            out=k.mod[:, base:base + 2], in0=k.ps[pb][:, s_:4:2], in1=k.adab[:, li * 72 + j0:li * 72 + j0 + 2], op=ALU.add))(pb, s_, base, li, j0),
            reads=[('ps', pb), 'adab'], writes=['mod'])


def bg_drain(k, pb):
    if k.bg_loaded is not None:
        li, piece, handle = k.bg_loaded
        k.bg_loaded = None
        k.ln_pending = []
        ada_compute(k, li, piece, pb, handle)


def bg_step(k, pb):
    bg_drain(k, pb)
    if k.bg:
        li, piece = k.bg.pop(0)
        k.bg_loaded = (li, piece, ada_load(k, li, piece))


def bg_flush(k):
    while k.bg or k.bg_loaded is not None:
        bg_step(k, next_ps(k))


def compute_mods(k):
    P = k.P
    P.op('scalar', lambda e: e.activation(out=k.sg[:], in_=k.cv[:], func=AF.Silu), reads=['cv'], writes=['sg'])
    P.op('vector', lambda e: e.tensor_copy(out=k.scb[:, :, 0], in_=k.sg[:, 0:NCH]), reads=['sg'], writes=['scb'])
    P.op('vector', lambda e: e.tensor_copy(out=k.scb[:, :, 1], in_=k.sg[:, NCH:2 * NCH]), reads=['sg'], writes=['scb'])
    k.bg = [(li, p) for li in k.layers[0:2] for p in range(36)]
    bg_flush(k)
    P.barrier()


def compute_coefs(k, li, identity_first):
    P = k.P
    for sub in range(3):
        for s_ in range(2):
            shift = mod_ap(k, li, s_, 3 * sub + 0)
            scale = mod_ap(k, li, s_, 3 * sub + 1)
            gate = mod_ap(k, li, s_, 3 * sub + 2)
            i0 = (((li * 3 + sub) * 2 + s_) * 5) * NCH
            Ah = k.coef[:, i0:i0 + NCH]
            Bh = k.coef[:, i0 + NCH:i0 + 2 * NCH]
            Ar = k.coef[:, i0 + 2 * NCH:i0 + 3 * NCH]
            Br = k.coef[:, i0 + 3 * NCH:i0 + 4 * NCH]
            G = k.coef[:, i0 + 4 * NCH:i0 + 5 * NCH]
            ident = identity_first and sub == 0
            if not ident:
                pl, psub = (li - 1, 2) if sub == 0 else (li, sub - 1)
                gp = k.lng[:, (pl * 3 + psub) * NCH:(pl * 3 + psub + 1) * NCH]
                bp = k.lnb[:, (pl * 3 + psub) * NCH:(pl * 3 + psub + 1) * NCH]
            rk = ['mod', 'lng', 'lnb', 'coef']
            if ident:
                P.op('vector', (lambda Ah, scale: lambda e: e.tensor_scalar(out=Ah, in0=scale, scalar1=1.0, scalar2=None, op0=ALU.add))(Ah, scale), reads=rk, writes=['coef'])
                P.op('vector', (lambda Bh, shift: lambda e: e.tensor_copy(out=Bh, in_=shift))(Bh, shift), reads=rk, writes=['coef'])
                P.op('vector', (lambda Ar: lambda e: e.memset(Ar, ALPHA))(Ar), reads=rk, writes=['coef'])
                P.op('vector', (lambda Br: lambda e: e.memset(Br, 0.0))(Br), reads=rk, writes=['coef'])
            else:
                P.op('vector', (lambda Ah, scale, gp: lambda e: e.scalar_tensor_tensor(out=Ah, in0=scale, scalar=1.0, in1=gp, op0=ALU.add, op1=ALU.mult))(Ah, scale, gp), reads=rk, writes=['coef'])
                P.op('vector', (lambda Bh, scale, bp: lambda e: e.scalar_tensor_tensor(out=Bh, in0=scale, scalar=1.0, in1=bp, op0=ALU.add, op1=ALU.mult))(Bh, scale, bp), reads=rk, writes=['coef'])
                P.op('vector', (lambda Bh, shift: lambda e: e.tensor_tensor(out=Bh, in0=Bh, in1=shift, op=ALU.add))(Bh, shift), reads=rk, writes=['coef'])
                P.op('vector', (lambda Ar, gp: lambda e: e.tensor_scalar(out=Ar, in0=gp, scalar1=ALPHA, scalar2=None, op0=ALU.mult))(Ar, gp), reads=rk, writes=['coef'])
                P.op('vector', (lambda Br, bp: lambda e: e.tensor_scalar(out=Br, in0=bp, scalar1=ALPHA, scalar2=None, op0=ALU.mult))(Br, bp), reads=rk, writes=['coef'])
            gsc = 1.0 if sub == 1 else 0.5
            P.op('vector', (lambda G, gate, gsc: lambda e: e.tensor_scalar(out=G, in0=gate, scalar1=gsc, scalar2=None, op0=ALU.mult))(G, gate, gsc), reads=rk, writes=['coef'])


def make_h(k, li, sub, blks, hbuf, hcol0, eng='scalar', hkey=None):
    P = k.P
    for b in blks:
        for c in range(NCH):
            wkey = ('h', c, b.key) if hkey is None else ('h', c, hkey(b))
            Ah = coef_ap(k, li, sub, b.s, 0, c)
            Bh = coef_ap(k, li, sub, b.s, 1, c)
            dst = hbuf[:, c, b.g0 - hcol0:b.g0 - hcol0 + b.n]
            src = k.nbuf[:, c, b.sl]
            P.op('scalar', (lambda dst, src, Ah, Bh: lambda e: e.activation(out=dst, in_=src, func=AF.Identity, bias=Bh, scale=Ah))(dst, src, Ah, Bh),
                 reads=[('n', c, b.key), 'coef'], writes=[wkey])


def make_xa(k, li, sub, blks, eng='gpsimd'):
    P = k.P
    for b in blks:
        for c in range(NCH):
            Ar = coef_ap(k, li, sub, b.s, 2, c)
            Br = coef_ap(k, li, sub, b.s, 3, c)
            v = k.nbuf[:, c, b.sl]
            P.op('scalar', (lambda v, Ar, Br: lambda e: e.activation(out=v, in_=v, func=AF.Identity, bias=Br, scale=Ar))(v, Ar, Br),
                 reads=[('n', c, b.key), 'coef'], writes=[('n', c, b.key)])


def layer_norm_steps(k, b, lnt):
    P = k.P
    n = b.n
    rb, sq, mean, rstd, nmr, tmp = lnt
    steps = []

    def s_cs(c):
        src = k.nbuf[:, c, b.sl]
        P.op('scalar', lambda e: e.activation(out=rb[:, c, 0:n], in_=src, func=AF.Copy), reads=[('n', c, b.key)], writes=[('rb', c)])
        P.op('scalar', lambda e: e.activation(out=sq[:, c, 0:n], in_=src, func=AF.Square), reads=[('n', c, b.key)], writes=[('sq', c)])
    for c in range(NCH):
        steps.append((lambda c: lambda: s_cs(c))(c))
    pp = {}

    def s_mm(which):
        pb = next_ps(k)
        pp[which] = pb
        src_, key_ = (rb, 'rb') if which == 0 else (sq, 'sq')
        for c in range(NCH):
            P.op('tensor', (lambda c: lambda e: e.matmul(k.ps[pb][:, 0:n], lhsT=k.ones[:], rhs=src_[:, c, 0:n], start=(c == 0), stop=(c == NCH - 1)))(c),
                 reads=[(key_, c), 'ones'], writes=[('ps', pb)])
    steps.append(lambda: s_mm(0))
    steps.append(lambda: s_mm(1))

    def s_small():
        p1, p2 = pp[0], pp[1]
        P.op('scalar', lambda e: e.activation(out=mean[:, 0:n], in_=k.ps[p1][:, 0:n], func=AF.Copy), reads=[('ps', p1)], writes=['ln_mean'])
        P.op('vector', lambda e: e.tensor_tensor(out=tmp[:, 0:n], in0=mean[:, 0:n], in1=mean[:, 0:n], op=ALU.mult), reads=['ln_mean'], writes=['ln_tmp'])
        P.op('vector', lambda e: e.tensor_tensor(out=tmp[:, 0:n], in0=k.ps[p2][:, 0:n], in1=tmp[:, 0:n], op=ALU.subtract), reads=[('ps', p2), 'ln_tmp'], writes=['ln_tmp'])
        P.op('scalar', lambda e: e.activation(out=tmp[:, 0:n], in_=tmp[:, 0:n], func=AF.Sqrt, bias=k.epsc[:, 0:1], scale=1.0), reads=['ln_tmp', 'epsc'], writes=['ln_tmp'])
        P.op('vector', lambda e: e.reciprocal(out=rstd[:, 0:n], in_=tmp[:, 0:n]), reads=['ln_tmp'], writes=['ln_rstd'])
        P.op('vector', lambda e: e.scalar_tensor_tensor(out=nmr[:, 0:n], in0=mean[:, 0:n], scalar=-1.0, in1=rstd[:, 0:n], op0=ALU.mult, op1=ALU.mult),
             reads=['ln_mean', 'ln_rstd'], writes=['ln_nmr'])
    steps.append(s_small)

    def s_norm(c):
        v = k.nbuf[:, c, b.sl]
        P.op('vector', lambda e: e.tensor_tensor(out=v, in0=v, in1=rstd[:, 0:n], op=ALU.mult), reads=[('n', c, b.key), 'ln_rstd'], writes=[('n', c, b.key)])
        P.op('vector', lambda e: e.tensor_tensor(out=v, in0=v, in1=nmr[:, 0:n], op=ALU.add), reads=[('n', c, b.key), 'ln_nmr'], writes=[('n', c, b.key)])
    for c in range(NCH):
        steps.append((lambda c: lambda: s_norm(c))(c))
    return steps


def layer_norm_blk(k, b, lnt):
    for st_ in layer_norm_steps(k, b, lnt):
        st_()


def alloc_ln_tmps(k, st):
    rb = k.sb('ln_rb', [128, NCH, 512], BF16, st)
    sq = k.sb('ln_sq', [128, NCH, 512], BF16, st)
    mean = k.sb('ln_mean', [128, 512], F32, st)
    rstd = k.sb('ln_rstd', [128, 512], F32, st)
    nmr = k.sb('ln_nmr', [128, 512], F32, st)
    tmp = k.sb('ln_tmp', [128, 512], F32, st)
    return (rb, sq, mean, rstd, nmr, tmp)


def out_proj_residual(k, li, sub, blks, w_dram, nkc, rhs_fn, rhs_keys_fn, lnt, do_ln=True):
    P = k.P
    nb = len(blks)
    dper = max(1, min(NCH, 6 // nb))
    d0 = 0
    while d0 < NCH:
        dn = min(dper, NCH - d0)
        banks = {(dc, bi): next_ps(k) for dc in range(dn) for bi in range(nb)}
        kc0 = 0
        while kc0 < nkc:
            kn = min(k.WSZ // (dn * 128), nkc - kc0)
            s, wv = load_w(k, w_dram[kc0 * 128:(kc0 + kn) * 128, d0 * 128:(d0 + dn) * 128], kn, dn * 128)
            for kk in range(kn):
                kc = kc0 + kk
                for dc in range(dn):
                    for bi, b in enumerate(blks):
                        pb = banks[(dc, bi)]
                        rhs = rhs_fn(kc, b)
                        P.op('tensor', (lambda pb, b, wv, kk, dc, kc, rhs: lambda e: e.matmul(
                            k.ps[pb][:, 0:b.n], lhsT=wv[:, kk, dc * 128:(dc + 1) * 128], rhs=rhs,
                            start=(kc == 0), stop=(kc == nkc - 1)))(pb, b, wv, kk, dc, kc, rhs),
                            reads=[('ws', s)] + rhs_keys_fn(kc, b), writes=[('ps', pb)])
            kc0 += kn
        for dc in range(dn):
            c = d0 + dc
            for bi, b in enumerate(blks):
                pb = banks[(dc, bi)]
                G = coef_ap(k, li, sub, b.s, 4, c)
                v = k.nbuf[:, c, b.sl]
                P.op('vector', (lambda pb, b, G, v: lambda e: e.scalar_tensor_tensor(
                    out=v, in0=k.ps[pb][:, 0:b.n], scalar=G, in1=v, op0=ALU.mult, op1=ALU.add))(pb, b, G, v),
                    reads=[('ps', pb), ('n', c, b.key), 'coef'], writes=[('n', c, b.key)])
        d0 += dn
    if do_ln:
        for b in blks:
            layer_norm_blk(k, b, lnt)


def ffn(k, li, sub, with_ctx, defer_tail=False):
    P = k.P
    wi = k.dram['ffa_wi' if sub == 0 else 'ffb_wi'][li]
    wo = k.dram['ffa_wo' if sub == 0 else 'ffb_wo'][li]
    groups = [[LAT_BLKS[0], LAT_BLKS[1]], [LAT_BLKS[2], LAT_BLKS[3]]]
    gw = 1024
    if with_ctx:
        groups[1] = groups[1] + [CTX_BLK]
        gw = 1280
    with ExitStack() as st:
        hb = k.sb('ffn_h', [128, NCH, gw], BF16, st)
        gb = k.sb('ffn_g', [128, NF, gw], BF16, st)
        stmp = [k.sb('ffn_s%d' % i, [128, 512], F32, st) for i in range(2)]
        lnt = alloc_ln_tmps(k, st)
        si = 0
        pending = []

        def hloc(gcol0):
            return lambda b: ('loc', (b.g0 - gcol0) // 512)
        make_h(k, li, sub, groups[0], hb, groups[0][0].g0, hkey=hloc(groups[0][0].g0))
        make_xa(k, li, sub, groups[0])
        for b in k.ln_pending:
            pending.extend(layer_norm_steps(k, b, lnt))
        k.ln_pending = []
        for gi, grp in enumerate(groups):
            gcol0 = grp[0].g0
            hk = hloc(gcol0)
            for f0 in range(0, NF, 2):
                fn_ = min(2, NF - f0)
                sa, wa = load_w(k, wi[:, f0 * 128:(f0 + fn_) * 128], NCH, fn_ * 128)
                su, wu = load_w(k, wi[:, DFF + f0 * 128:DFF + (f0 + fn_) * 128], NCH, fn_ * 128)
                for ff in range(fn_):
                    f = f0 + ff
                    for b in grp:
                        lc = b.g0 - gcol0
                        pa, pu = next_ps(k, 2)
                        for (pb, wv, s) in ((pa, wa, sa), (pu, wu, su)):
                            for kc in range(NCH):
                                P.op('tensor', (lambda pb, wv, kc, ff, lc, b: lambda e: e.matmul(
                                    k.ps[pb][:, 0:b.n], lhsT=wv[:, kc, ff * 128:(ff + 1) * 128], rhs=hb[:, kc, lc:lc + b.n],
                                    start=(kc == 0), stop=(kc == NCH - 1)))(pb, wv, kc, ff, lc, b),
                                    reads=[('ws', s), ('h', kc, hk(b))], writes=[('ps', pb)])
                        tmp = stmp[si % 2]
                        tk = ('ffn_s', si % 2)
                        si += 1
                        P.op('scalar', (lambda tmp, pa, b: lambda e: e.activation(out=tmp[:, 0:b.n], in_=k.ps[pa][:, 0:b.n], func=AF.Silu))(tmp, pa, b),
                             reads=[('ps', pa)], writes=[tk])
                        P.op('vector', (lambda tmp, pu, b, f, lc: lambda e: e.tensor_tensor(
                            out=gb[:, f, lc:lc + b.n], in0=k.ps[pu][:, 0:b.n], in1=tmp[:, 0:b.n], op=ALU.mult))(tmp, pu, b, f, lc),
                            reads=[('ps', pu), tk], writes=[('g', f, b.key)])
                        for _ in range(2):
                            if pending:
                                pending.pop(0)()
            while pending:
                pending.pop(0)()
            if gi + 1 < len(groups):
                nxt = groups[gi + 1]
                make_h(k, li, sub, nxt, hb, nxt[0].g0, hkey=hloc(nxt[0].g0))
                make_xa(k, li, sub, nxt)
            out_proj_residual(k, li, sub, grp, wo, NF,
                              lambda kc, b: gb[:, kc, b.g0 - gcol0:b.g0 - gcol0 + b.n],
                              lambda kc, b: [('g', kc, b.key)], lnt, do_ln=False)
            if gi + 1 == len(groups) and defer_tail:
                k.ln_pending = list(grp)
            else:
                for b in grp:
                    pending.extend(layer_norm_steps(k, b, lnt))
            if gi + 1 == len(groups):
                while pending:
                    pending.pop(0)()


def shortconv(k, li):
    P = k.P
    sub = 1
    blks = LAT_BLKS
    w_in = k.dram['sc_w_in']
    with ExitStack() as st0:
      zb = k.sb('sc_z', [128, NCH, SEQ], BF16, st0)
      with ExitStack() as st:
        hb = k.sb('sc_h', [128, NCH, SEQ], BF16, st)
        vb = [k.sb('sc_v%d' % i, [128, SEQ + 2], F32, st) for i in range(2)]
        tb = [k.sb('sc_t%d' % i, [128, 512], F32, st) for i in range(2)]
        make_h(k, li, sub, blks, hb, 0)
        make_xa(k, li, sub, blks)
        for i in range(2):
            P.op('vector', (lambda i: lambda e: e.memset(vb[i][:, 0:1], 0.0))(i), writes=[('sc_v', i, 'pad')])
            P.op('vector', (lambda i: lambda e: e.memset(vb[i][:, SEQ + 1:SEQ + 2], 0.0))(i), writes=[('sc_v', i, 'pad')])
        ti = 0
        for c in range(NCH):
            vi = c % 2
            v = vb[vi]
            w3s = [load_w(k, w_in[:, j * D + c * 128:j * D + (c + 1) * 128], NCH, 128) for j in range(3)]
            for b in blks:
                pbg, pcg, pu = next_ps(k, 3)
                for j, pb in enumerate((pbg, pcg, pu)):
                    s, wv = w3s[j]
                    for kc in range(NCH):
                        P.op('tensor', (lambda pb, kc, b, wv: lambda e: e.matmul(
                            k.ps[pb][:, 0:b.n], lhsT=wv[:, kc, :], rhs=hb[:, kc, b.sl],
                            start=(kc == 0), stop=(kc == NCH - 1)))(pb, kc, b, wv),
                            reads=[('ws', s), ('h', kc, b.key)], writes=[('ps', pb)])
                t = tb[ti % 2]
                tk = ('sc_t', ti % 2)
                ti += 1
                P.op('scalar', (lambda t, pcg, b: lambda e: e.activation(out=t[:, 0:b.n], in_=k.ps[pcg][:, 0:b.n], func=AF.Copy))(t, pcg, b),
                     reads=[('ps', pcg)], writes=[tk])
                P.op('vector', (lambda t, pu, b, v: lambda e: e.tensor_tensor(out=v[:, 1 + b.t0:1 + b.t0 + b.n], in0=k.ps[pu][:, 0:b.n], in1=t[:, 0:b.n], op=ALU.mult))(t, pu, b, v),
                     reads=[('ps', pu), tk], writes=[('sc_v', vi, b.key)])
                P.op('scalar', (lambda pbg, b, c: lambda e: e.activation(out=zb[:, c, b.sl], in_=k.ps[pbg][:, 0:b.n], func=AF.Copy))(pbg, b, c),
                     reads=[('ps', pbg)], writes=[('z', c, b.key)])
            for bi, b in enumerate(blks):
                t = tb[ti % 2]
                tk = ('sc_t', ti % 2)
                ti += 1
                rk = [('sc_v', vi, bb.key) for bb in blks[max(0, bi - 1):bi + 2]] + [('sc_v', vi, 'pad'), 'scw']
                w0 = k.scw[:, 0 * NCH + c:0 * NCH + c + 1]
                w1 = k.scw[:, 1 * NCH + c:1 * NCH + c + 1]
                w2 = k.scw[:, 2 * NCH + c:2 * NCH + c + 1]
                o = 1 + b.t0
                P.op('scalar', (lambda t, v, o, b, w1: lambda e: e.activation(out=t[:, 0:b.n], in_=v[:, o:o + b.n], func=AF.Copy, scale=w1))(t, v, o, b, w1),
                     reads=rk, writes=[tk])
                P.op('vector', (lambda t, v, o, b, w0: lambda e: e.scalar_tensor_tensor(out=t[:, 0:b.n], in0=v[:, o - 1:o - 1 + b.n], scalar=w0, in1=t[:, 0:b.n], op0=ALU.mult, op1=ALU.add))(t, v, o, b, w0),
                     reads=rk + [tk], writes=[tk])
                P.op('vector', (lambda t, v, o, b, w2: lambda e: e.scalar_tensor_tensor(out=t[:, 0:b.n], in0=v[:, o + 1:o + 1 + b.n], scalar=w2, in1=t[:, 0:b.n], op0=ALU.mult, op1=ALU.add))(t, v, o, b, w2),
                     reads=rk + [tk], writes=[tk])
                P.op('vector', (lambda t, b, c: lambda e: e.tensor_tensor(out=zb[:, c, b.sl], in0=zb[:, c, b.sl], in1=t[:, 0:b.n], op=ALU.mult))(t, b, c),
                     reads=[tk, ('z', c, b.key)], writes=[('z', c, b.key)])
      P.barrier()
      with ExitStack() as st:
        lnt = alloc_ln_tmps(k, st)
        for gi_, grp in enumerate(([blks[0], blks[1]], [blks[2], blks[3]])):
            out_proj_residual(k, li, sub, grp, k.dram['sc_w_o'], NCH,
                              lambda kc, b: zb[:, kc, b.sl], lambda kc, b: [('z', kc, b.key)], lnt, do_ln=(gi_ == 0))
            if gi_ == 1:
                k.ln_pending = list(grp)


def final_out(k, li, dbg):
    P = k.P
    k.final_waits = []
    with ExitStack() as st:
        ob = [k.sb('fo%d' % i, [128, SEQ], F32, st) for i in range(2)]
        for c in range(NCH):
            o = ob[c % 2]
            g = k.lng[:, (li * 3 + 2) * NCH + c:(li * 3 + 2) * NCH + c + 1]
            bb = k.lnb[:, (li * 3 + 2) * NCH + c:(li * 3 + 2) * NCH + c + 1]
            if c % 2 == 0:
                P.op('vector', (lambda o, c, g, bb: lambda e: e.tensor_scalar(out=o[:], in0=k.nbuf[:, c, 0:SEQ], scalar1=g, scalar2=bb, op0=ALU.mult, op1=ALU.add))(o, c, g, bb),
                     reads=[('n', c, b.key) for b in LAT_BLKS] + ['lng', 'lnb'], writes=[('fo', c % 2)])
            else:
                P.op('scalar', (lambda o, c, g, bb: lambda e: e.activation(out=o[:], in_=k.nbuf[:, c, 0:SEQ], func=AF.Identity, bias=bb, scale=g))(o, c, g, bb),
                     reads=[('n', c, b.key) for b in LAT_BLKS] + ['lng', 'lnb'], writes=[('fo', c % 2)])
            d = P.op('sync', (lambda o, c: lambda e: e.dma_start(out=k.out[c * 128:(c + 1) * 128, :], in_=o[:]))(o, c),
                     reads=[('fo', c % 2)], writes=[('out', c)], dma=True)
            k.final_waits.append(('sync', d))
        if dbg:
            oc = k.sb('foc', [128, NCH, CTX], F32, st)
            for c in range(NCH):
                g = k.lng[:, (li * 3 + 2) * NCH + c:(li * 3 + 2) * NCH + c + 1]
                bb = k.lnb[:, (li * 3 + 2) * NCH + c:(li * 3 + 2) * NCH + c + 1]
                P.op('vector', (lambda c, g, bb: lambda e: e.tensor_scalar(out=oc[:, c, :], in0=k.nbuf[:, c, SEQ:TT], scalar1=g, scalar2=bb, op0=ALU.mult, op1=ALU.add))(c, g, bb),
                     reads=[('n', c, CTX_BLK.key), 'lng', 'lnb'], writes=[('foc', c)])
            d = P.op('sync', lambda e: e.dma_start(out=k.dbgc.rearrange("(c p) t -> p c t", p=128), in_=oc[:]),
                     reads=[('foc', c) for c in range(NCH)], writes=['dbgc'], dma=True)
            k.final_waits.append(('sync', d))


def diff_attn(k, li, use_ctx, ctx_next):
    P = k.P
    nc = k.nc
    sub = 1
    lam_init = 0.8 - 0.6 * math.exp(-0.3 * li)
    allb = LAT_BLKS + [CTX_BLK]
    wqkv = k.dram['da_w_qkv']
    with ExitStack() as st:
        hb = k.sb('da_h', [128, NCH, TT], BF16, st)
        rope = k.sb('da_rope', [128, 2 * SEQ], BF16, st)
        qT = k.sb('da_q', [128, 2, TT], BF16, st)
        kT = k.sb('da_k', [128, 2, TT], BF16, st)
        vtm = k.sb('da_v', [128, 18, 256], BF16, st)
        ob = k.sb('da_o', [128, 2, TT], BF16, st)
        pt = [k.sb('da_p%d' % i, [128, 2, 512], BF16, st) for i in range(3)]
        zacc = k.sb('da_zacc', [128, 512], F32, st)
        zhl = k.sb('da_zhl', [128, 2, 512], BF16, st)
        onesf = k.sb('da_onesf', [128, 128], F32, st)
        tt = [k.sb('da_t%d' % i, [128, 512], F32, st) for i in range(4)]
        sqb = k.sb('da_sq', [128, 512], BF16, st)
        lam = k.sb('da_lam', [128, 8], F32, st)
        lamT = k.sb('da_lamT', [64, 4], F32, st)
        subg = k.sb('da_subg', [128, 1], F32, st)
        onef = k.sb('da_onef', [64, 128], F32, st)
        P.op('sync', lambda e: e.dma_start(out=rope[:], in_=k.dram['rope_da']), writes=['rope'], dma=True)
        P.op('sync', lambda e: e.dma_start(out=lamT[:], in_=k.dram['da_lamT']), writes=['lamT'], dma=True)
        P.op('sync', lambda e: e.dma_start(out=subg[:], in_=k.dram['da_subg']), writes=['subg'], dma=True)
        P.op('vector', lambda e: e.memset(onef[:], 1.0), writes=['onef'])
        P.op('vector', lambda e: e.memset(onesf[:], 1.0), writes=['onesf'])
        P.op('vector', lambda e: e.tensor_tensor(out=lamT[:, 0:1], in0=lamT[:, 0:1], in1=lamT[:, 1:2], op=ALU.mult), reads=['lamT'], writes=['lamT'])
        P.op('vector', lambda e: e.tensor_tensor(out=lamT[:, 1:2], in0=lamT[:, 2:3], in1=lamT[:, 3:4], op=ALU.mult), reads=['lamT'], writes=['lamT'])
        pl = next_ps(k)
        P.op('tensor', lambda e: e.matmul(k.ps[pl][:, 0:2], lhsT=onef[:], rhs=lamT[:, 0:2], start=True, stop=True), reads=['lamT', 'onef'], writes=[('ps', pl)])
        P.op('scalar', lambda e: e.activation(out=lam[:, 0:2], in_=k.ps[pl][:, 0:2], func=AF.Exp), reads=[('ps', pl)], writes=['lam'])
        P.op('vector', lambda e: e.tensor_tensor(out=lam[:, 2:3], in0=lam[:, 1:2], in1=lam[:, 0:1], op=ALU.subtract), reads=['lam'], writes=['lam'])
        P.op('vector', lambda e: e.tensor_scalar(out=lam[:, 2:3], in0=lam[:, 2:3], scalar1=-lam_init, scalar2=None, op0=ALU.add), reads=['lam'], writes=['lam'])
        P.op('vector', lambda e: e.tensor_scalar(out=lam[:, 3:4], in0=subg[:, 0:1], scalar1=1.0 - lam_init, scalar2=None, op0=ALU.mult), reads=['subg', 'lam'], writes=['lam'])
        neg_lam = lam[:, 2:3]
        gsub = lam[:, 3:4]

        make_h(k, li, sub, allb, hb, 0)
        make_xa(k, li, sub, allb)
        cosT = rope[:, 0:SEQ]
        sinS = rope[:, SEQ:2 * SEQ]
        pi = [0]
        ti = [0]
        sr = [0]
        fin = []

        def nxt_p():
            i = pi[0] % len(pt)
            pi[0] += 1
            return pt[i], ('da_p', i)

        def nxt_t():
            i = ti[0] % len(tt)
            ti[0] += 1
            return tt[i], ('da_t', i)

        for hp in range(4):
            for (dstT, c0, wsw, dkey) in ((qT, hp * 256, k.dram['da_wq_sw'], 'q'), (kT, D + hp * 256, k.dram['da_wk_sw'], 'k')):
                s1, w1 = load_w(k, wqkv[:, c0:c0 + 256], NCH, 256)
                s2, w2 = load_w(k, wsw[:, hp * 256:(hp + 1) * 256], NCH, 256)
                for j in range(2):
                    for b in allb:
                        pa = next_ps(k)
                        for kc in range(NCH):
                            P.op('tensor', (lambda pa, w1, kc, j, b: lambda e: e.matmul(
                                k.ps[pa][:, 0:b.n], lhsT=w1[:, kc, j * 128:(j + 1) * 128], rhs=hb[:, kc, b.sl],
                                start=(kc == 0), stop=(kc == NCH - 1)))(pa, w1, kc, j, b),
                                reads=[('ws', s1), ('h', kc, b.key)], writes=[('ps', pa)])
                        if b.s == 1:
                            P.op('vector', (lambda pa, dstT, j, b: lambda e: e.tensor_copy(out=dstT[:, j, b.sl], in_=k.ps[pa][:, 0:b.n]))(pa, dstT, j, b),
                                 reads=[('ps', pa)], writes=[(dkey, j, b.key)])
                            continue
                        pbk = next_ps(k)
                        for kc in range(NCH):
                            P.op('tensor', (lambda pbk, w2, kc, j, b: lambda e: e.matmul(
                                k.ps[pbk][:, 0:b.n], lhsT=w2[:, kc, j * 128:(j + 1) * 128], rhs=hb[:, kc, b.sl],
                                start=(kc == 0), stop=(kc == NCH - 1)))(pbk, w2, kc, j, b),
                                reads=[('ws', s2), ('h', kc, b.key)], writes=[('ps', pbk)])
                        t1, k1 = nxt_t()
                        t2, k2 = nxt_t()
                        P.op('vector', (lambda t1, pa, b: lambda e: e.tensor_tensor(out=t1[:, 0:b.n], in0=k.ps[pa][:, 0:b.n], in1=cosT[:, b.sl], op=ALU.mult))(t1, pa, b),
                             reads=[('ps', pa), 'rope'], writes=[k1])
                        P.op('vector', (lambda t2, pbk, b: lambda e: e.tensor_tensor(out=t2[:, 0:b.n], in0=k.ps[pbk][:, 0:b.n], in1=sinS[:, b.sl], op=ALU.mult))(t2, pbk, b),
                             reads=[('ps', pbk), 'rope'], writes=[k2])
                        P.op('vector', (lambda t1, t2, dstT, j, b: lambda e: e.tensor_tensor(out=dstT[:, j, b.sl], in0=t1[:, 0:b.n], in1=t2[:, 0:b.n], op=ALU.add))(t1, t2, dstT, j, b),
                             reads=[k1, k2], writes=[(dkey, j, b.key)])
            s3, w3 = load_w(k, wqkv[:, 2 * D + hp * 256:2 * D + (hp + 1) * 256], NCH, 256)
            for kc18 in range(18):
                pv = next_ps(k)
                bkey = allb[kc18 // 4].key if kc18 < 16 else CTX_BLK.key
                for kc in range(NCH):
                    P.op('tensor', (lambda pv, kc, kc18, w3: lambda e: e.matmul(
                        k.ps[pv][:, 0:256], lhsT=hb[:, kc, kc18 * 128:(kc18 + 1) * 128], rhs=w3[:, kc, :],
                        start=(kc == 0), stop=(kc == NCH - 1)))(pv, kc, kc18, w3),
                        reads=[('ws', s3), ('h', kc, bkey)], writes=[('ps', pv)])
                P.op('vector', (lambda pv, kc18: lambda e: e.tensor_copy(out=vtm[:, kc18, :], in_=k.ps[pv][:, 0:256]))(pv, kc18),
                     reads=[('ps', pv)], writes=[('v', kc18)])
            qblks = allb if ctx_next else LAT_BLKS
            spairs = [(4, 5), (6, 7)]
            for j in range(2):
                for b in qblks:
                    kcs = list(range(18)) if b.s == 0 else [16, 17]
                    n = b.n
                    pend = []
                    nk = len(kcs)
                    for idx in range(nk + 1):
                        if idx < nk:
                            kc18 = kcs[idx]
                            kblk = allb[kc18 // 4].key if kc18 < 16 else CTX_BLK.key
                            pair = spairs[sr[0] % 2]
                            sr[0] += 1
                            for m in range(2):
                                pr = slice(64 * m, 64 * m + 64)
                                P.op('tensor', (lambda m, pr, kc18, j, b, sbm: lambda e: e.matmul(
                                    k.ps[sbm][:, 0:b.n], lhsT=kT[pr, j, kc18 * 128:(kc18 + 1) * 128], rhs=qT[pr, j, b.sl],
                                    start=True, stop=True))(m, pr, kc18, j, b, pair[m]),
                                    reads=[('k', j, kblk), ('q', j, b.key)], writes=[('ps', pair[m])])
                            pt_, pk = nxt_p()
                            for m in range(2):
                                P.op('scalar', (lambda pt_, pm, n, m: lambda e: e.activation(out=pt_[:, m, 0:n], in_=k.ps[pm][:, 0:n], func=AF.Exp, scale=0.125))(pt_, pair[m], n, m),
                                     reads=[('ps', pair[m])], writes=[pk])
                            if idx == 0:
                                P.op('vector', (lambda pt_, n: lambda e: e.tensor_copy(out=zacc[:, 0:n], in_=pt_[:, 0, 0:n]))(pt_, n), reads=[pk], writes=['zacc'])
                            else:
                                P.op('vector', (lambda pt_, n: lambda e: e.tensor_tensor(out=zacc[:, 0:n], in0=zacc[:, 0:n], in1=pt_[:, 0, 0:n], op=ALU.add))(pt_, n), reads=[pk, 'zacc'], writes=['zacc'])
                            pend.append((pt_, pk, kc18, idx))
                            if fin and idx >= 1:
                                fin.pop(0)()
                        if idx >= 1:
                            pt_, pk, kc18, pidx = pend.pop(0)
                            first = (pidx == 0)
                            last = (pidx == nk - 1)
                            for m in range(2):
                                P.op('tensor', (lambda pt_, kc18, j, m, n, first, last: lambda e: e.matmul(
                                    k.ps[m][:, 0:n], lhsT=vtm[:, kc18, j * 128:(j + 1) * 128], rhs=pt_[:, m, 0:n], start=first, stop=last))(pt_, kc18, j, m, n, first, last),
                                    reads=[pk, ('v', kc18)], writes=[('ps', m)])
                            P.op('tensor', (lambda pt_, n, first, last: lambda e: e.matmul(
                                k.ps[2][:, 0:n], lhsT=k.one1[:], rhs=pt_[:, 1, 0:n], start=first, stop=last))(pt_, n, first, last),
                                reads=[pk, 'one1'], writes=[('ps', 2)])
                    while fin:
                        fin.pop(0)()
                    P.op('vector', (lambda n: lambda e: e.tensor_copy(out=zhl[:, 0, 0:n], in_=zacc[:, 0:n]))(n), reads=['zacc'], writes=['zhl'])
                    P.op('vector', (lambda n: lambda e: e.tensor_tensor(out=zhl[:, 1, 0:n], in0=zacc[:, 0:n], in1=zhl[:, 0, 0:n], op=ALU.subtract))(n), reads=['zacc', 'zhl'], writes=['zhl'])
                    for hl in range(2):
                        P.op('tensor', (lambda n, hl: lambda e: e.matmul(k.ps[3][:, 0:n], lhsT=k.one1[:], rhs=zhl[:, hl, 0:n], start=(hl == 0), stop=(hl == 1)))(n, hl),
                             reads=['zhl', 'one1'], writes=[('ps', 3)])
                    while fin:
                        fin.pop(0)()
                    c0, c1, c2, c3 = tt
                    P.op('vector', (lambda n: lambda e: e.tensor_copy(out=c0[:, 0:n], in_=k.ps[0][:, 0:n]))(n), reads=[('ps', 0)], writes=[('da_t', 0)])
                    P.op('vector', (lambda n: lambda e: e.tensor_copy(out=c1[:, 0:n], in_=k.ps[1][:, 0:n]))(n), reads=[('ps', 1)], writes=[('da_t', 1)])
                    P.op('scalar', (lambda n: lambda e: e.activation(out=c2[:, 0:n], in_=k.ps[2][:, 0:n], func=AF.Ln))(n), reads=[('ps', 2)], writes=[('da_t', 2)])

                    def mk(n, j, b):
                        return [
                            lambda: P.op('scalar', lambda e: e.activation(out=c3[:, 0:n], in_=k.ps[3][:, 0:n], func=AF.Ln), reads=[('ps', 3)], writes=[('da_t', 3)]),
                            lambda: P.op('scalar', lambda e: e.activation(out=c2[:, 0:n], in_=c2[:, 0:n], func=AF.Exp, scale=-1.0), reads=[('da_t', 2)], writes=[('da_t', 2)]),
                            lambda: P.op('scalar', lambda e: e.activation(out=c3[:, 0:n], in_=c3[:, 0:n], func=AF.Exp, scale=-1.0), reads=[('da_t', 3)], writes=[('da_t', 3)]),
                            lambda: P.op('vector', lambda e: e.tensor_tensor(out=c0[:, 0:n], in0=c0[:, 0:n], in1=c3[:, 0:n], op=ALU.mult), reads=[('da_t', 0), ('da_t', 3)], writes=[('da_t', 0)]),
                            lambda: P.op('vector', lambda e: e.tensor_tensor(out=c1[:, 0:n], in0=c1[:, 0:n], in1=c2[:, 0:n], op=ALU.mult), reads=[('da_t', 1), ('da_t', 2)], writes=[('da_t', 1)]),
                            lambda: P.op('vector', lambda e: e.scalar_tensor_tensor(out=c0[:, 0:n], in0=c1[:, 0:n], scalar=neg_lam, in1=c0[:, 0:n], op0=ALU.mult, op1=ALU.add),
                                         reads=[('da_t', 0), ('da_t', 1), 'lam'], writes=[('da_t', 0)]),
                            lambda: P.op('scalar', lambda e: e.activation(out=sqb[:, 0:n], in_=c0[:, 0:n], func=AF.Square), reads=[('da_t', 0)], writes=['da_sq']),
                            lambda: P.op('tensor', lambda e: e.matmul(k.ps[3][:, 0:n], lhsT=k.one1[:], rhs=sqb[:, 0:n], start=True, stop=True), reads=['da_sq', 'one1'], writes=[('ps', 3)]),
                            lambda: P.op('scalar', lambda e: e.activation(out=c1[:, 0:n], in_=k.ps[3][:, 0:n], func=AF.Ln, bias=k.epsc[:, 0:1], scale=1.0 / 128.0),
                                         reads=[('ps', 3), 'epsc'], writes=[('da_t', 1)]),
                            lambda: P.op('scalar', lambda e: e.activation(out=c1[:, 0:n], in_=c1[:, 0:n], func=AF.Exp, scale=-0.5), reads=[('da_t', 1)], writes=[('da_t', 1)]),
                            lambda: P.op('vector', lambda e: e.scalar_tensor_tensor(out=ob[:, j, b.sl], in0=c0[:, 0:n], scalar=gsub, in1=c1[:, 0:n], op0=ALU.mult, op1=ALU.mult),
                                         reads=[('da_t', 0), ('da_t', 1), 'lam'], writes=[('o', j, b.key)]),
                        ]
                    fin.extend(mk(n, j, b))
            while fin:
                fin.pop(0)()
            wo_part = k.dram['da_w_o'][hp * 256:(hp + 1) * 256, :]
            for grp in ([LAT_BLKS[0], LAT_BLKS[1]], [LAT_BLKS[2], LAT_BLKS[3]]) + (([CTX_BLK],) if ctx_next else ()):
                out_proj_residual(k, li, sub, grp, wo_part, 2,
                                  lambda kc, b: ob[:, kc, b.sl], lambda kc, b: [('o', kc, b.key)], None, do_ln=False)
    P.barrier()
    with ExitStack() as st:
        lnt = alloc_ln_tmps(k, st)
        for b in (allb if ctx_next else LAT_BLKS):
            if b.s == 0 and b.t0 < 1024:
                layer_norm_blk(k, b, lnt)
            else:
                k.ln_pending.append(b)


def retention(k, li, use_ctx, ctx_next):
    P = k.P
    sub = 1
    allb = LAT_BLKS + [CTX_BLK]
    w_in = k.dram['rt_w_in']
    with ExitStack() as st:
        hb = k.sb('rt_h', [128, NCH, TT], BF16, st)
        cst = k.sb('rt_cst', [128, 1024 + 4 + 16 + 18], F32, st)
        qT = k.sb('rt_q', [128, 2, SEQ], BF16, st)
        kT = k.sb('rt_k', [128, 2, TT], BF16, st)
        vtm = k.sb('rt_v', [128, 18, 512], BF16, st)
        lg = k.sb('rt_lg', [128, 8], F32, st)
        gng = k.sb('rt_gng', [128, 16], F32, st)
        ksf = k.sb('rt_ksf', [128, 16], F32, st)
        ksb = k.sb('rt_ksb', [128, 18], F32, st)
        o512 = k.sb('rt_o512', [128, 128], BF16, st)
        P.op('sync', lambda e: e.dma_start(out=cst[:], in_=k.dram['rt_const']), writes=['rt_cst'], dma=True)
        P.op('sync', lambda e: e.dma_start(out=lg[:], in_=k.dram['rt_decay']), writes=['rt_lg'], dma=True)
        P.op('sync', lambda e: e.dma_start(out=gng[:], in_=k.dram['rt_gng']), writes=['rt_gng'], dma=True)
        P.op('vector', lambda e: e.memset(o512[:], 1.0 / 512.0), writes=['o512'])
        io_f = cst[:, 0:512]
        io_b = cst[:, 512:1024]
        offs = cst[:, 1024:1028]
        E_f = cst[:, 1028:1044]
        E_b = cst[:, 1044:1062]
        P.op('scalar', lambda e: e.activation(out=lg[:], in_=lg[:], func=AF.Exp, scale=-1.0), reads=['rt_lg'], writes=['rt_lg'])
        P.op('scalar', lambda e: e.activation(out=lg[:], in_=lg[:], func=AF.Ln, bias=1.0, scale=1.0), reads=['rt_lg'], writes=['rt_lg'])
        P.op('vector', lambda e: e.tensor_scalar(out=lg[:], in0=lg[:], scalar1=-1.0, scalar2=None, op0=ALU.mult), reads=['rt_lg'], writes=['rt_lg'])
        make_h(k, li, sub, allb, hb, 0)
        make_xa(k, li, sub, LAT_BLKS)
        sr = [0]

        def sbank():
            b_ = 4 + (sr[0] % 4)
            sr[0] += 1
            return b_

        for hd in range(4):
            lgf = lg[:, hd:hd + 1]
            lgb = lg[:, 4 + hd:5 + hd]
            P.op('scalar', (lambda lgf: lambda e: e.activation(out=ksf[:], in_=E_f, func=AF.Exp, scale=lgf))(lgf), reads=['rt_cst', 'rt_lg'], writes=['ksf'])
            P.op('scalar', (lambda lgb: lambda e: e.activation(out=ksb[:], in_=E_b, func=AF.Exp, scale=lgb))(lgb), reads=['rt_cst', 'rt_lg'], writes=['ksb'])
            P.op('vector', lambda e: e.tensor_scalar(out=ksf[:], in0=ksf[:], scalar1=1.0 / 16.0, scalar2=None, op0=ALU.mult), reads=['ksf'], writes=['ksf'])
            P.op('vector', lambda e: e.tensor_scalar(out=ksb[:], in0=ksb[:], scalar1=1.0 / 16.0, scalar2=None, op0=ALU.mult), reads=['ksb'], writes=['ksb'])
            with ExitStack() as s1:
                rope = k.sb('rt_rope', [128, 2 * SEQ], BF16, s1)
                tt = [k.sb('rt_t%d' % i, [128, 512], F32, s1) for i in range(4)]
                P.op('sync', lambda e: e.dma_start(out=rope[:], in_=k.dram['rope_rt']), writes=['rope'], dma=True)
                cosT = rope[:, 0:SEQ]
                sinT = rope[:, SEQ:2 * SEQ]
                for (dstT, c0, dkey, blks_) in ((qT, hd * 256, 'q', LAT_BLKS), (kT, D + hd * 256, 'k', allb)):
                    s_, w_ = load_w(k, w_in[:, c0:c0 + 256], NCH, 256)
                    for b in blks_:
                        pa, pb_ = next_ps(k, 2)
                        for j, pp in enumerate((pa, pb_)):
                            for kc in range(NCH):
                                P.op('tensor', (lambda pp, w_, kc, j, b: lambda e: e.matmul(
                                    k.ps[pp][:, 0:b.n], lhsT=w_[:, kc, j * 128:(j + 1) * 128], rhs=hb[:, kc, b.sl],
                                    start=(kc == 0), stop=(kc == NCH - 1)))(pp, w_, kc, j, b),
                                    reads=[('ws', s_), ('h', kc, b.key)], writes=[('ps', pp)])
                        if b.s == 1:
                            for j, pp in enumerate((pa, pb_)):
                                P.op('vector', (lambda pp, dstT, j, b: lambda e: e.tensor_copy(out=dstT[:, j, b.sl], in_=k.ps[pp][:, 0:b.n]))(pp, dstT, j, b),
                                     reads=[('ps', pp)], writes=[(dkey, j, b.key)])
                            continue
                        for j in range(2):
                            t1, t2 = tt[2 * j], tt[2 * j + 1]
                            k1, k2 = ('rt_t', 2 * j), ('rt_t', 2 * j + 1)
                            tabA = cosT if j == 0 else sinT
                            tabB = sinT if j == 0 else cosT
                            opf = ALU.subtract if j == 0 else ALU.add
                            P.op('vector', (lambda t1, pa, b, tabA: lambda e: e.tensor_tensor(out=t1[:, 0:b.n], in0=k.ps[pa][:, 0:b.n], in1=tabA[:, b.sl], op=ALU.mult))(t1, pa, b, tabA),
                                 reads=[('ps', pa), 'rope'], writes=[k1])
                            P.op('vector', (lambda t2, pb_, b, tabB: lambda e: e.tensor_tensor(out=t2[:, 0:b.n], in0=k.ps[pb_][:, 0:b.n], in1=tabB[:, b.sl], op=ALU.mult))(t2, pb_, b, tabB),
                                 reads=[('ps', pb_), 'rope'], writes=[k2])
                            P.op('vector', (lambda t1, t2, dstT, j, b, opf: lambda e: e.tensor_tensor(out=dstT[:, j, b.sl], in0=t1[:, 0:b.n], in1=t2[:, 0:b.n], op=opf))(t1, t2, dstT, j, b, opf),
                                 reads=[k1, k2], writes=[(dkey, j, b.key)])
                wv = [load_w(k, w_in[:, 2 * D + hd * 512 + hh * 256:2 * D + hd * 512 + (hh + 1) * 256], NCH, 256) for hh in range(2)]
                for kc18 in range(18):
                    pv = next_ps(k)
                    bkey = allb[kc18 // 4].key if kc18 < 16 else CTX_BLK.key
                    for hh in range(2):
                        s_, w_ = wv[hh]
                        for kc in range(NCH):
                            P.op('tensor', (lambda pv, kc, kc18, w_, hh: lambda e: e.matmul(
                                k.ps[pv][:, hh * 256:(hh + 1) * 256], lhsT=hb[:, kc, kc18 * 128:(kc18 + 1) * 128], rhs=w_[:, kc, :],
                                start=(kc == 0), stop=(kc == NCH - 1)))(pv, kc, kc18, w_, hh),
                                reads=[('ws', s_), ('h', kc, bkey)], writes=[('ps', pv)])
                    P.op('scalar', (lambda pv, kc18: lambda e: e.activation(out=vtm[:, kc18, :], in_=k.ps[pv][:], func=AF.Copy))(pv, kc18),
                         reads=[('ps', pv)], writes=[('v', kc18)])
            P.barrier()
            with ExitStack() as s2:
                masks = k.sb('rt_mask', [128, 4, 512], BF16, s2)
                Dq = k.sb('rt_dq', [128, 1024], F32, s2)
                pt = [k.sb('rt_p%d' % i, [128, 512], BF16, s2) for i in range(3)]
                obf = k.sb('rt_obf', [128, 4, 512], BF16, s2)
                sqf = k.sb('rt_sqf', [128, 4, 512], BF16, s2)
                rstd = k.sb('rt_rstd', [128, 512], F32, s2)
                ta = k.sb('rt_ta', [128, 512], F32, s2)
                tb_ = k.sb('rt_tb', [128, 512], F32, s2)
                tc_ = k.sb('rt_tc', [128, 512], F32, s2)
                zb = obf
                P.op('scalar', (lambda lgf: lambda e: e.activation(out=Dq[:, 0:512], in_=io_f, func=AF.Exp, scale=lgf))(lgf), reads=['rt_cst', 'rt_lg'], writes=['dq'])
                P.op('scalar', (lambda lgb: lambda e: e.activation(out=Dq[:, 512:1024], in_=io_b, func=AF.Exp, scale=lgb))(lgb), reads=['rt_cst', 'rt_lg'], writes=['dq'])
                def build_mask(v, lgf=lgf, lgb=lgb):
                    ov = offs[:, v:v + 1]
                    P.op('vector', (lambda ov: lambda e: e.tensor_scalar(out=tc_[:], in0=io_f, scalar1=ov, scalar2=None, op0=ALU.subtract))(ov), reads=['rt_cst'], writes=['tc'])
                    P.op('vector', lambda e: e.tensor_scalar(out=ta[:], in0=tc_[:], scalar1=0.0, scalar2=None, op0=ALU.max), reads=['tc'], writes=['ta'])
                    P.op('scalar', (lambda lgf: lambda e: e.activation(out=ta[:], in_=ta[:], func=AF.Exp, scale=lgf))(lgf), reads=['ta', 'rt_lg'], writes=['ta'])
                    P.op('vector', lambda e: e.tensor_scalar(out=tb_[:], in0=tc_[:], scalar1=0.0, scalar2=None, op0=ALU.is_ge), reads=['tc'], writes=['tb'])
                    P.op('vector', lambda e: e.tensor_tensor(out=ta[:], in0=ta[:], in1=tb_[:], op=ALU.mult), reads=['ta', 'tb'], writes=['ta'])
                    P.op('vector', lambda e: e.tensor_scalar(out=tb_[:], in0=tc_[:], scalar1=-1.0, scalar2=0.0, op0=ALU.mult, op1=ALU.max), reads=['tc'], writes=['tb'])
                    P.op('scalar', (lambda lgb: lambda e: e.activation(out=tb_[:], in_=tb_[:], func=AF.Exp, scale=lgb))(lgb), reads=['tb', 'rt_lg'], writes=['tb'])
                    P.op('vector', lambda e: e.tensor_scalar(out=tc_[:], in0=tc_[:], scalar1=0.0, scalar2=None, op0=ALU.is_le), reads=['tc'], writes=['tc'])
                    P.op('vector', lambda e: e.tensor_tensor(out=tb_[:], in0=tb_[:], in1=tc_[:], op=ALU.mult), reads=['tb', 'tc'], writes=['tb'])
                    P.op('vector', lambda e: e.tensor_tensor(out=ta[:], in0=ta[:], in1=tb_[:], op=ALU.add), reads=['ta', 'tb'], writes=['ta'])
                    P.op('vector', (lambda v: lambda e: e.tensor_scalar(out=masks[:, v, :], in0=ta[:], scalar1=1.0 / 16.0, scalar2=None, op0=ALU.mult))(v), reads=['ta'], writes=[('mask', v)])
                pi = [0]
                for bi, b in enumerate(LAT_BLKS):
                    tiles = []
                    for cc in range(2):
                        tiles.append((16 + cc, 'f', 4 * bi + 2 - cc))
                    for kc in range(16):
                        if kc < 4 * bi:
                            tiles.append((kc, 'f', 4 * bi - kc))
                        elif kc > 4 * bi + 3:
                            tiles.append((kc, 'b', kc - 4 * bi))
                        else:
                            tiles.append((kc, 'd', kc - 4 * bi))
                    for cc in range(2):
                        tiles.append((16 + cc, 'b', 16 + cc - 4 * bi))
                    tiles = [t_ for t_ in tiles if t_[1] != 'd'] + [t_ for t_ in tiles if t_[1] == 'd']
                    acc = [0, 1, 2, 3]
                    pend = None
                    for idx in range(len(tiles) + 1):
                        cur = None
                        if idx < len(tiles):
                            kc18, kind, par = tiles[idx]
                            kblk = allb[kc18 // 4].key if kc18 < 16 else CTX_BLK.key
                            if bi == 0 and idx in (1, 4, 8, 11):
                                build_mask((1, 4, 8, 11).index(idx))
                            if idx in (2, 7, 12):
                                bg_step(k, sbank())
                            elif idx == 17:
                                bg_drain(k, sbank())
                            sbk = sbank()
                            for c in range(2):
                                P.op('tensor', (lambda c, kc18, b, sbk: lambda e: e.matmul(
                                    k.ps[sbk][:], lhsT=kT[:, c, kc18 * 128:(kc18 + 1) * 128], rhs=qT[:, c, b.sl],
                                    start=(c == 0), stop=(c == 1)))(c, kc18, b, sbk),
                                    reads=[('k', c, kblk), ('q', c, b.key)], writes=[('ps', sbk)])
                            p_ = pt[pi[0] % 3]
                            pk = ('rt_p', pi[0] % 3)
                            pi[0] += 1
                            if kind == 'f':
                                P.op('vector', (lambda p_, sbk, par: lambda e: e.scalar_tensor_tensor(out=p_[:], in0=k.ps[sbk][:], scalar=ksf[:, par:par + 1], in1=Dq[:, 0:512], op0=ALU.mult, op1=ALU.mult))(p_, sbk, par),
                                     reads=[('ps', sbk), 'ksf', 'dq'], writes=[pk])
                            elif kind == 'b':
                                P.op('vector', (lambda p_, sbk, par: lambda e: e.scalar_tensor_tensor(out=p_[:], in0=k.ps[sbk][:], scalar=ksb[:, par:par + 1], in1=Dq[:, 512:1024], op0=ALU.mult, op1=ALU.mult))(p_, sbk, par),
                                     reads=[('ps', sbk), 'ksb', 'dq'], writes=[pk])
                            else:
                                P.op('vector', (lambda p_, sbk, par: lambda e: e.tensor_tensor(out=p_[:], in0=k.ps[sbk][:], in1=masks[:, par, :], op=ALU.mult))(p_, sbk, par),
                                     reads=[('ps', sbk), ('mask', par)], writes=[pk])
                            cur = (p_, pk, kc18, idx)
                        if pend is not None:
                            p_, pk, kc18, pidx = pend
                            for e_ in range(4):
                                P.op('tensor', (lambda p_, kc18, e_, pidx: lambda e: e.matmul(
                                    k.ps[acc[e_]][:], lhsT=vtm[:, kc18, e_ * 128:(e_ + 1) * 128], rhs=p_[:],
                                    start=(pidx == 0), stop=(pidx == len(tiles) - 1)))(p_, kc18, e_, pidx),
                                    reads=[pk, ('v', kc18)], writes=[('ps', acc[e_])])
                        pend = cur
                    for e_ in range(4):
                        P.op('scalar', (lambda e_: lambda e: e.activation(out=obf[:, e_, :], in_=k.ps[acc[e_]][:], func=AF.Copy))(e_), reads=[('ps', acc[e_])], writes=[('obf', e_)])
                        P.op('scalar', (lambda e_: lambda e: e.activation(out=sqf[:, e_, :], in_=k.ps[acc[e_]][:], func=AF.Square))(e_), reads=[('ps', acc[e_])], writes=[('sqf', e_)])
                    p1, p2 = sbank(), sbank()
                    for e_ in range(4):
                        P.op('tensor', (lambda e_, p1: lambda e: e.matmul(k.ps[p1][:], lhsT=o512[:], rhs=obf[:, e_, :], start=(e_ == 0), stop=(e_ == 3)))(e_, p1),
                             reads=[('obf', e_), 'o512'], writes=[('ps', p1)])
                    for e_ in range(4):
                        P.op('tensor', (lambda e_, p2: lambda e: e.matmul(k.ps[p2][:], lhsT=o512[:], rhs=sqf[:, e_, :], start=(e_ == 0), stop=(e_ == 3)))(e_, p2),
                             reads=[('sqf', e_), 'o512'], writes=[('ps', p2)])
                    P.op('scalar', (lambda p1: lambda e: e.activation(out=tc_[:], in_=k.ps[p1][:], func=AF.Copy))(p1), reads=[('ps', p1)], writes=['tc'])
                    P.op('vector', lambda e: e.tensor_tensor(out=ta[:], in0=tc_[:], in1=tc_[:], op=ALU.mult), reads=['tc'], writes=['ta'])
                    P.op('vector', (lambda p2: lambda e: e.tensor_tensor(out=ta[:], in0=k.ps[p2][:], in1=ta[:], op=ALU.subtract))(p2), reads=[('ps', p2), 'ta'], writes=['ta'])
                    P.op('scalar', lambda e: e.activation(out=ta[:], in_=ta[:], func=AF.Sqrt, bias=k.epsc[:, 0:1], scale=1.0), reads=['ta', 'epsc'], writes=['ta'])
                    P.op('vector', lambda e: e.reciprocal(out=rstd[:], in_=ta[:]), reads=['ta'], writes=['rt_rstd'])
                    P.op('vector', lambda e: e.scalar_tensor_tensor(out=tc_[:], in0=tc_[:], scalar=-1.0, in1=rstd[:], op0=ALU.mult, op1=ALU.mult), reads=['tc', 'rt_rstd'], writes=['tc'])
                    wg = [load_w(k, w_in[:, 4 * D + hd * 512 + hh * 256:4 * D + hd * 512 + (hh + 1) * 256], NCH, 256) for hh in range(2)]
                    for e_ in range(4):
                        pg = sbank()
                        s_, w_ = wg[e_ // 2]
                        for kc in range(NCH):
                            P.op('tensor', (lambda pg, kc, w_, e_, b: lambda e: e.matmul(
                                k.ps[pg][:], lhsT=w_[:, kc, (e_ % 2) * 128:(e_ % 2 + 1) * 128], rhs=hb[:, kc, b.sl],
                                start=(kc == 0), stop=(kc == NCH - 1)))(pg, kc, w_, e_, b),
                                reads=[('ws', s_), ('h', kc, b.key)], writes=[('ps', pg)])
                        P.op('scalar', (lambda pg: lambda e: e.activation(out=tb_[:], in_=k.ps[pg][:], func=AF.Silu))(pg), reads=[('ps', pg)], writes=['tb'])
                        P.op('vector', (lambda e_: lambda e: e.tensor_tensor(out=ta[:], in0=k.ps[acc[e_]][:], in1=rstd[:], op=ALU.mult))(e_), reads=[('ps', acc[e_]), 'rt_rstd'], writes=['ta'])
                        P.op('vector', lambda e: e.tensor_tensor(out=ta[:], in0=ta[:], in1=tc_[:], op=ALU.add), reads=['ta', 'tc'], writes=['ta'])
                        gcol = gng[:, hd * 4 + e_:hd * 4 + e_ + 1]
                        P.op('vector', (lambda e_, gcol: lambda e: e.scalar_tensor_tensor(out=zb[:, e_, :], in0=ta[:], scalar=gcol, in1=tb_[:], op0=ALU.mult, op1=ALU.mult))(e_, gcol),
                             reads=['ta', 'tb', 'rt_gng'], writes=[('obf', e_)])
                    out_proj_residual(k, li, sub, [b], k.dram['rt_w_o'][hd * 512:(hd + 1) * 512, :], 4,
                                      lambda kc, b: zb[:, kc, :], lambda kc, b: [('obf', kc)], None, do_ln=False)
            P.barrier()
    P.barrier()
    with ExitStack() as st:
        lnt = alloc_ln_tmps(k, st)
        for b in LAT_BLKS:
            if b.s == 0 and b.t0 < 1024:
                layer_norm_blk(k, b, lnt)
            else:
                k.ln_pending.append(b)


def hyena(k, li, use_ctx, ctx_next):
    P = k.P
    sub = 1
    allb = LAT_BLKS + [CTX_BLK]
    w_in = k.dram['hy_w_in']
    PI = math.pi
    UW = TT + 4

    def ucol(b):
        return 1 + b.t0 if b.s == 0 else SEQ + 3 + b.t0

    with ExitStack() as st:
        hb = k.sb('hy_h', [128, NCH, TT], BF16, st)
        h2T = k.sb('hy_h2T', [64, TT], BF16, st)
        cw = k.sb('hy_cw', [128, 72], F32, st)
        cb = k.sb('hy_cb', [128, 24], F32, st)
        tn = k.sb('hy_tn', [128, 18], F32, st)
        ident = k.sb('hy_ident', [128, 128], BF16, st)
        P.op('sync', lambda e: e.dma_start(out=cw[:], in_=k.dram['hy_cw']), writes=['hy_cw'], dma=True)
        P.op('sync', lambda e: e.dma_start(out=cb[:], in_=k.dram['hy_cb']), writes=['hy_cb'], dma=True)
        P.op('sync', lambda e: e.dma_start(out=tn[:], in_=k.dram['hy_tn']), writes=['hy_tn'], dma=True)
        P.op('sync', lambda e: e.dma_start(out=ident[:], in_=k.dram['ident']), writes=['hy_ident'], dma=True)
        make_h(k, li, sub, allb, hb, 0)
        make_xa(k, li, sub, allb)
        with ExitStack() as s0:
            fw1 = k.sb('hy_fw1', [33, 64], F32, s0)
            fw2 = k.sb('hy_fw2', [64, 64], F32, s0)
            v64 = k.sb('hy_v64', [64, 4], F32, s0)
            npi = k.sb('hy_npi', [64, 1], F32, s0)
            zt = [k.sb('hy_zt%d' % i, [33, 512], F32, s0) for i in range(2)]
            m1 = k.sb('hy_m1', [64, 512], F32, s0)
            m2 = k.sb('hy_m2', [64, 512], F32, s0)
            mw = k.sb('hy_mw', [64, 512], F32, s0)
            P.op('sync', lambda e: e.dma_start(out=fw1[:], in_=k.dram['hy_fw1']), writes=['fw1'], dma=True)
            P.op('sync', lambda e: e.dma_start(out=fw2[:], in_=k.dram['hy_fw2']), writes=['fw2'], dma=True)
            P.op('sync', lambda e: e.dma_start(out=v64[:], in_=k.dram['hy_vec64']), writes=['v64'], dma=True)
            P.op('vector', lambda e: e.memset(npi[:], -PI), writes=['npi'])
            for bi, b in enumerate(allb):
                z_ = zt[bi % 2]
                zk = ('zt', bi % 2)
                n = b.n
                P.op('sync', (lambda z_, b: lambda e: e.dma_start(out=z_[:, 0:b.n], in_=k.dram['hy_zT'][:, b.sl]))(z_, b), writes=[zk], dma=True)
                p1 = next_ps(k)
                P.op('tensor', (lambda p1, z_, n: lambda e: e.matmul(k.ps[p1][0:64, 0:n], lhsT=fw1[:], rhs=z_[:, 0:n], start=True, stop=True))(p1, z_, n),
                     reads=['fw1', zk], writes=[('ps', p1)])
                P.op('vector', (lambda p1, n: lambda e: e.tensor_scalar(out=m1[:, 0:n], in0=k.ps[p1][0:64, 0:n], scalar1=v64[:, 0:1], scalar2=v64[:, 1:2], op0=ALU.add, op1=ALU.mult))(p1, n),
                     reads=[('ps', p1), 'v64'], writes=['m1'])
                P.op('vector', (lambda n: lambda e: e.tensor_scalar(out=mw[:, 0:n], in0=m1[:, 0:n], scalar1=PI, scalar2=None, op0=ALU.is_gt))(n), reads=['m1'], writes=['mw'])
                P.op('vector', (lambda n: lambda e: e.scalar_tensor_tensor(out=m1[:, 0:n], in0=mw[:, 0:n], scalar=-2.0 * PI, in1=m1[:, 0:n], op0=ALU.mult, op1=ALU.add))(n), reads=['m1', 'mw'], writes=['m1'])
                P.op('vector', (lambda n: lambda e: e.tensor_scalar(out=mw[:, 0:n], in0=m1[:, 0:n], scalar1=-PI, scalar2=None, op0=ALU.is_lt))(n), reads=['m1'], writes=['mw'])
                P.op('vector', (lambda n: lambda e: e.scalar_tensor_tensor(out=m1[:, 0:n], in0=mw[:, 0:n], scalar=2.0 * PI, in1=m1[:, 0:n], op0=ALU.mult, op1=ALU.add))(n), reads=['m1', 'mw'], writes=['m1'])
                P.op('scalar', (lambda n: lambda e: e.activation(out=m1[:, 0:n], in_=m1[:, 0:n], func=AF.Sin))(n), reads=['m1'], writes=['m1'])
                p2 = next_ps(k)
                P.op('tensor', (lambda p2, n: lambda e: e.matmul(k.ps[p2][0:64, 0:n], lhsT=fw2[:], rhs=m1[:, 0:n], start=True, stop=True))(p2, n),
                     reads=['fw2', 'm1'], writes=[('ps', p2)])
                P.op('vector', (lambda p2, n: lambda e: e.tensor_scalar(out=m2[:, 0:n], in0=k.ps[p2][0:64, 0:n], scalar1=v64[:, 2:3], scalar2=v64[:, 3:4], op0=ALU.add, op1=ALU.mult))(p2, n),
                     reads=[('ps', p2), 'v64'], writes=['m2'])
                P.op('vector', (lambda n: lambda e: e.tensor_scalar(out=mw[:, 0:n], in0=m2[:, 0:n], scalar1=PI, scalar2=None, op0=ALU.is_gt))(n), reads=['m2'], writes=['mw'])
                P.op('vector', (lambda n: lambda e: e.scalar_tensor_tensor(out=m2[:, 0:n], in0=mw[:, 0:n], scalar=-2.0 * PI, in1=m2[:, 0:n], op0=ALU.mult, op1=ALU.add))(n), reads=['m2', 'mw'], writes=['m2'])
                P.op('vector', (lambda n: lambda e: e.tensor_scalar(out=mw[:, 0:n], in0=m2[:, 0:n], scalar1=-PI, scalar2=None, op0=ALU.is_lt))(n), reads=['m2'], writes=['mw'])
                P.op('vector', (lambda n: lambda e: e.scalar_tensor_tensor(out=m2[:, 0:n], in0=mw[:, 0:n], scalar=2.0 * PI, in1=m2[:, 0:n], op0=ALU.mult, op1=ALU.add))(n), reads=['m2', 'mw'], writes=['m2'])
                P.op('scalar', (lambda n, b: lambda e: e.activation(out=h2T[:, b.sl], in_=m2[:, 0:n], func=AF.Sin))(n, b), reads=['m2'], writes=[('h2T', b.key)])
        P.barrier()
        HS = os.environ.get('HYSTOP', '')

        def proj_conv(qi, fc, wslot, wcol, ub, ubk, emit_fn, tmps):
            s_, w_ = wslot
            for b in allb:
                pp = next_ps(k)
                for kc in range(NCH):
                    P.op('tensor', (lambda pp, kc, b, w_, wcol: lambda e: e.matmul(
                        k.ps[pp][:, 0:b.n], lhsT=w_[:, kc, wcol * 128:(wcol + 1) * 128], rhs=hb[:, kc, b.sl],
                        start=(kc == 0), stop=(kc == NCH - 1)))(pp, kc, b, w_, wcol),
                        reads=[('ws', s_), ('h', kc, b.key)], writes=[('ps', pp)])
                P.op('scalar', (lambda pp, b: lambda e: e.activation(out=ub[:, ucol(b):ucol(b) + b.n], in_=k.ps[pp][:, 0:b.n], func=AF.Copy))(pp, b),
                     reads=[('ps', pp)], writes=[(ubk, b.key)])
            w0 = cw[:, 0 * 24 + fc:0 * 24 + fc + 1]
            w1 = cw[:, 1 * 24 + fc:1 * 24 + fc + 1]
            w2 = cw[:, 2 * 24 + fc:2 * 24 + fc + 1]
            bia = cb[:, fc:fc + 1]
            for bi, b in enumerate(allb):
                t_, tk = tmps[bi % 2]
                o = ucol(b)
                nb_keys = [(ubk, bb.key) for bb in allb if bb.s == b.s and abs(bb.t0 - b.t0) <= 512] + [(ubk, 'pad'), 'hy_cw', 'hy_cb']
                P.op('scalar', (lambda t_, o, b: lambda e: e.activation(out=t_[:, 0:b.n], in_=ub[:, o:o + b.n], func=AF.Identity, bias=bia, scale=w1))(t_, o, b),
                     reads=nb_keys, writes=[tk])
                P.op('vector', (lambda t_, o, b: lambda e: e.scalar_tensor_tensor(out=t_[:, 0:b.n], in0=ub[:, o - 1:o - 1 + b.n], scalar=w0, in1=t_[:, 0:b.n], op0=ALU.mult, op1=ALU.add))(t_, o, b),
                     reads=nb_keys + [tk], writes=[tk])
                P.op('vector', (lambda t_, o, b: lambda e: e.scalar_tensor_tensor(out=t_[:, 0:b.n], in0=ub[:, o + 1:o + 1 + b.n], scalar=w2, in1=t_[:, 0:b.n], op0=ALU.mult, op1=ALU.add))(t_, o, b),
                     reads=nb_keys + [tk], writes=[tk])
                emit_fn(b, t_, tk)

        def zero_pads(ub, ubk):
            for c0 in (0, SEQ + 1, SEQ + 2, TT + 3):
                P.op('vector', (lambda c0: lambda e: e.memset(ub[:, c0:c0 + 1], 0.0))(c0), writes=[(ubk, 'pad')])

        for cg in range(4 if HS == '' else (0 if HS == 'mlp' else 1)):
            with ExitStack() as sY:
                Y = k.sb('hy_Y', [128, 18, 512], BF16, sY)
                with ExitStack() as sP:
                    ptm = k.sb('hy_ptm', [128, 18, 256], BF16, sP)
                    with ExitStack() as sA:
                        ub0 = k.sb('hy_ub0', [128, UW], F32, sA)
                        x1c = k.sb('hy_x1c', [128, TT], F32, sA)
                        pfm = k.sb('hy_pfm', [128, 2, TT], BF16, sA)
                        tA = [(k.sb('hy_ta%d' % i, [128, 512], F32, sA), ('hy_ta', i)) for i in range(2)]
                        zero_pads(ub0, 'ub0')
                        wx1 = load_w(k, w_in[:, D + cg * 256:D + (cg + 1) * 256], NCH, 256)
                        wv_ = load_w(k, w_in[:, 2 * D + cg * 256:2 * D + (cg + 1) * 256], NCH, 256)
                        for c2 in range(2):
                            def emit_x1(b, t_, tk):
                                P.op('scalar', (lambda b, t_: lambda e: e.activation(out=x1c[:, b.sl], in_=t_[:, 0:b.n], func=AF.Copy))(b, t_),
                                     reads=[tk], writes=[('x1c', b.key)])
                            proj_conv(1, 8 + cg * 2 + c2, wx1, c2, ub0, 'ub0', emit_x1, tA)

                            def emit_v(b, t_, tk, c2=c2):
                                P.op('vector', (lambda b, t_, c2: lambda e: e.tensor_tensor(out=pfm[:, c2, b.sl], in0=t_[:, 0:b.n], in1=x1c[:, b.sl], op=ALU.mult))(b, t_, c2),
                                     reads=[tk, ('x1c', b.key)], writes=[('pfm', c2, b.key)])
                            proj_conv(2, 16 + cg * 2 + c2, wv_, c2, ub0, 'ub0', emit_v, tA)
                        for tc in range(0, 18, 2):
                            pb = next_ps(k)
                            for dt_ in range(2):
                                bkey = allb[(tc + dt_) // 4].key if tc + dt_ < 16 else CTX_BLK.key
                                for c2 in range(2):
                                    P.op('tensor', (lambda pb, tc, dt_, c2: lambda e: e.matmul(
                                        k.ps[pb][:, dt_ * 256 + c2 * 128:dt_ * 256 + (c2 + 1) * 128], lhsT=pfm[:, c2, (tc + dt_) * 128:(tc + dt_ + 1) * 128], rhs=ident[:],
                                        start=True, stop=True))(pb, tc, dt_, c2),
                                        reads=[('pfm', c2, bkey), 'hy_ident'], writes=[('ps', pb)])
                            P.op('scalar', (lambda pb, tc: lambda e: e.activation(out=ptm[:, tc:tc + 2, :], in_=k.ps[pb][:].rearrange("p (a b) -> p a b", a=2), func=AF.Copy))(pb, tc),
                                 reads=[('ps', pb)], writes=[('ptm', tc), ('ptm', tc + 1)])
                    P.barrier()
                    if HS == 'A':
                        continue
                    with ExitStack() as sH:
                        hsum = k.sb('hy_hsum', [128, 18, 256], BF16, sH)
                        hdif = k.sb('hy_hdif', [128, 18, 256], BF16, sH)
                        with ExitStack() as sF:
                            w3c = k.sb('hy_w3c', [64, 512], BF16, sF)
                            dlc = k.sb('hy_dlc', [128, 256], F32, sF)
                            decs = [k.sb('hy_dec%d' % i, [128, 256], F32, sF) for i in range(2)]
                            fas = [k.sb('hy_fa%d' % i, [128, 256], F32, sF) for i in range(2)]
                            fbs = [k.sb('hy_fb%d' % i, [128, 256], F32, sF) for i in range(2)]
                            fcs = [k.sb('hy_fc%d' % i, [128, 256], F32, sF) for i in range(2)]
                            dsk = k.sb('hy_dsk', [1, 256], F32, sF)
                            P.op('sync', (lambda cg: lambda e: e.dma_start(out=dsk[:], in_=k.dram['hy_dskip'][:, cg * 256:(cg + 1) * 256]))(cg), writes=['hy_dsk'], dma=True)
                            w3f = k.sb('hy_w3f', [64, 512], F32, sF)
                            P.op('sync', (lambda cg: lambda e: e.dma_start(out=w3f[:, 0:256], in_=k.dram['hy_fw3'][:, cg * 256:(cg + 1) * 256]))(cg), writes=['w3f_f'], dma=True)
                            P.op('sync', (lambda cg: lambda e: e.dma_start(out=w3f[:, 256:512], in_=k.dram['hy_fw3'][:, D + cg * 256:D + (cg + 1) * 256]))(cg), writes=['w3f_b'], dma=True)
                            P.op('scalar', lambda e: e.activation(out=w3c[:, 0:256], in_=w3f[:, 0:256], func=AF.Copy), reads=['w3f_f'], writes=['w3c_f'])
                            P.op('scalar', lambda e: e.activation(out=w3c[:, 256:512], in_=w3f[:, 256:512], func=AF.Copy), reads=['w3f_b'], writes=['w3c_b'])
                            P.op('sync', (lambda cg: lambda e: e.dma_start(out=dlc[:], in_=k.dram['hy_delta'][:, cg * 256:(cg + 1) * 256]))(cg), writes=['dlc'], dma=True)
                            for tix in range(18):
                                pcol = tix * 128
                                bkey = allb[tix // 4].key if tix < 16 else CTX_BLK.key
                                first = tix in (0, 16)
                                par = tix % 2
                                dec, fa, fb_, fc_ = decs[par], fas[par], fbs[par], fcs[par]
                                kd, ka, kb, kc_ = ('dec', par), ('fa', par), ('fb', par), ('fc', par)
                                pf = next_ps(k)
                                for hh, wk in ((0, 'w3c_f'), (1, 'w3c_b')):
                                    P.op('tensor', (lambda pf, hh, pcol: lambda e: e.matmul(
                                        k.ps[pf][:, hh * 256:(hh + 1) * 256], lhsT=h2T[:, pcol:pcol + 128], rhs=w3c[:, hh * 256:(hh + 1) * 256],
                                        start=True, stop=True))(pf, hh, pcol),
                                        reads=[('h2T', bkey), wk], writes=[('ps', pf)])
                                P.op('scalar', (lambda tix, dec: lambda e: e.activation(out=dec[:], in_=dlc[:], func=AF.Exp, scale=tn[:, tix:tix + 1]))(tix, dec),
                                     reads=['dlc', 'hy_tn'], writes=[kd])
                                P.op('vector', (lambda pf, fa, dec: lambda e: e.tensor_tensor(out=fa[:], in0=k.ps[pf][:, 0:256], in1=dec[:], op=ALU.mult))(pf, fa, dec), reads=[('ps', pf), kd], writes=[ka])
                                P.op('vector', (lambda pf, fb_, dec: lambda e: e.tensor_tensor(out=fb_[:], in0=k.ps[pf][:, 256:512], in1=dec[:], op=ALU.mult))(pf, fb_, dec), reads=[('ps', pf), kd], writes=[kb])
                                if first:
                                    P.op('vector', (lambda fb_: lambda e: e.memset(fb_[0:1, :], 0.0))(fb_), reads=[kb], writes=[kb])
                                P.op('vector', (lambda fa, fb_, fc_: lambda e: e.tensor_tensor(out=fc_[:], in0=fa[:], in1=fb_[:], op=ALU.add))(fa, fb_, fc_), reads=[ka, kb], writes=[kc_])
                                if first:
                                    P.op('vector', (lambda fc_: lambda e: e.tensor_tensor(out=fc_[0:1, :], in0=fc_[0:1, :], in1=dsk[0:1, :], op=ALU.add))(fc_),
                                         reads=[kc_, 'hy_dsk'], writes=[kc_])
                                P.op('scalar', (lambda tix, fc_: lambda e: e.activation(out=hsum[:, tix, :], in_=fc_[:], func=AF.Copy))(tix, fc_), reads=[kc_], writes=[('hsum', tix)])
                                P.op('vector', (lambda tix, fa, fb_: lambda e: e.tensor_tensor(out=hdif[:, tix, :], in0=fa[:], in1=fb_[:], op=ALU.subtract))(tix, fa, fb_), reads=[ka, kb], writes=[('hdif', tix)])
                        P.barrier()
                        if HS == 'F':
                            continue
                        with ExitStack() as sB:
                            fs = [k.sb('hy_fs%d' % i, [128, 2048], BF16, sB) for i in range(3)]
                            Gs = k.sb('hy_Gs', [128, 512], F32, sB)
                            tq = [k.sb('hy_tq%d' % i, [128, 256], F32, sB) for i in range(2)]
                            fsi = 0
                            for (nkch, ntch, t0x, src) in ((16, 16, 0, 'dft_f'), (2, 2, 16, 'dft_fc')):
                                for kch in range(nkch):
                                    bg_step(k, next_ps(k))
                                    pu, pg = next_ps(k, 2)
                                    nel = ntch * 128
                                    for m in range(2):
                                        f_ = fs[fsi % 3]
                                        fk = ('fs', fsi % 3)
                                        fsi += 1
                                        P.op('sync', (lambda f_, kch, nel, src, m: lambda e: e.dma_start(out=f_[:, 0:nel], in_=k.dram[src][kch][:, m * nel:(m + 1) * nel]))(f_, kch, nel, src, m), writes=[fk], dma=True)
                                        fv = f_[:, 0:nel].rearrange("p (t q) -> p t q", t=ntch)
                                        col = 256 * m
                                        for tch in range(ntch):
                                            tix = t0x + tch
                                            for (pb, rhs, rk) in ((pu, ptm, 'ptm'), (pg, hsum if m == 0 else hdif, 'hsum' if m == 0 else 'hdif')):
                                                P.op('tensor', (lambda pb, col, rhs, tch, tix, fv: lambda e: e.matmul(
                                                    k.ps[pb][:, col:col + 256], lhsT=fv[:, tch, :], rhs=rhs[:, tix, :],
                                                    start=(tch == 0), stop=(tch == ntch - 1)))(pb, col, rhs, tch, tix, fv),
                                                    reads=[fk, (rk, tix)], writes=[('ps', pb)])
                                    kix = t0x + kch
                                    P.op('scalar', (lambda pg: lambda e: e.activation(out=Gs[:], in_=k.ps[pg][:], func=AF.Copy))(pg), reads=[('ps', pg)], writes=['Gs'])
                                    P.op('vector', (lambda pu: lambda e: e.tensor_tensor(out=tq[0][:], in0=k.ps[pu][:, 0:256], in1=Gs[:, 0:256], op=ALU.mult))(pu), reads=[('ps', pu), 'Gs'], writes=['tq0'])
                                    P.op('vector', (lambda pu: lambda e: e.tensor_tensor(out=tq[1][:], in0=k.ps[pu][:, 256:512], in1=Gs[:, 256:512], op=ALU.mult))(pu), reads=[('ps', pu), 'Gs'], writes=['tq1'])
                                    P.op('vector', (lambda kix: lambda e: e.tensor_tensor(out=Y[:, kix, 0:256], in0=tq[0][:], in1=tq[1][:], op=ALU.subtract))(kix), reads=['tq0', 'tq1'], writes=[('Y', kix)])
                                    P.op('vector', (lambda pu: lambda e: e.tensor_tensor(out=tq[0][:], in0=k.ps[pu][:, 0:256], in1=Gs[:, 256:512], op=ALU.mult))(pu), reads=[('ps', pu), 'Gs'], writes=['tq0'])
                                    P.op('vector', (lambda pu: lambda e: e.tensor_tensor(out=tq[1][:], in0=k.ps[pu][:, 256:512], in1=Gs[:, 0:256], op=ALU.mult))(pu), reads=[('ps', pu), 'Gs'], writes=['tq1'])
                                    P.op('vector', (lambda kix: lambda e: e.tensor_tensor(out=Y[:, kix, 256:512], in0=tq[0][:], in1=tq[1][:], op=ALU.add))(kix), reads=['tq0', 'tq1'], writes=[('Yi', kix)])
                            bg_drain(k, next_ps(k))
                        P.barrier()
                P.barrier()
                if HS == 'B':
                    continue
                with ExitStack() as sC:
                    isl = [k.sb('hy_is%d' % i, [128, 2048], BF16, sC) for i in range(3)]
                    ub0 = k.sb('hy_ubc', [128, UW], F32, sC)
                    x0c = k.sb('hy_x0c', [128, 2, TT], BF16, sC)
                    zb = k.sb('hy_z', [128, 2, TT], BF16, sC)
                    tA = [(k.sb('hy_tc%d' % i, [128, 512], F32, sC), ('hy_ta', i)) for i in range(2)]
                    zero_pads(ub0, 'ub0')
                    wx0 = load_w(k, w_in[:, cg * 256:(cg + 1) * 256], NCH, 256)
                    for c2 in range(2):
                        def emit_x0(b, t_, tk, c2=c2):
                            P.op('scalar', (lambda b, t_, c2: lambda e: e.activation(out=x0c[:, c2, b.sl], in_=t_[:, 0:b.n], func=AF.Copy))(b, t_, c2),
                                 reads=[tk], writes=[('x0c', c2, b.key)])
                        proj_conv(0, cg * 2 + c2, wx0, c2, ub0, 'ub0', emit_x0, tA)
                    isi = 0
                    for b in allb:
                        lat = (b.s == 0)
                        nkg = 4 if lat else 1
                        kpg = 4 if lat else 2
                        scale = 2.0 / (2 * SEQ) if lat else 2.0 / (2 * CTX)
                        acc = next_ps(k, 2)
                        for kg in range(nkg):
                            for m in range(2):
                                i_ = isl[isi % 3]
                                ik = ('is', isi % 3)
                                isi += 1
                                if lat:
                                    src = k.dram['dft_i'][(b.t0 // 512) * 4 + kg][:, m * 2048:(m + 1) * 2048]
                                    nel = 2048
                                else:
                                    src = k.dram['dft_ic'][:, m * 512:(m + 1) * 512]
                                    nel = 512
                                P.op('sync', (lambda i_, nel, src: lambda e: e.dma_start(out=i_[:, 0:nel], in_=src))(i_, nel, src), writes=[ik], dma=True)
                                iv = i_[:, 0:nel].rearrange("p (q t) -> p q t", q=kpg)
                                for kq in range(kpg):
                                    kix = (kg * 4 + kq) if lat else (16 + kq)
                                    for c2 in range(2):
                                        st_ = (kg == 0 and kq == 0 and m == 0)
                                        sp_ = (kg == nkg - 1 and kq == kpg - 1 and m == 1)
                                        pacc = acc[c2]
                                        P.op('tensor', (lambda c2, m, kix, kq, iv, b, st_, sp_, pacc: lambda e: e.matmul(
                                            k.ps[pacc][:, 0:b.n], lhsT=Y[:, kix, m * 256 + c2 * 128:m * 256 + (c2 + 1) * 128], rhs=iv[:, kq, :],
                                            start=st_, stop=sp_))(c2, m, kix, kq, iv, b, st_, sp_, pacc),
                                            reads=[ik, ('Y', kix), ('Yi', kix)], writes=[('ps', acc[c2])])
                        for c2 in range(2):
                            P.op('vector', (lambda c2, b, scale, pacc: lambda e: e.scalar_tensor_tensor(out=zb[:, c2, b.sl], in0=k.ps[pacc][:, 0:b.n], scalar=scale, in1=x0c[:, c2, b.sl], op0=ALU.mult, op1=ALU.mult))(c2, b, scale, acc[c2]),
                                 reads=[('ps', acc[c2]), ('x0c', c2, b.key)], writes=[('z', c2, b.key)])
                    for grp in ([LAT_BLKS[0], LAT_BLKS[1]], [LAT_BLKS[2], LAT_BLKS[3]], [CTX_BLK]):
                        out_proj_residual(k, li, sub, grp, k.dram['hy_w_o'][cg * 256:(cg + 1) * 256, :], 2,
                                          lambda kc, b: zb[:, kc, b.sl], lambda kc, b: [('z', kc, b.key)], None, do_ln=False)
                P.barrier()
    P.barrier()
    with ExitStack() as st:
        lnt = alloc_ln_tmps(k, st)
        for b in allb:
            if b.s == 0 and b.t0 < 1024:
                layer_norm_blk(k, b, lnt)
            else:
                k.ln_pending.append(b)


MIXERS = {0: diff_attn, 1: hyena, 2: retention}


def pvec(v):
    v = np.asarray(v, np.float32)
    lead = v.shape[:-1]
    v = v.reshape(lead + (v.shape[-1] // 128, 128))
    v = np.moveaxis(v, -1, 0)
    return np.ascontiguousarray(v.reshape(128, -1))


def axial_angles(n_tokens, dim):
    rows = n_tokens // 64
    row = np.repeat(np.arange(rows), 64).astype(np.float32)
    col = np.tile(np.arange(64), rows).astype(np.float32)
    n_freq = dim // 4
    inv = (np.float32(10000.0) ** (-np.arange(n_freq, dtype=np.float32) / np.float32(n_freq))).astype(np.float32)
    return np.concatenate([row[:, None] * inv, col[:, None] * inv], axis=-1).astype(np.float32)


def rope_table_da():
    ang = axial_angles(SEQ, 64)
    p = np.arange(128)
    d = p % 64
    fi = d % 32
    sign = np.where(d < 32, -1.0, 1.0).astype(np.float32)
    cosT = np.cos(ang)[:, fi].T
    sinS = (np.sin(ang)[:, fi] * sign[None, :]).T
    return np.ascontiguousarray(np.concatenate([cosT, sinS], axis=1).astype(ml_dtypes.bfloat16))


def rt_tables():
    ang = axial_angles(SEQ, 256)
    rope = np.concatenate([np.cos(ang).T, np.sin(ang).T], axis=1).astype(ml_dtypes.bfloat16)
    p = np.arange(128, dtype=np.float32)[:, None]
    t = np.arange(512, dtype=np.float32)[None, :]
    io_f = np.broadcast_to(t, (128, 512))
    io_b = np.broadcast_to(511.0 - t, (128, 512))
    offs = 128.0 * np.arange(4, dtype=np.float32)[None, :] + p
    E_f = 128.0 * np.arange(16, dtype=np.float32)[None, :] - p
    E_b = 128.0 * np.arange(18, dtype=np.float32)[None, :] - 511.0 + p
    cst = np.concatenate([io_f, io_b, offs, E_f, E_b], axis=1).astype(np.float32)
    return np.ascontiguousarray(rope), np.ascontiguousarray(cst)


_HY_CACHE = {}


def hy_tables():
    if _HY_CACHE:
        return _HY_CACHE
    f32 = np.float32
    zs = []
    tns = []
    for n in (SEQ, CTX):
        t = np.linspace(0.0, 1.0, n, dtype=f32)[:, None]
        fr = np.linspace(1e-4, 15.0, 16, dtype=f32)[None, :]
        w = (f32(2.0 * math.pi) * np.arange(n, dtype=f32)[:, None] / f32(n)).astype(f32)
        z = np.concatenate([t, np.cos(fr * w), -np.sin(fr * w)], axis=-1).astype(f32)
        zs.append(z.T)
        tns.append((-t[:, 0]).reshape(n // 128, 128).T)
    _HY_CACHE['hy_zT'] = np.ascontiguousarray(np.concatenate(zs, axis=1))
    _HY_CACHE['hy_tn'] = np.ascontiguousarray(np.concatenate(tns, axis=1).astype(f32))
    max_decay = math.log(1e-2) / 0.3
    min_decay = math.log(1e-2) / 1.5
    deltas = np.abs(np.linspace(min_decay, max_decay, D, dtype=f32))
    _HY_CACHE['hy_delta'] = np.ascontiguousarray(np.broadcast_to(deltas[None, :], (128, D)).astype(f32))
    _HY_CACHE['ident'] = np.eye(128, dtype=f32).astype(ml_dtypes.bfloat16)
    bf = ml_dtypes.bfloat16

    def cs(n):
        N = 2 * n
        t = np.arange(n, dtype=np.float64)[:, None]
        kk = np.arange(n, dtype=np.float64)[None, :] + 0.5
        ang = 2.0 * np.pi * t * kk / N
        return np.cos(ang), np.sin(ang)

    C, S = cs(SEQ)
    CS = np.stack([C, S], 0)
    a = CS.reshape(2, 16, 128, 16, 128)
    _HY_CACHE['dft_f'] = np.ascontiguousarray(a.transpose(3, 2, 0, 1, 4).reshape(16, 128, 4096).astype(bf))
    a = CS.reshape(2, 4, 512, 4, 4, 128)
    _HY_CACHE['dft_i'] = np.ascontiguousarray(a.transpose(1, 3, 5, 0, 4, 2).reshape(16, 128, 4096).astype(bf))
    C, S = cs(CTX)
    CS = np.stack([C, S], 0)
    a = CS.reshape(2, 2, 128, 2, 128)
    _HY_CACHE['dft_fc'] = np.ascontiguousarray(a.transpose(3, 2, 0, 1, 4).reshape(2, 128, 512).astype(bf))
    a = CS.reshape(2, 256, 2, 128)
    _HY_CACHE['dft_ic'] = np.ascontiguousarray(a.transpose(3, 0, 2, 1).reshape(128, 1024).astype(bf))
    return _HY_CACHE


def make_in_maps(inputs, n_cores=8):
    f = lambda a: np.ascontiguousarray(np.asarray(a, np.float32))
    shared = {
        'ada_w': f(inputs['ada_w']),
        'ada_b': pvec(np.asarray(inputs['ada_b']).reshape(DEPTH, 9, D).reshape(DEPTH, 9 * D).reshape(DEPTH, 72, 128).reshape(DEPTH, 72 * 128)) if False else None,
    }
    ada_b = np.asarray(inputs['ada_b'], np.float32).reshape(DEPTH, 72, 128)
    shared['ada_b'] = np.ascontiguousarray(ada_b.transpose(2, 0, 1).reshape(128, DEPTH * 72))
    shared['ln_g'] = pvec(inputs['ln_g'])
    shared['ln_b'] = pvec(inputs['ln_b'])
    for nm in ('ffa_wi', 'ffa_wo', 'ffb_wi', 'ffb_wo'):
        shared[nm] = f(inputs[nm])
    wqkv = np.asarray(inputs['da_w_qkv'][0], np.float32)
    shared['da_w_qkv'] = f(wqkv)
    swp = np.arange(D).reshape(D // 64, 2, 32)[:, ::-1, :].reshape(D)
    shared['da_wq_sw'] = f(wqkv[:, 0:D][:, swp])
    shared['da_wk_sw'] = f(wqkv[:, D:2 * D][:, swp])
    shared['da_w_o'] = f(inputs['da_w_o'][0])
    shared['da_lamT'] = f(np.asarray(inputs['da_lambda'][0], np.float32).T)
    shared['da_subg'] = f(np.asarray(inputs['da_subln_g'][0], np.float32).reshape(128, 1))
    shared['rope_da'] = rope_table_da()
    shared['rt_w_in'] = f(inputs['rt_w_in'][0])
    shared['rt_w_o'] = f(inputs['rt_w_o'][0])
    shared['rt_decay'] = np.ascontiguousarray(np.tile(np.asarray(inputs['rt_decay_logit'][0], np.float32).reshape(1, 8), (128, 1)))
    shared['rt_gng'] = pvec(inputs['rt_gn_g'][0])
    shared['rope_rt'], shared['rt_const'] = rt_tables()
    shared['hy_w_in'] = f(inputs['hy_w_in'][0])
    shared['hy_w_o'] = f(inputs['hy_w_o'][0])
    shared['hy_cw'] = pvec(inputs['hy_conv_w'][0])
    shared['hy_cb'] = pvec(inputs['hy_conv_b'][0])
    shared['hy_fw1'] = f(inputs['hy_fw1'][0])
    shared['hy_fw2'] = f(inputs['hy_fw2'][0])
    shared['hy_vec64'] = f(np.stack([np.asarray(inputs[n_][0], np.float32) for n_ in ('hy_fb1', 'hy_ff1', 'hy_fb2', 'hy_ff2')], 1))
    shared['hy_fw3'] = f(inputs['hy_fw3'][0])
    shared['hy_dskip'] = f(np.asarray(inputs['hy_d_skip'][0], np.float32).reshape(1, D))
    shared.update(hy_tables())
    shared['sc_w_in'] = f(inputs['sc_w_in'][0])
    shared['sc_conv_w'] = pvec(inputs['sc_conv_w'][0])
    shared['sc_w_o'] = f(inputs['sc_w_o'][0])
    maps = []
    for b in range(n_cores):
        m = dict(shared)
        m['xT'] = np.ascontiguousarray(np.asarray(inputs['x'][b], np.float32).T)
        m['ctxT'] = np.ascontiguousarray(np.asarray(inputs['ctx'][b], np.float32).T)
        cv = np.stack([np.asarray(inputs['c'][b], np.float32), np.asarray(inputs['c_ctx'], np.float32)], 0)
        m['cvec'] = pvec(cv)
        maps.append(m)
    return maps


def kernel(**inputs):
    nc, k = build_program()
    maps = make_in_maps(inputs, 8)
    res = run_bass_kernel_spmd(nc, maps, core_ids=list(range(8)))
    out = np.stack([np.ascontiguousarray(r['outT'].T) for r in res.results], 0)
    return out.astype(np.float32)
```
